# Optimizing a Trainium2 kernel written in Bass

```python
import math
import jax, jax.numpy as jnp
from jax import lax
import numpy as np

D_MODEL = 1024
BATCH = 8
SEQ = 2048
DEPTH = 2

CTX_LEN = 256
GRID_W = 64
D_MIX = D_MODEL
NA_HEAD_DIM = 64
NA_WIDTH = 3 * D_MIX // 8
NA_HEADS = NA_WIDTH // NA_HEAD_DIM
WIN_ROWS = 8
WIN_COLS = 16
MLA_V_DIM = 64
MLA_WIDTH = 3 * D_MIX // 8
MLA_HEADS = MLA_WIDTH // MLA_V_DIM
MLA_NOPE = 64
MLA_ROPE = 32
MLA_QK = MLA_NOPE + MLA_ROPE
Q_LORA = 3 * D_MODEL // 8
KV_LORA = D_MODEL // 4
S5_WIDTH = D_MIX - NA_WIDTH - MLA_WIDTH
S5_GROUP = 16
S5_GROUPS = S5_WIDTH // S5_GROUP
S5_STATE = 64
DT_MIN = 0.001
DT_MAX = 0.1
D_FF = 2816
ROPE_THETA = 10000.0
EPS = 1e-6
Q_BLOCK = 128
N_MOD = 9
NEG_INF = -1e30
IN_SIZES = (NA_WIDTH, NA_WIDTH, NA_WIDTH, Q_LORA, KV_LORA, MLA_ROPE, S5_WIDTH)
IN_COLS = 3 * NA_WIDTH + Q_LORA + KV_LORA + MLA_ROPE + S5_WIDTH

kernel_name = "hybrid_na_mla_s5_prefix_dit"


def rms_norm(x, g):
    x32 = x.astype(jnp.float32)
    y = x32 * lax.rsqrt(jnp.mean(x32 * x32, axis=-1, keepdims=True) + EPS)
    return (y * g.astype(jnp.float32)).astype(x.dtype)


def modulate(x, g, shift, scale):
    return rms_norm(x, g) * (1 + scale) + shift


def swiglu(x, w_gu, w_down):
    gate, up = jnp.split(x @ w_gu, 2, axis=-1)
    return (jax.nn.silu(gate) * up) @ w_down


def split_cols(z):
    parts, off = [], 0
    for s in IN_SIZES:
        parts.append(z[..., off:off + s])
        off += s
    return parts


def attend(q, k, v, scale):
    s = jnp.einsum('bqhd,bkhd->bhqk', q, k).astype(jnp.float32) * scale
    p = jax.nn.softmax(s, axis=-1).astype(v.dtype)
    return jnp.einsum('bhqk,bkhd->bqhd', p, v)


def attend_joint(q, k1, v1, k2, v2, scale):
    s = jnp.concatenate([jnp.einsum('bqhd,bkhd->bhqk', q, k1),
                         jnp.einsum('bqhd,bkhd->bhqk', q, k2)], axis=-1).astype(jnp.float32) * scale
    p = jax.nn.softmax(s, axis=-1).astype(v1.dtype)
    n1 = k1.shape[1]
    return (jnp.einsum('bhqk,bkhd->bqhd', p[..., :n1], v1)
            + jnp.einsum('bhqk,bkhd->bqhd', p[..., n1:], v2))


def blocked_joint_attention(q, k, v, kc, vc, scale):
    B, N, H, d = q.shape
    nb = N // Q_BLOCK
    qb = q.reshape(B, nb, Q_BLOCK, H, d).transpose(1, 0, 2, 3, 4)
    ob = lax.map(lambda qi: attend_joint(qi, k, v, kc, vc, scale), qb)
    return ob.transpose(1, 0, 2, 3, 4).reshape(B, N, H * v.shape[-1])


def rope_2d(x, row, col):
    half = x.shape[-1] // 2
    quarter = half // 2
    inv = ROPE_THETA ** (-jnp.arange(quarter, dtype=jnp.float32) / quarter)

    def rot(y, pos):
        ang = pos[:, None] * inv
        cos = jnp.cos(ang)[:, None, :].astype(y.dtype)
        sin = jnp.sin(ang)[:, None, :].astype(y.dtype)
        y1, y2 = y[..., :quarter], y[..., quarter:]
        return jnp.concatenate([y1 * cos - y2 * sin, y2 * cos + y1 * sin], axis=-1)

    return jnp.concatenate([rot(x[..., :half], row), rot(x[..., half:], col)], axis=-1)


def rope_tail(x, row, col):
    return jnp.concatenate([x[..., :MLA_NOPE], rope_2d(x[..., MLA_NOPE:], row, col)], axis=-1)


def na_attention(q, k, v, kc, vc, rpb):
    B, N, H, hd = q.shape
    rows = N // GRID_W
    kh = min(WIN_ROWS, rows)
    r = jnp.arange(rows)
    row_start = jnp.clip(r - kh // 2, 0, rows - kh)
    key_rows = row_start[:, None] + jnp.arange(kh)[None, :]
    qg = q.reshape(B, rows, GRID_W, H, hd)
    kg = k.reshape(B, rows, GRID_W, H, hd)[:, key_rows].reshape(B, rows, kh * GRID_W, H, hd)
    vg = v.reshape(B, rows, GRID_W, H, hd)[:, key_rows].reshape(B, rows, kh * GRID_W, H, hd)
    cq = jnp.arange(GRID_W)
    col_start = jnp.clip(cq - WIN_COLS // 2, 0, GRID_W - WIN_COLS)
    kcol = jnp.tile(jnp.arange(GRID_W), kh)
    in_win = (kcol[None, :] >= col_start[:, None]) & (kcol[None, :] < col_start[:, None] + WIN_COLS)
    d_row = jnp.repeat(key_rows - r[:, None], GRID_W, axis=1)
    d_col = jnp.clip(kcol[None, :] - cq[:, None], -(WIN_COLS - 1), WIN_COLS - 1)
    bias = rpb[:, d_row[:, None, :] + WIN_ROWS - 1, d_col[None, :, :] + WIN_COLS - 1]
    scale = hd ** -0.5
    s_win = jnp.einsum('brqhd,brkhd->bhrqk', qg, kg).astype(jnp.float32) * scale + bias[None].astype(jnp.float32)
    s_win = jnp.where(in_win[None, None, None], s_win, NEG_INF)
    s_ctx = jnp.einsum('brqhd,bkhd->bhrqk', qg, kc).astype(jnp.float32) * scale
    p = jax.nn.softmax(jnp.concatenate([s_win, s_ctx], axis=-1), axis=-1).astype(v.dtype)
    kw = kh * GRID_W
    out = (jnp.einsum('bhrqk,brkhd->brqhd', p[..., :kw], vg)
           + jnp.einsum('bhrqk,bkhd->brqhd', p[..., kw:], vc))
    return out.reshape(B, N, H * hd)


def mla_queries(cq, g_cq, w_uq, g_q):
    B, N, _ = cq.shape
    q = (rms_norm(cq, g_cq) @ w_uq).reshape(B, N, MLA_HEADS, MLA_QK)
    return rms_norm(q, g_q)


def mla_keys_values(ckv, kr, g_ckv, w_ukv, g_k):
    B, N, _ = ckv.shape
    kv = (rms_norm(ckv, g_ckv) @ w_ukv).reshape(B, N, MLA_HEADS, MLA_NOPE + MLA_V_DIM)
    k_nope, v = kv[..., :MLA_NOPE], kv[..., MLA_NOPE:]
    k_rope = jnp.broadcast_to(kr[:, :, None, :], (B, N, MLA_HEADS, MLA_ROPE))
    k = rms_norm(jnp.concatenate([k_nope, k_rope], axis=-1), g_k)
    return k, v


def s5_discretize(lam_re, lam_im, log_dt, b_re, b_im):
    lam = lam_re.astype(jnp.float32) + 1j * lam_im.astype(jnp.float32)
    dt = jnp.exp(log_dt.astype(jnp.float32))[:, None]
    lam_bar = jnp.exp(lam * dt)
    b = b_re.astype(jnp.float32) + 1j * b_im.astype(jnp.float32)
    b_bar = ((lam_bar - 1) / lam)[..., None] * b
    return lam_bar, b_bar


def _scan_op(e1, e2):
    a1, b1 = e1
    a2, b2 = e2
    return a1 * a2, a2 * b1 + b2


def linear_scan(lam_bar, bu, h0, reverse):
    if h0 is not None:
        first = -1 if reverse else 0
        bu = bu.at[:, first].add(lam_bar * h0)
    a = jnp.broadcast_to(lam_bar, bu.shape)
    _, h = lax.associative_scan(_scan_op, (a, bu), axis=1, reverse=reverse)
    return h


def s5_mixer(ux, uc, need_ctx_out, lam_re, lam_im, log_dt, b_re, b_im, c_re, c_im, d_skip, w_glu, b_glu):
    B, N, _ = ux.shape
    L = uc.shape[1]
    uxg = ux.reshape(B, N, S5_GROUPS, S5_GROUP).astype(jnp.complex64)
    ucg = uc.reshape(B, L, S5_GROUPS, S5_GROUP).astype(jnp.complex64)
    ys_x, ys_c = [], []
    for d in range(2):
        reverse = d == 1
        lam_bar, b_bar = s5_discretize(lam_re[d], lam_im[d], log_dt[d], b_re[d], b_im[d])
        c_mat = c_re[d].astype(jnp.float32) + 1j * c_im[d].astype(jnp.float32)
        h_c = linear_scan(lam_bar, jnp.einsum('blgm,gpm->blgp', ucg, b_bar), None, reverse)
        h_last = h_c[:, 0] if reverse else h_c[:, -1]
        h_x = linear_scan(lam_bar, jnp.einsum('bngm,gpm->bngp', uxg, b_bar), h_last, reverse)
        ys_x.append(jnp.einsum('bngp,gmp->bngm', h_x, c_mat).real)
        if need_ctx_out:
            ys_c.append(jnp.einsum('blgp,gmp->blgm', h_c, c_mat).real)

    def readout(ys, u):
        y = (ys[0] + ys[1]).reshape(u.shape).astype(u.dtype) + d_skip * u
        a, g = jnp.split(jax.nn.gelu(y) @ w_glu + b_glu, 2, axis=-1)
        return a * jax.nn.sigmoid(g)

    ox = readout(ys_x, ux)
    oc = readout(ys_c, uc) if need_ctx_out else None
    return ox, oc


def token_mixers(zx, zc, need_ctx_out, row, col, na_qk_g, na_rpb, mla_cq_g, mla_ckv_g, mla_w_uq,
                 mla_w_ukv, mla_qk_g, s5_lam_re, s5_lam_im, s5_log_dt, s5_b_re, s5_b_im, s5_c_re,
                 s5_c_im, s5_d, s5_w_glu, s5_b_glu):
    B, N, _ = zx.shape
    L = zc.shape[1]
    qa_x, ka_x, va_x, cq_x, ckv_x, kr_x, u_x = split_cols(zx)
    qa_c, ka_c, va_c, cq_c, ckv_c, kr_c, u_c = split_cols(zc)

    def heads(t):
        return t.reshape(t.shape[0], t.shape[1], NA_HEADS, NA_HEAD_DIM)

    ka_ch = rms_norm(heads(ka_c), na_qk_g[1])
    va_ch = heads(va_c)
    a_x = na_attention(rms_norm(heads(qa_x), na_qk_g[0]), rms_norm(heads(ka_x), na_qk_g[1]),
                       heads(va_x), ka_ch, va_ch, na_rpb)
    kb_c, vb_c = mla_keys_values(ckv_c, kr_c, mla_ckv_g, mla_w_ukv, mla_qk_g[1])
    kb_x, vb_x = mla_keys_values(ckv_x, kr_x, mla_ckv_g, mla_w_ukv, mla_qk_g[1])
    qb_x = rope_tail(mla_queries(cq_x, mla_cq_g, mla_w_uq, mla_qk_g[0]), row, col)
    kb_x = rope_tail(kb_x, row, col)
    b_x = blocked_joint_attention(qb_x, kb_x, vb_x, kb_c, vb_c, MLA_QK ** -0.5)
    s_x, s_c = s5_mixer(u_x, u_c, need_ctx_out, s5_lam_re, s5_lam_im, s5_log_dt, s5_b_re, s5_b_im,
                        s5_c_re, s5_c_im, s5_d, s5_w_glu, s5_b_glu)
    ox = jnp.concatenate([a_x, b_x, s_x], axis=-1)
    if not need_ctx_out:
        return ox, None
    a_c = attend(rms_norm(heads(qa_c), na_qk_g[0]), ka_ch, va_ch, NA_HEAD_DIM ** -0.5).reshape(B, L, NA_WIDTH)
    qb_c = mla_queries(cq_c, mla_cq_g, mla_w_uq, mla_qk_g[0])
    b_c = attend(qb_c, kb_c, vb_c, MLA_QK ** -0.5).reshape(B, L, MLA_WIDTH)
    oc = jnp.concatenate([a_c, b_c, s_c], axis=-1)
    return ox, oc


def setup_inputs(seed: int = 0) -> dict:
    key = jax.random.key(seed)
    keys = jax.random.split(key, 32)
    ctr = [0]
    f32 = jnp.float32

    def nk():
        ctr[0] += 1
        return keys[ctr[0] - 1]

    def nrm(shape, std):
        return std * jax.random.normal(nk(), shape, f32)

    def gain(shape):
        return 1.0 + 0.02 * jax.random.normal(nk(), shape, f32)

    G, P, M = S5_GROUPS, S5_STATE, S5_GROUP
    lam_im_base = jnp.broadcast_to(math.pi * jnp.arange(P, dtype=f32), (DEPTH, 2, G, P))
    return {
        "x": nrm((BATCH, SEQ, D_MODEL), 1.0),
        "c": nrm((BATCH, D_MODEL), 1.0),
        "ctx": nrm((BATCH, CTX_LEN, D_MODEL), 1.0),
        "c_ctx": nrm((D_MODEL,), 1.0),
        "w_mod": nrm((DEPTH, D_MODEL, N_MOD * D_MODEL), 0.2 * D_MODEL ** -0.5),
        "b_mod": nrm((DEPTH, N_MOD * D_MODEL), 0.02),
        "norm_g": gain((DEPTH, 3, D_MODEL)),
        "ffn_w_gu": nrm((DEPTH, 2, D_MODEL, 2 * D_FF), D_MODEL ** -0.5),
        "ffn_w_down": nrm((DEPTH, 2, D_FF, D_MODEL), D_FF ** -0.5),
        "w_in": nrm((DEPTH, D_MODEL, IN_COLS), D_MODEL ** -0.5),
        "w_out": nrm((DEPTH, D_MIX, D_MODEL), D_MIX ** -0.5),
        "na_qk_g": gain((DEPTH, 2, NA_HEAD_DIM)),
        "na_rpb": nrm((DEPTH, NA_HEADS, 2 * WIN_ROWS - 1, 2 * WIN_COLS - 1), 0.02),
        "mla_cq_g": gain((DEPTH, Q_LORA)),
        "mla_ckv_g": gain((DEPTH, KV_LORA)),
        "mla_w_uq": nrm((DEPTH, Q_LORA, MLA_HEADS * MLA_QK), Q_LORA ** -0.5),
        "mla_w_ukv": nrm((DEPTH, KV_LORA, MLA_HEADS * (MLA_NOPE + MLA_V_DIM)), KV_LORA ** -0.5),
        "mla_qk_g": gain((DEPTH, 2, MLA_QK)),
        "s5_lam_re": -0.5 + nrm((DEPTH, 2, G, P), 0.01),
        "s5_lam_im": lam_im_base + nrm((DEPTH, 2, G, P), 0.01),
        "s5_log_dt": jax.random.uniform(nk(), (DEPTH, 2, G), f32, math.log(DT_MIN), math.log(DT_MAX)),
        "s5_b_re": nrm((DEPTH, 2, G, P, M), (2 * M) ** -0.5),
        "s5_b_im": nrm((DEPTH, 2, G, P, M), (2 * M) ** -0.5),
        "s5_c_re": nrm((DEPTH, 2, G, M, P), P ** -0.5),
        "s5_c_im": nrm((DEPTH, 2, G, M, P), P ** -0.5),
        "s5_d": nrm((DEPTH, S5_WIDTH), 1.0),
        "s5_w_glu": nrm((DEPTH, S5_WIDTH, 2 * S5_WIDTH), S5_WIDTH ** -0.5),
        "s5_b_glu": nrm((DEPTH, 2 * S5_WIDTH), 0.02),
    }


def reference(x, c, ctx, c_ctx, w_mod, b_mod, norm_g, ffn_w_gu, ffn_w_down, w_in, w_out, na_qk_g, na_rpb,
              mla_cq_g, mla_ckv_g, mla_w_uq, mla_w_ukv, mla_qk_g, s5_lam_re, s5_lam_im, s5_log_dt,
              s5_b_re, s5_b_im, s5_c_re, s5_c_im, s5_d, s5_w_glu, s5_b_glu):
    B, N, _ = x.shape
    t = jnp.arange(N)
    row = (t // GRID_W).astype(jnp.float32)
    col = (t % GRID_W).astype(jnp.float32)
    hx, hc = x, ctx
    for l in range(DEPTH):
        need_ctx_out = l < DEPTH - 1
        mx = (jax.nn.silu(c) @ w_mod[l] + b_mod[l]).reshape(B, N_MOD, 1, D_MODEL)
        mc = (jax.nn.silu(c_ctx) @ w_mod[l] + b_mod[l]).reshape(N_MOD, D_MODEL)
        hx = hx + 0.5 * mx[:, 2] * swiglu(modulate(hx, norm_g[l, 0], mx[:, 0], mx[:, 1]), ffn_w_gu[l, 0], ffn_w_down[l, 0])
        hc = hc + 0.5 * mc[2] * swiglu(modulate(hc, norm_g[l, 0], mc[0], mc[1]), ffn_w_gu[l, 0], ffn_w_down[l, 0])
        zx = modulate(hx, norm_g[l, 1], mx[:, 3], mx[:, 4]) @ w_in[l]
        zc = modulate(hc, norm_g[l, 1], mc[3], mc[4]) @ w_in[l]
        ox, oc = token_mixers(zx, zc, need_ctx_out, row, col, na_qk_g[l], na_rpb[l], mla_cq_g[l],
                              mla_ckv_g[l], mla_w_uq[l], mla_w_ukv[l], mla_qk_g[l], s5_lam_re[l],
                              s5_lam_im[l], s5_log_dt[l], s5_b_re[l], s5_b_im[l], s5_c_re[l], s5_c_im[l],
                              s5_d[l], s5_w_glu[l], s5_b_glu[l])
        hx = hx + mx[:, 5] * (ox @ w_out[l])
        hx = hx + 0.5 * mx[:, 8] * swiglu(modulate(hx, norm_g[l, 2], mx[:, 6], mx[:, 7]), ffn_w_gu[l, 1], ffn_w_down[l, 1])
        if need_ctx_out:
            hc = hc + mc[5] * (oc @ w_out[l])
            hc = hc + 0.5 * mc[8] * swiglu(modulate(hc, norm_g[l, 2], mc[6], mc[7]), ffn_w_gu[l, 1], ffn_w_down[l, 1])
    return hx
```

```python
import math
import numpy as np
from contextlib import ExitStack
import concourse.bass as bass
import concourse.mybir as mybir
from concourse.ap import AP
from concourse.bass_utils import run_bass_kernel_spmd

F32 = mybir.dt.float32
BF16 = mybir.dt.bfloat16
I32 = mybir.dt.int32
AF = mybir.ActivationFunctionType
ALU = mybir.AluOpType

D = 1024
NT = 2304
LCTX = 256
NX = 2048
DFF = 2816
KC = 8
EPS = 1e-6
RANGES = [(0, 256)] + [(256 + 384 * k, 256 + 384 * (k + 1)) for k in range(5)] + [(2176, 2304)]
NEG = -30000.0
TWO_PI = 2.0 * math.pi

VP_BMOD = 0
VP_NG = 72
VP_S5D = 96
VP_BGLU = 98
VP_CQG = 102
VP_CKVG = 105
VP_NAQ = 107
VP_NAK = 108
VP_MQ = 109
VP_MK = 110


class Buf:
    __slots__ = ("name", "w", "r", "ps")

    def __init__(self, name="", ps=False):
        self.name = name
        self.w = None
        self.r = {}
        self.ps = ps


class Eng:
    def __init__(self, name, sem):
        self.name = name
        self.sem = sem
        self.cnt = 0
        self.seen = {}
        self.ops = []


class Ctx:
    def __init__(self, nc, sems, dma_sems):
        self.nc = nc
        self.E = {n: Eng(n, sems[n]) for n in ("pe", "act", "dve", "pool", "sp")}
        self.semobj = {("c", n): sems[n] for n in self.E}
        self.dma_sems = dma_sems
        self.dma_cnt = {q: [0] * len(v) for q, v in dma_sems.items()}
        self.dma_rr = {q: 0 for q in dma_sems}
        for q, v in dma_sems.items():
            for i, s in enumerate(v):
                self.semobj[("d", q, i)] = s
        self.ninstr = 0

    def _need(self, E, ev, waits):
        if ev is None:
            return
        key, val = ev
        if E.name == "pe" and key == ("c", "pe"):
            return
        if E.seen.get(key, 0) >= val:
            return
        if waits.get(key, 0) < val:
            waits[key] = val

    def _deps(self, E, reads, writes, accumulate):
        waits = {}
        for b in reads:
            self._need(E, b.w, waits)
            if b.ps:
                for k, v in b.r.items():
                    if k != ("c", E.name):
                        self._need(E, (k, v), waits)
        for b in writes:
            if not (accumulate and b.w is not None and b.w[0] == ("c", E.name)):
                self._need(E, b.w, waits)
            for k, v in b.r.items():
                self._need(E, (k, v), waits)
        for k, v in waits.items():
            E.seen[k] = v
        return list(waits.items())

    def op(self, eng, fn, reads=(), writes=(), accumulate=False):
        E = self.E[eng]
        waits = self._deps(E, reads, writes, accumulate)
        E.cnt += 1
        ev = (("c", eng), E.cnt)
        sem = E.sem
        semobj = self.semobj

        def run(e):
            for k, v in waits:
                e.wait_ge(semobj[k], v)
            fn(e).then_inc(sem, 1)

        E.ops.append(run)
        for b in writes:
            b.w = ev
            b.r = {}
        for b in reads:
            b.r[ev[0]] = ev[1]
        self.ninstr += 1
        return ev

    def dma(self, queue, fn, reads=(), writes=()):
        E = self.E[queue]
        waits = self._deps(E, reads, writes, False)
        i = self.dma_rr[queue]
        self.dma_rr[queue] = (i + 1) % len(self.dma_sems[queue])
        key = ("d", queue, i)
        prev = self.dma_cnt[queue][i]
        if prev > 0 and E.seen.get(key, 0) < prev:
            waits.append((key, prev))
            E.seen[key] = prev
        val = prev + 16
        self.dma_cnt[queue][i] = val
        sem = self.semobj[key]
        semobj = self.semobj

        def run(e):
            for k, v in waits:
                e.wait_ge(semobj[k], v)
            fn(e).then_inc(sem, 16)

        E.ops.append(run)
        ev = (key, val)
        for b in writes:
            b.w = ev
            b.r = {}
        for b in reads:
            b.r[key] = val
        self.ninstr += 1
        return ev

    def barrier(self):
        evs = []
        for n, E in self.E.items():
            if E.cnt > 0:
                evs.append((("c", n), E.cnt))
        for q, cnts in self.dma_cnt.items():
            for i, c in enumerate(cnts):
                if c > 0:
                    evs.append((("d", q, i), c))
        semobj = self.semobj
        for n, E in self.E.items():
            waits = []
            for k, v in evs:
                if k == ("c", n):
                    continue
                if E.seen.get(k, 0) < v:
                    waits.append((k, v))
                    E.seen[k] = v

            def run(e, waits=waits):
                for k, v in waits:
                    e.wait_ge(semobj[k], v)

            E.ops.append(run)

    def flush(self):
        self.barrier()
        E = self.E
        with self.nc.Block() as block:
            @block.tensor
            def _(e):
                for f in E["pe"].ops:
                    f(e)

            @block.scalar
            def _(e):
                for f in E["act"].ops:
                    f(e)

            @block.vector
            def _(e):
                for f in E["dve"].ops:
                    f(e)

            @block.gpsimd
            def _(e):
                for f in E["pool"].ops:
                    f(e)

            @block.sync
            def _(e):
                for f in E["sp"].ops:
                    f(e)
        for n in E:
            E[n].ops = []


class Arena:
    def __init__(self, ap, words):
        self.ap = ap
        self.words = words
        self.total = words
        self.top = 0

    def high_f32(self, n):
        self.words -= n
        assert self.top <= self.words
        return self.ap[:, self.words:self.words + n]

    def high_release(self):
        self.words = self.total

    def mark(self):
        return self.top

    def release(self, m):
        self.top = m

    def f32(self, n):
        a = self.top
        assert a + n <= self.words, ("arena overflow", a, n, self.words)
        self.top = a + n
        return self.ap[:, a:a + n]

    def bf16(self, n):
        w = (n + 1) // 2
        return self.f32(w).bitcast(BF16)

    def i32(self, n):
        return self.f32(n).bitcast(I32)


class K:
    def __init__(self, n_layers=2, stop=None):
        self.n_layers = n_layers
        self.stop = stop

    def mm(self, out, lhsT, rhs, start, stop, R, W):
        self.C.op("pe", lambda e: e.matmul(out, lhsT=lhsT, rhs=rhs, start=start, stop=stop),
                  reads=R, writes=W, accumulate=not start)

    def tr(self, out, in_, ident, R, W):
        self.C.op("pe", lambda e: e.transpose(out=out, in_=in_, identity=ident), reads=R, writes=W)

    def act(self, out, in_, func, R, W, bias=None, scale=None):
        kw = {}
        if bias is not None:
            kw["bias"] = bias
        if scale is not None:
            kw["scale"] = scale
        self.C.op("act", lambda e: e.activation(out=out, in_=in_, func=func, **kw), reads=R, writes=W)

    def tt(self, eng, out, in0, in1, op, R, W):
        self.C.op(eng, lambda e: e.tensor_tensor(out=out, in0=in0, in1=in1, op=op), reads=R, writes=W)

    def ts(self, eng, out, in0, s1, s2, op0, op1, R, W):
        if op1 is None:
            self.C.op(eng, lambda e: e.tensor_scalar(out=out, in0=in0, scalar1=s1, scalar2=None, op0=op0),
                      reads=R, writes=W)
        else:
            self.C.op(eng, lambda e: e.tensor_scalar(out=out, in0=in0, scalar1=s1, scalar2=s2, op0=op0, op1=op1),
                      reads=R, writes=W)

    def stt(self, eng, out, in0, scalar, in1, op0, op1, R, W):
        self.C.op(eng, lambda e: e.scalar_tensor_tensor(out=out, in0=in0, scalar=scalar, in1=in1, op0=op0, op1=op1),
                  reads=R, writes=W)

    def cp(self, eng, out, in_, R, W):
        if eng == "act":
            self.C.op("act", lambda e: e.activation(out=out, in_=in_, func=AF.Copy), reads=R, writes=W)
        else:
            self.C.op(eng, lambda e: e.tensor_copy(out=out, in_=in_), reads=R, writes=W)

    def recip(self, out, in_, R, W):
        self.C.op("dve", lambda e: e.reciprocal(out=out, in_=in_), reads=R, writes=W)

    def memset(self, eng, ap, val, W):
        self.C.op(eng, lambda e: e.memset(ap, val), reads=[], writes=W)

    def scan(self, out, d0, d1, init, R, W):
        self.C.op("dve", lambda e: e.tensor_tensor_scan(out=out, data0=d0, data1=d1, initial=init,
                                                        op0=ALU.mult, op1=ALU.add), reads=R, writes=W)

    def ld(self, out, in_, W, R=(), cast=False):
        q = "pool" if cast else "sp"
        self.C.dma(q, lambda e: e.dma_start(out=out, in_=in_), reads=list(R), writes=W)

    def st(self, out, in_, R):
        self.C.dma("sp", lambda e: e.dma_start(out=out, in_=in_), reads=R, writes=[])

    def bank(self, i, w=512):
        return self.PS[i][:, 0:w]

    def build(self):
        nc = bass.Bass("TRN2", target_bir_lowering=False)
        self.nc = nc
        NL = 2

        def din(name, shape):
            return nc.dram_tensor(name, list(shape), F32, kind="ExternalInput").ap()

        T = {}
        T["x"] = din("x", [NX, D])
        T["ctx"] = din("ctx", [LCTX, D])
        T["cvec"] = din("cvec", [16, 128])
        T["vecpack"] = din("vecpack", [NL, 128, 128])
        T["w_mod"] = din("w_mod", [NL, D, 9 * D])
        T["w_gu"] = din("ffn_w_gu", [NL, 2, D, 2 * DFF])
        T["w_down"] = din("ffn_w_down", [NL, 2, DFF, D])
        T["w_in"] = din("w_in", [NL, D, 2080])
        T["w_out"] = din("w_out", [NL, D, D])
        T["w_uq"] = din("mla_w_uq", [NL, 384, 576])
        T["w_ukv"] = din("mla_w_ukv", [NL, 256, 768])
        T["w_glu"] = din("s5_w_glu", [NL, 256, 512])
        T["nab"] = din("nab", [NL, 128, 126, 128])
        T["s5p"] = din("s5p", [NL, 128, 3, 16])
        T["s5B"] = din("s5B", [NL, 128, 16, 2, 128])
        T["s5C"] = din("s5C", [NL, 128, 16, 2, 64])
        T["cf32"] = din("cf32", [128, 5, 128])
        T["rope"] = din("rope", [96, 2, NT])
        T["iota"] = din("iota", [128, 384])
        self.T = T
        out = nc.dram_tensor("out", [NX, D], F32, kind="ExternalOutput").ap()
        self.dbg = None
        if self.stop is not None:
            self.dbg = nc.dram_tensor("dbg", [128, KC * NT], F32, kind="ExternalOutput").ap()

        with ExitStack() as es:
            def sb(name, shape, dt):
                return es.enter_context(nc.sbuf_tensor(name, shape, dt))

            sems = {n: es.enter_context(nc.semaphore("s_" + n)) for n in ("pe", "act", "dve", "pool", "sp")}
            dsems = {q: [es.enter_context(nc.semaphore(f"d_{q}{i}")) for i in range(8)] for q in ("sp", "pool")}
            self.C = Ctx(nc, sems, dsems)
            self.PS = [es.enter_context(nc.psum_tensor(f"ps{i}", [128, 512], F32)) for i in range(8)]
            self.BPS = [Buf(f"ps{i}", ps=True) for i in range(8)]
            AW = 49152
            arena_t = sb("arena", [128, AW], F32)
            self.A = Arena(arena_t[:], AW)
            self.cf = sb("cf", [128, 5, 128], F32)
            self.Bcf = Buf("cf")
            self.identb = sb("identb", [128, 128], BF16)
            self.Bidb = Buf("identb")
            self.iota0 = sb("iota0", [128, 384], F32)
            self.Biota = Buf("iota")
            self.vp = sb("vp", [128, 128], F32)
            self.Bvp = Buf("vp")
            self.modT = sb("modT", [128, 72, 2], F32)
            self.Amod = sb("Amod", [128, 3, 8, 2], F32)
            self.Gmod = sb("Gmod", [128, 3, 8, 2], F32)
            self.Bmod = Buf("mod")
            self.scT = sb("scT", [128, 2, 8], BF16)
            self.Bsc = Buf("scT")
            self.small = sb("small", [128, 64], F32)
            self.Bsmall = Buf("small")

            self.hT = self.A.f32(KC * NT).rearrange("p (c t) -> p c t", c=KC)
            self.Bh = [Buf(f"h{r}") for r in range(len(RANGES))]

            self.setup()
            if self.stop == "load":
                self.dump_h()
            else:
                for l in range(self.n_layers):
                    if not self.layer(l):
                        break
                else:
                    self.final(out)
            self.C.flush()
        return nc

    def ranges_of(self, c0, c1):
        return [i for i, (a, b) in enumerate(RANGES) if a < c1 and b > c0]

    def dump_h(self):
        self.C.flush()
        self.st(self.dbg, self.hT.rearrange("p c t -> p (c t)"), R=self.Bh)

    def dump_ap(self, ap, nwords, R):
        self.C.flush()
        self.st(self.dbg[:, 0:nwords], ap, R=R)

    def setup(self):
        T = self.T
        self.ld(self.cf[:], T["cf32"], [self.Bcf])
        self.ld(self.identb[:], T["cf32"][:, 0, :], [self.Bidb], cast=True)
        self.ld(self.iota0[:], T["iota"], [self.Biota])
        self.ident = self.cf[:, 0, :]
        self.ones = self.cf[:, 1, :]
        self.bones = self.cf[:, 2, :]
        self.perm = self.cf[0:96, 3, 0:96]
        self.misc = self.cf[:, 4, :]
        m = self.A.mark()
        stg = [self.A.f32(D) for _ in range(2)]
        Bstg = [Buf("stg0"), Buf("stg1")]
        for t in range(18):
            src = T["ctx"][t * 128:(t + 1) * 128, :] if t < 2 else T["x"][(t - 2) * 128:(t - 1) * 128, :]
            s = t % 2
            self.ld(stg[s], src, [Bstg[s]])
            rs = self.ranges_of(t * 128, t * 128 + 128)
            for half in range(2):
                b = (2 * t + half) % 8
                pb = self.bank(b).rearrange("p (a q) -> p a q", a=4)
                for a in range(4):
                    c = half * 4 + a
                    self.tr(pb[:, a, :], stg[s][:, c * 128:(c + 1) * 128], self.ident,
                            R=[Bstg[s], self.Bcf], W=[self.BPS[b]])
                eng = "act" if half == 0 else "dve"
                self.cp(eng, self.hT[:, half * 4:half * 4 + 4, t * 128:(t + 1) * 128], pb,
                        R=[self.BPS[b]], W=[self.Bh[r] for r in rs])
        self.C.flush()
        self.A.release(m)

    def layer(self, l):
        self.l = l
        self.adaln(l)
        if self.stop == f"mod{l}":
            self.dump_ap(self.modT[:].rearrange("p a b -> p (a b)"), 144, [self.Bmod])
            return False
        self.ffn(l, 0, 0)
        if self.stop == f"ffn1_{l}":
            self.dump_h()
            return False
        if not self.mixer(l):
            return False
        if self.stop == f"mix_{l}":
            self.dump_h()
            return False
        self.ffn(l, 1, 2)
        if self.stop == f"ffn2_{l}":
            self.dump_h()
            return False
        return True

    def adaln(self, l):
        T = self.T
        A = self.A
        m = A.mark()
        stg = A.f32(128)
        Bst = Buf()
        b = 0
        if l == 0:
            self.ld(stg[0:16, :], T["cvec"], [Bst])
            self.tr(self.bank(b)[:, 0:16], stg[0:16, :], self.ident[0:16, 0:16], R=[Bst, self.Bcf], W=[self.BPS[b]])
            self.act(self.scT[:], self.bank(b)[:, 0:16].rearrange("p (s k) -> p s k", s=2), AF.Silu, R=[self.BPS[b]], W=[self.Bsc])
        stg2 = A.f32(128)
        Bst2 = Buf()
        self.ld(stg2, T["vecpack"][l], [Bst2])
        self.tr(self.bank(1)[:, 0:128], stg2, self.ident, R=[Bst2, self.Bcf], W=[self.BPS[1]])
        self.cp("dve", self.vp[:], self.bank(1)[:, 0:128], R=[self.BPS[1]], W=[self.Bvp])
        NP = 8
        PW = 1152
        wb = [A.bf16(KC * PW).rearrange("p (k n) -> p k n", k=KC) for _ in range(2)]
        Bwb = [Buf(), Buf()]
        pm = self.bank(2)
        wsrc = T["w_mod"][l].rearrange("(k p) n -> p k n", p=128)
        for pc in range(NP):
            s = pc % 2
            self.ld(wb[s][:, :, :], wsrc[:, :, pc * PW:(pc + 1) * PW], [Bwb[s]], cast=True)
            for j in range(9):
                cc = pc * 9 + j
                for k in range(KC):
                    self.mm(pm[:, 2 * cc:2 * cc + 2], wb[s][:, k, j * 128:(j + 1) * 128],
                            self.scT[:, :, k], k == 0, k == KC - 1, R=[Bwb[s], self.Bsc], W=[self.BPS[2]])
        self.tt("dve", self.modT[:], pm[:, 0:144].rearrange("p (a b) -> p a b", b=2),
                self.vp[:, 0:72].unsqueeze(2).to_broadcast([128, 72, 2]), ALU.add,
                R=[self.BPS[2], self.Bvp], W=[self.Bmod])
        for n in range(3):
            sc = self.modT[:, (3 * n + 1) * 8:(3 * n + 1) * 8 + 8, :]
            gt = self.modT[:, (3 * n + 2) * 8:(3 * n + 2) * 8 + 8, :]
            g = self.vp[:, VP_NG + 8 * n:VP_NG + 8 * n + 8].unsqueeze(2).to_broadcast([128, 8, 2])
            self.stt("dve", self.Amod[:, n, :, :], sc, 1.0, g, ALU.add, ALU.mult, R=[self.Bmod, self.Bvp], W=[self.Bmod])
            self.ts("dve", self.Gmod[:, n, :, :], gt, 0.5 if n != 1 else 1.0, None, ALU.mult, None,
                    R=[self.Bmod], W=[self.Bmod])
        self.ts("dve", self.small[:, 0:1], self.vp[:, VP_NAQ:VP_NAQ + 1], 0.125, None, ALU.mult, None,
                R=[self.Bvp], W=[self.Bsmall])
        self.ts("dve", self.small[:, 1:2], self.vp[:, VP_MQ:VP_MQ + 1], 96.0 ** -0.5, None, ALU.mult, None,
                R=[self.Bvp], W=[self.Bsmall])
        self.C.flush()
        A.release(m)

    def shift_ap(self, n, c, s):
        return self.modT[:, 3 * n * 8 + c, s:s + 1]

    def modulate(self, n, xm, Bxm, rlist):
        A = self.A
        m = A.mark()
        sq = [A.f32(384) for _ in range(2)]
        Bsq = [Buf(), Buf()]
        rstd = [A.f32(384) for _ in range(2)]
        Brs = [Buf(), Buf()]
        tmp = [A.f32(384) for _ in range(2)]
        Btmp = [Buf(), Buf()]
        q = 0
        for ri in rlist:
            c0, c1 = RANGES[ri]
            w = c1 - c0
            s = 1 if ri == 0 else 0
            b = ri % 2
            pb = self.bank(b, w)
            for c in range(KC):
                u = q % 2
                q += 1
                self.act(sq[u][:, 0:w], self.hT[:, c, c0:c1], AF.Square, R=[self.Bh[ri]], W=[Bsq[u]])
                self.mm(pb, self.ones, sq[u][:, 0:w], c == 0, c == KC - 1, R=[Bsq[u], self.Bcf], W=[self.BPS[b]])
            rb = ri % 2
            self.act(rstd[rb][:, 0:w], pb, AF.Ln, R=[self.BPS[b], self.Bcf], W=[Brs[rb]],
                     bias=self.misc[:, 0:1], scale=1.0 / D)
            self.act(rstd[rb][:, 0:w], rstd[rb][:, 0:w], AF.Exp, R=[Brs[rb]], W=[Brs[rb]], scale=-0.5)
            for c in range(KC):
                u = q % 2
                q += 1
                self.tt("dve", tmp[u][:, 0:w], self.hT[:, c, c0:c1], rstd[rb][:, 0:w], ALU.mult,
                        R=[self.Bh[ri], Brs[rb]], W=[Btmp[u]])
                self.act(xm[:, c, c0:c1], tmp[u][:, 0:w], AF.Identity, R=[Btmp[u], self.Bmod], W=[Bxm[ri]],
                         bias=self.shift_ap(n, c, s), scale=self.Amod[:, n, c, s:s + 1])
        self.C.flush()
        A.release(m)

    def ffn(self, l, f, n):
        T = self.T
        A = self.A
        rlist = list(range(len(RANGES)))
        if l == self.n_layers - 1 and f == 1:
            rlist = rlist[1:]
        m0 = A.mark()
        xm = A.bf16(KC * NT).rearrange("p (c t) -> p c t", c=KC)
        Bxm = [Buf() for _ in RANGES]
        self.modulate(n, xm, Bxm, rlist)
        GS = 4
        groups = [(j0, min(GS, 22 - j0)) for j0 in range(0, 22, GS)]
        wgu = [A.bf16(KC * 2 * 512).rearrange("p (k g n) -> p k g n", k=KC, g=2) for _ in range(2)]
        Bwgu = [Buf(), Buf()]
        wdn = [A.bf16(GS * D).rearrange("p (j n) -> p j n", j=GS) for _ in range(2)]
        Bwdn = [Buf(), Buf()]
        hid = A.bf16(GS * NT).rearrange("p (j t) -> p j t", j=GS)
        Bhid = [[Buf() for _ in RANGES] for _ in range(GS)]
        stmp = [A.f32(384) for _ in range(2)]
        Bstmp = [Buf(), Buf()]
        gsrc = T["w_gu"][l, f].rearrange("(k p) n -> p k n", p=128)
        dsrc = T["w_down"][l, f].rearrange("(j p) n -> p j n", p=128)
        q = 0
        pbk = 0
        for gi, (j0, nj) in enumerate(groups):
            s = gi % 2
            ncol = nj * 128
            self.ld(wgu[s][:, :, 0, 0:ncol], gsrc[:, :, j0 * 128:j0 * 128 + ncol], [Bwgu[s]], cast=True)
            self.ld(wgu[s][:, :, 1, 0:ncol], gsrc[:, :, DFF + j0 * 128:DFF + j0 * 128 + ncol], [Bwgu[s]], cast=True)
            self.ld(wdn[s][:, 0:nj, :], dsrc[:, j0:j0 + nj, :], [Bwdn[s]], cast=True)
            for jj in range(nj):
                for ri in rlist:
                    c0, c1 = RANGES[ri]
                    w = c1 - c0
                    bg = pbk % 8
                    bu = (pbk + 1) % 8
                    pbk += 2
                    pg = self.bank(bg, w)
                    pu = self.bank(bu, w)
                    for k in range(KC):
                        self.mm(pg, wgu[s][:, k, 0, jj * 128:(jj + 1) * 128], xm[:, k, c0:c1], k == 0, k == KC - 1,
                                R=[Bwgu[s], Bxm[ri]], W=[self.BPS[bg]])
                    for k in range(KC):
                        self.mm(pu, wgu[s][:, k, 1, jj * 128:(jj + 1) * 128], xm[:, k, c0:c1], k == 0, k == KC - 1,
                                R=[Bwgu[s], Bxm[ri]], W=[self.BPS[bu]])
                    u = q % 2
                    q += 1
                    self.act(stmp[u][:, 0:w], pg, AF.Silu, R=[self.BPS[bg]], W=[Bstmp[u]])
                    self.tt("dve", hid[:, jj, c0:c1], stmp[u][:, 0:w], pu, ALU.mult,
                            R=[Bstmp[u], self.BPS[bu]], W=[Bhid[jj][ri]])
            for oc in range(KC):
                for ri in rlist:
                    c0, c1 = RANGES[ri]
                    w = c1 - c0
                    sidx = 1 if ri == 0 else 0
                    b = pbk % 8
                    pbk += 1
                    po = self.bank(b, w)
                    for jj in range(nj):
                        self.mm(po, wdn[s][:, jj, oc * 128:(oc + 1) * 128], hid[:, jj, c0:c1], jj == 0, jj == nj - 1,
                                R=[Bwdn[s], Bhid[jj][ri]], W=[self.BPS[b]])
                    self.stt("dve", self.hT[:, oc, c0:c1], po, self.Gmod[:, n, oc, sidx:sidx + 1], self.hT[:, oc, c0:c1],
                             ALU.mult, ALU.add, R=[self.BPS[b], self.Bmod, self.Bh[ri]], W=[self.Bh[ri]])
        self.C.flush()
        A.release(m0)

    def mixer(self, l):
        A = self.A
        m0 = A.mark()
        self.need_ctx = l < self.n_layers - 1
        xm = A.high_f32(KC * NT // 2).bitcast(BF16).rearrange("p (c t) -> p c t", c=KC)
        Bxm = [Buf() for _ in RANGES]
        self.xm2 = xm
        self.Bxm2 = Bxm
        self.modulate(1, xm, Bxm, list(range(len(RANGES))))
        self.mix_na(l)
        if self.stop == f"na_{l}":
            self.dump_h()
            return False
        self.mix_s5(l)
        if self.stop in ("s5a", "s5b", "s5c"):
            return False
        if self.stop == f"s5_{l}":
            self.dump_h()
            return False
        self.mix_mla(l, m0)
        A.release(m0)
        return True

    def load_win(self, l, col0, ncols, dst, Bdst):
        src = self.T["w_in"][l].rearrange("(k p) n -> p k n", p=128)
        self.ld(dst[:, :, 0:ncols], src[:, :, col0:col0 + ncols], [Bdst], cast=True)

    def outproj_tiles(self, l, a_tok, Ba, tiles, krow0, nkc, wout, Bwout):
        A = self.A
        m = A.mark()
        oT = [A.bf16(nkc * 128).rearrange("p (k t) -> p k t", k=nkc) for _ in range(2)]
        BoT = [Buf(), Buf()]
        pbk = 0
        for qi, t in enumerate(tiles):
            u = qi % 2
            ri = self.ranges_of(t * 128, t * 128 + 128)
            sidx = 1 if t < 2 else 0
            bt = 6 + (qi % 2)
            ptr = self.PS[bt][:].bitcast(BF16)
            for k in range(nkc):
                self.tr(ptr[:, k * 128:(k + 1) * 128], a_tok[:, t, k * 128:(k + 1) * 128], self.identb[:],
                        R=[Ba, self.Bidb], W=[self.BPS[bt]])
            self.cp("act", oT[u][:], ptr[:, 0:nkc * 128].rearrange("p (k t) -> p k t", k=nkc),
                    R=[self.BPS[bt]], W=[BoT[u]])
            for half in range(2):
                b = pbk % 6
                pbk += 1
                pb = self.bank(b).rearrange("p (a q) -> p a q", a=4)
                for a in range(4):
                    oc = half * 4 + a
                    for k in range(nkc):
                        self.mm(pb[:, a, :], wout[:, k, oc * 128:(oc + 1) * 128], oT[u][:, k, :], k == 0, k == nkc - 1,
                                R=[Bwout, BoT[u]], W=[self.BPS[b]])
                for a in range(4):
                    oc = half * 4 + a
                    dst = self.hT[:, oc, t * 128:(t + 1) * 128]
                    self.stt("dve", dst, pb[:, a, :], self.Gmod[:, 1, oc, sidx:sidx + 1], dst, ALU.mult, ALU.add,
                             R=[self.BPS[b], self.Bmod] + [self.Bh[r] for r in ri], W=[self.Bh[r] for r in ri])
        self.C.flush()
        A.release(m)

    def load_wout(self, l, krow0, nkc):
        wout = self.A.bf16(nkc * D).rearrange("p (k n) -> p k n", k=nkc)
        Bw = Buf()
        src = self.T["w_out"][l].rearrange("(k p) n -> p k n", p=128)
        self.ld(wout[:, :, :], src[:, krow0:krow0 + nkc, :], [Bw], cast=True)
        return wout, Bw

    def headnorm_fm(self, zps, Bz, w, nparts, onesmat, inv_d, gain_ap, out_ap, Bout, tmps):
        sq, Bsq, rs, Brs, bss = tmps
        self.act(sq[0:nparts, 0:w], zps, AF.Square, R=[Bz], W=[Bsq])
        pss = self.bank(bss, w)[0:nparts, :]
        self.mm(pss, onesmat, sq[0:nparts, 0:w], True, True, R=[Bsq, self.Bcf], W=[self.BPS[bss]])
        self.act(rs[0:nparts, 0:w], pss, AF.Ln, R=[self.BPS[bss], self.Bcf], W=[Brs],
                 bias=self.misc[0:nparts, 0:1], scale=inv_d)
        self.act(rs[0:nparts, 0:w], rs[0:nparts, 0:w], AF.Exp, R=[Brs], W=[Brs], scale=-0.5)
        self.stt("dve", out_ap, zps, gain_ap, rs[0:nparts, 0:w], ALU.mult, ALU.mult,
                 R=[Bz, Brs, self.Bsmall, self.Bvp], W=[Bout])

    def mix_na(self, l):
        T = self.T
        A = self.A
        m = A.mark()
        xm, Bxm = self.xm2, self.Bxm2
        qn = A.bf16(3 * NT).rearrange("p (c t) -> p c t", c=3)
        kn = A.bf16(3 * NT).rearrange("p (c t) -> p c t", c=3)
        Bqn, Bkn = Buf(), Buf()
        vna = A.bf16(18 * 6 * 65).rearrange("p (t h d) -> p t h d", t=18, h=6)
        Bv = Buf()
        a_tok = A.bf16(18 * 384).rearrange("p (t d) -> p t d", t=18)
        Ba = Buf()
        m1 = A.mark()
        wbuf = [A.bf16(KC * 384).rearrange("p (k n) -> p k n", k=KC) for _ in range(2)]
        Bw = [Buf(), Buf()]
        NSL = 3
        sq = [A.f32(384) for _ in range(NSL)]
        rs = [A.f32(384) for _ in range(NSL)]
        Bsq = [Buf() for _ in range(NSL)]
        Brs = [Buf() for _ in range(NSL)]
        self.memset("pool", vna[:, :, :, 64:65], 1.0, [Bv])
        self.load_win(l, 0, 384, wbuf[0], Bw[0])
        self.load_win(l, 384, 384, wbuf[1], Bw[1])

        def na_chain(part, cch, ri, u):
            dstT, Bd = (qn, Bqn) if part == 0 else (kn, Bkn)
            gain = self.small[:, 0:1] if part == 0 else self.vp[:, VP_NAK:VP_NAK + 1]
            c0, c1 = RANGES[ri]
            w = c1 - c0
            bz = 2 * u
            bs_ = 2 * u + 1
            pz = self.bank(bz, w)
            for k in range(KC):
                self.mm(pz, wbuf[part][:, k, cch * 128:(cch + 1) * 128], xm[:, k, c0:c1], k == 0, k == KC - 1,
                        R=[Bw[part], Bxm[ri]], W=[self.BPS[bz]])
            yield
            self.act(sq[u][:, 0:w], pz, AF.Square, R=[self.BPS[bz]], W=[Bsq[u]])
            pss = self.bank(bs_, w)
            self.mm(pss, self.bones, sq[u][:, 0:w], True, True, R=[Bsq[u], self.Bcf], W=[self.BPS[bs_]])
            yield
            self.act(rs[u][:, 0:w], pss, AF.Ln, R=[self.BPS[bs_], self.Bcf], W=[Brs[u]],
                     bias=self.misc[:, 0:1], scale=1.0 / 64)
            yield
            self.act(rs[u][:, 0:w], rs[u][:, 0:w], AF.Exp, R=[Brs[u]], W=[Brs[u]], scale=-0.5)
            yield
            self.stt("dve", dstT[:, cch, c0:c1], pz, gain, rs[u][:, 0:w], ALU.mult, ALU.mult,
                     R=[self.BPS[bz], Brs[u], self.Bsmall, self.Bvp], W=[Bd])
            yield

        lst = [(part, cch, ri) for part in range(2) for cch in range(3) for ri in range(len(RANGES))]

        def na_stream(u):
            for _ in range(2 * u):
                yield
            for (part, cch, ri) in lst[u::NSL]:
                yield from na_chain(part, cch, ri, u)

        live = [na_stream(u) for u in range(NSL)]
        while live:
            for g in list(live):
                try:
                    next(g)
                except StopIteration:
                    live.remove(g)
        self.load_win(l, 768, 384, wbuf[0], Bw[0])
        for t in range(18):
            b = t % 4
            pv = self.bank(b, 384)
            ri = self.ranges_of(t * 128, t * 128 + 128)
            for k in range(KC):
                self.mm(pv, xm[:, k, t * 128:(t + 1) * 128], wbuf[0][:, k, :], k == 0, k == KC - 1,
                        R=[Bw[0]] + [Bxm[r] for r in ri], W=[self.BPS[b]])
            self.cp("act" if t % 2 else "dve", vna[:, t, :, 0:64], pv.rearrange("p (h d) -> p h d", h=6),
                    R=[self.BPS[b]], W=[Bv])
        self.C.flush()
        A.release(m1)
        bias = [A.bf16(21 * 128).rearrange("p (t q) -> p t q", t=21) for _ in range(2)]
        Bb = [Buf(), Buf()]
        pT = [A.bf16(512).rearrange("p (t q) -> p t q", t=4) for _ in range(3)]
        BpT = [Buf() for _ in range(3)]
        rc = A.f32(8)
        Brc = Buf()
        cls = {}
        ci = 0
        for dm in range(-2, 3):
            cls[("int", dm)] = ci
            ci += 1
        for i in (0, 1, 14, 15):
            mlo = {0: 0, 1: 0, 14: 12, 15: 12}[i]
            for mm_ in range(mlo, mlo + 4):
                cls[(i, mm_)] = ci
                ci += 1
        units = []
        for h in range(6):
            qtiles = list(range(2, 18)) + ([0, 1] if self.need_ctx else [])
            for ti, t in enumerate(qtiles):
                if t >= 2:
                    i = t - 2
                    if 2 <= i <= 13:
                        kts = [(0, None), (1, None)] + [(2 + i + dm, cls[("int", dm)]) for dm in range(-2, 3)]
                    else:
                        mlo = {0: 0, 1: 0, 14: 12, 15: 12}[i]
                        kts = [(0, None), (1, None)] + [(2 + mm_, cls[(i, mm_)]) for mm_ in range(mlo, mlo + 4)]
                else:
                    kts = [(0, None), (1, None)]
                nk = len(kts)
                for g0 in range(0, nk, 4):
                    units.append(dict(h=h, t=t, grp=kts[g0:g0 + 4], g0=g0, nk=nk, pair=len(units) if g0 == 0 else None,
                                      newhead=(ti == 0 and g0 == 0)))
        npair = 0
        for un in units:
            if un["g0"] == 0:
                npair += 1
            un["bo"] = 4 + (npair % 2)

        def emit_S(i):
            un = units[i]
            h, t = un["h"], un["t"]
            hb = h % 2
            if un["newhead"]:
                self.ld(bias[hb][:], T["nab"][l][:, h * 21:(h + 1) * 21, :], [Bb[hb]], cast=True)
            cch = h // 2
            p0 = (h % 2) * 64
            qsl = qn[p0:p0 + 64, cch, t * 128:(t + 1) * 128]
            b = i % 4
            pS = self.bank(b).rearrange("p (a q) -> p a q", a=4)
            for gi, (kt, bc) in enumerate(un["grp"]):
                self.mm(pS[:, gi, :], kn[p0:p0 + 64, cch, kt * 128:(kt + 1) * 128], qsl, True, bc is None,
                        R=[Bkn, Bqn], W=[self.BPS[b]])
                if bc is not None:
                    self.mm(pS[:, gi, :], self.identb[:], bias[hb][:, bc, :], False, True,
                            R=[self.Bidb, Bb[hb]], W=[self.BPS[b]])

        emit_S(0)
        for i, un in enumerate(units):
            h, t = un["h"], un["t"]
            b = i % 4
            u = i % 3
            bo = un["bo"]
            po = self.bank(bo, 65)
            pS = self.bank(b).rearrange("p (a q) -> p a q", a=4)
            ng = len(un["grp"])
            self.act(pT[u][:, 0:ng, :], pS[:, 0:ng, :], AF.Exp, R=[self.BPS[b]], W=[BpT[u]])
            if i + 1 < len(units):
                emit_S(i + 1)
            for gi, (kt, bc) in enumerate(un["grp"]):
                done = un["g0"] + gi
                self.mm(po, pT[u][:, gi, :], vna[:, kt, h, :], done == 0, done == un["nk"] - 1,
                        R=[BpT[u], Bv], W=[self.BPS[bo]])
            if un["g0"] + ng == un["nk"]:
                self.recip(rc[:, 0:1], po[:, 64:65], R=[self.BPS[bo]], W=[Brc])
                self.ts("dve", a_tok[:, t, h * 64:(h + 1) * 64], po[:, 0:64], rc[:, 0:1], None, ALU.mult, None,
                        R=[self.BPS[bo], Brc], W=[Ba])
        self.C.flush()
        wout, Bwo = self.load_wout(l, 0, 3)
        tiles = list(range(2, 18)) + ([0, 1] if self.need_ctx else [])
        self.outproj_tiles(l, a_tok, Ba, tiles, 0, 3, wout, Bwo)
        A.release(m)

    def mix_s5(self, l):
        T = self.T
        A = self.A
        m = A.mark()
        xm, Bxm = self.xm2, self.Bxm2
        ysb = A.f32(2 * NT).rearrange("p (c t) -> p c t", c=2)
        By = [[Buf() for _ in RANGES] for _ in range(2)]
        ubf = A.bf16(2 * NT).rearrange("p (c t) -> p c t", c=2)
        Bu = [Buf() for _ in RANGES]
        m1 = A.mark()
        wbuf = A.bf16(KC * 256).rearrange("p (k n) -> p k n", k=KC)
        Bw = Buf()
        self.load_win(l, 1824, 256, wbuf, Bw)
        pbk = 0
        for ch in range(2):
            for ri in range(len(RANGES)):
                c0, c1 = RANGES[ri]
                w = c1 - c0
                b = pbk % 4
                pbk += 1
                pz = self.bank(b, w)
                for k in range(KC):
                    self.mm(pz, wbuf[:, k, ch * 128:(ch + 1) * 128], xm[:, k, c0:c1], k == 0, k == KC - 1,
                            R=[Bw, Bxm[ri]], W=[self.BPS[b]])
                self.ts("dve", ysb[:, ch, c0:c1], pz, self.vp[:, VP_S5D + ch:VP_S5D + ch + 1], None, ALU.mult, None,
                        R=[self.BPS[b], self.Bvp], W=[By[ch][ri]])
                self.cp("dve", ubf[:, ch, c0:c1], pz, R=[self.BPS[b]], W=[Bu[ri]])
        self.C.flush()
        A.release(m1)
        if self.stop == "s5a":
            self.dump_h()
            return
        prm = A.f32(48).rearrange("p (a k) -> p a k", a=3)
        Bp = Buf()
        self.ld(prm, T["s5p"][l], [Bp])
        NS = 24
        sc = A.f32(NS * 16).rearrange("p (a k) -> p a k", a=NS)
        Bs = Buf()
        (DT, RHO, TH, SS, UT, FR, SN, CS, NR, NI, DEN, FRE, FIM, NFI, T1, T2, AFR, CRE, CIM, OFF, KI) = range(21)
        lr, li, ldt = prm[:, 0, :], prm[:, 1, :], prm[:, 2, :]
        sgn = self.misc[:, 16:32]
        hp = self.misc[:, 1:2]
        R_ = [Bp, Bs, self.Bcf]
        W_ = [Bs]
        self.act(sc[:, DT, :], ldt, AF.Exp, R=R_, W=W_)
        self.tt("dve", sc[:, T1, :], lr, sc[:, DT, :], ALU.mult, R=R_, W=W_)
        self.act(sc[:, RHO, :], sc[:, T1, :], AF.Exp, R=R_, W=W_)
        self.tt("dve", sc[:, TH, :], li, sc[:, DT, :], ALU.mult, R=R_, W=W_)
        self.ts("dve", sc[:, UT, :], sc[:, TH, :], 1.0 / TWO_PI, None, ALU.mult, None, R=R_, W=W_)
        self.tt("dve", sc[:, SS, :], sc[:, UT, :], sgn, ALU.mult, R=R_, W=W_)
        ki = A.i32(16)
        self.cp("dve", ki, sc[:, UT, :], R=R_, W=W_)
        self.tt("dve", sc[:, FR, :], sc[:, UT, :], ki, ALU.subtract, R=R_, W=W_)
        self.act(sc[:, SN, :], sc[:, FR, :], AF.Sin, R=R_, W=W_, scale=TWO_PI)
        self.act(sc[:, AFR, :], sc[:, FR, :], AF.Abs, R=R_, W=W_)
        self.act(sc[:, CS, :], sc[:, AFR, :], AF.Sin, R=R_, W=W_, scale=-TWO_PI, bias=hp)
        self.tt("dve", sc[:, NR, :], sc[:, RHO, :], sc[:, CS, :], ALU.mult, R=R_, W=W_)
        self.ts("dve", sc[:, NR, :], sc[:, NR, :], -1.0, None, ALU.add, None, R=R_, W=W_)
        self.tt("dve", sc[:, NI, :], sc[:, RHO, :], sc[:, SN, :], ALU.mult, R=R_, W=W_)
        self.tt("dve", sc[:, DEN, :], lr, lr, ALU.mult, R=R_, W=W_)
        self.tt("dve", sc[:, T1, :], li, li, ALU.mult, R=R_, W=W_)
        self.tt("dve", sc[:, DEN, :], sc[:, DEN, :], sc[:, T1, :], ALU.add, R=R_, W=W_)
        self.recip(sc[:, DEN, :], sc[:, DEN, :], R=R_, W=W_)
        self.tt("dve", sc[:, T1, :], sc[:, NR, :], lr, ALU.mult, R=R_, W=W_)
        self.tt("dve", sc[:, T2, :], sc[:, NI, :], li, ALU.mult, R=R_, W=W_)
        self.tt("dve", sc[:, T1, :], sc[:, T1, :], sc[:, T2, :], ALU.add, R=R_, W=W_)
        self.tt("dve", sc[:, FRE, :], sc[:, T1, :], sc[:, DEN, :], ALU.mult, R=R_, W=W_)
        self.tt("dve", sc[:, T1, :], sc[:, NI, :], lr, ALU.mult, R=R_, W=W_)
        self.tt("dve", sc[:, T2, :], sc[:, NR, :], li, ALU.mult, R=R_, W=W_)
        self.tt("dve", sc[:, T1, :], sc[:, T1, :], sc[:, T2, :], ALU.subtract, R=R_, W=W_)
        self.tt("dve", sc[:, FIM, :], sc[:, T1, :], sc[:, DEN, :], ALU.mult, R=R_, W=W_)
        self.ts("dve", sc[:, NFI, :], sc[:, FIM, :], -1.0, None, ALU.mult, None, R=R_, W=W_)
        (ECN, ESN) = (21, 22)
        self.ts("dve", sc[:, T1, :], sc[:, SS, :], float(NT), 16.0, ALU.mult, ALU.add, R=R_, W=W_)
        ki2 = A.i32(16)
        self.cp("dve", ki2, sc[:, T1, :], R=R_, W=W_)
        self.tt("dve", sc[:, T1, :], sc[:, T1, :], ki2, ALU.subtract, R=R_, W=W_)
        self.act(sc[:, ESN, :], sc[:, T1, :], AF.Sin, R=R_, W=W_, scale=TWO_PI)
        self.act(sc[:, T2, :], sc[:, T1, :], AF.Abs, R=R_, W=W_)
        self.act(sc[:, ECN, :], sc[:, T2, :], AF.Sin, R=R_, W=W_, scale=-TWO_PI, bias=hp)
        if self.stop == "s5c":
            self.dump_ap(sc[:].rearrange("p a k -> p (a k)"), NS * 16, [Bs])
            return
        Bm = A.bf16(16 * 2 * 128).rearrange("p (k c n) -> p k c n", k=16, c=2)
        BBm = Buf()
        for k4 in range(4):
            self.ld(Bm[:, 4 * k4:4 * k4 + 4], T["s5B"][l][:, 4 * k4:4 * k4 + 4], [BBm], cast=True)
        Lm = A.bf16(16 * 3 * 64).rearrange("p (k c n) -> p k c n", k=16, c=3)
        BLm = Buf()
        mC = A.mark()
        Cf = A.f32(16 * 2 * 64).rearrange("p (k c n) -> p k c n", k=16, c=2)
        BCf = Buf()
        self.ld(Cf, T["s5C"][l], [BCf])
        ct = A.f32(128).rearrange("p (a n) -> p a n", a=2)
        Bct = Buf()
        for kk in range(16):
            fr = sc[:, FRE, kk:kk + 1]
            fi = sc[:, FIM, kk:kk + 1]
            nfi = sc[:, NFI, kk:kk + 1]
            RR = [BCf, Bs, Bct]
            self.ts("dve", ct[:, 0, :], Cf[:, kk, 0, :], fr, None, ALU.mult, None, R=RR, W=[Bct])
            self.stt("dve", ct[:, 0, :], Cf[:, kk, 1, :], nfi, ct[:, 0, :], ALU.mult, ALU.add, R=RR, W=[Bct])
            self.ts("dve", ct[:, 1, :], Cf[:, kk, 0, :], fi, None, ALU.mult, None, R=RR, W=[Bct])
            self.stt("dve", ct[:, 1, :], Cf[:, kk, 1, :], fr, ct[:, 1, :], ALU.mult, ALU.add, R=RR, W=[Bct])
            self.cp("dve", Lm[:, kk, 0, :], ct[:, 0, :], R=[Bct], W=[BLm])
            self.ts("dve", Lm[:, kk, 1, :], ct[:, 0, :], -1.0, None, ALU.mult, None, R=[Bct], W=[BLm])
            self.ts("dve", Lm[:, kk, 2, :], ct[:, 1, :], -1.0, None, ALU.mult, None, R=[Bct], W=[BLm])
        self.C.flush()
        A.release(mC)
        if self.stop == "s5b":
            self.dump_ap(sc[:].rearrange("p a k -> p (a k)"), NS * 16, [Bs])
            return
        car = A.f32(64).rearrange("p (a k) -> p a k", a=4)
        Bcar = Buf()
        NB = 2
        def blk():
            return [A.f32(384) for _ in range(NB)], [Buf() for _ in range(NB)]
        UA, BUA = blk()
        KI_, BKI = [A.i32(384) for _ in range(NB)], [Buf() for _ in range(NB)]
        ES, BES = blk()
        EC, BEC = blk()
        t1, Bt1 = blk(); t2, Bt2 = blk()
        VR, BVR = blk(); VI, BVI = blk()
        GR, BGR = blk(); GI, BGI = blk()
        AF_, BAF = VR, BVR
        t3, Bt3 = GR, BGR
        t4, Bt4 = GI, BGI
        Pb = [[A.bf16(384) for _ in range(4)] for _ in range(NB)]
        BPb = [[Buf() for _ in range(4)] for _ in range(NB)]
        offt = A.f32(4)
        Boff = Buf()
        border = [0, 6, 5, 4, 3, 2, 1]
        Bck = [Buf() for _ in range(16)]
        Bofu = [Buf(), Buf()]

        def junction():
            cr, cim = car[:, 0, 8:16], car[:, 1, 8:16]
            ecn, esn = sc[:, ECN, 8:16], sc[:, ESN, 8:16]
            tr_, ti_ = car[:, 2, 8:16], car[:, 3, 8:16]
            RR = Bck[8:16] + [Bs, Bcar]
            WW = Bck[8:16] + [Bcar]
            self.tt("dve", tr_, cr, ecn, ALU.mult, R=RR, W=WW)
            self.tt("dve", ti_, cim, esn, ALU.mult, R=RR, W=WW)
            self.tt("dve", tr_, tr_, ti_, ALU.add, R=RR, W=WW)
            self.tt("dve", ti_, cim, ecn, ALU.mult, R=RR, W=WW)
            self.tt("dve", cim, cr, esn, ALU.mult, R=RR, W=WW)
            self.tt("dve", cim, ti_, cim, ALU.subtract, R=RR, W=WW)
            self.cp("dve", cr, tr_, R=RR, W=WW)

        def iteration(step, d, ch, g4, u):
            ri = step if d == 0 else border[step]
            c0, c1 = RANGES[ri]
            w = c1 - c0
            by = 6 + ch
            gp = ch * 4 + g4
            kk = d * 8 + gp
            sap = sc[:, SS, kk:kk + 1]
            o = offt[:, u:u + 1]
            self.ts("dve", o, sap, float(c0), 16.0, ALU.mult, ALU.add, R=[Bs], W=[Bofu[u]])
            self.ts("dve", UA[u][:, 0:w], self.iota0[:, 0:w], sap, o, ALU.mult, ALU.add,
                    R=[self.Biota, Bs, Bofu[u]], W=[BUA[u]])
            yield
            kf = KI_[u].bitcast(F32)
            self.ts("dve", kf[:, 0:w], UA[u][:, 0:w], 12582912.0, 12582912.0, ALU.add, ALU.subtract, R=[BUA[u]], W=[BKI[u]])
            self.tt("dve", UA[u][:, 0:w], UA[u][:, 0:w], kf[:, 0:w], ALU.subtract,
                    R=[BUA[u], BKI[u]], W=[BUA[u]])
            yield
            self.act(ES[u][:, 0:w], UA[u][:, 0:w], AF.Sin, R=[BUA[u]], W=[BES[u]], scale=TWO_PI)
            self.act(AF_[u][:, 0:w], UA[u][:, 0:w], AF.Abs, R=[BUA[u]], W=[BAF[u]])
            self.act(EC[u][:, 0:w], AF_[u][:, 0:w], AF.Sin, R=[BAF[u], self.Bcf], W=[BEC[u]],
                     scale=-TWO_PI, bias=hp)
            b0 = 2 * u
            b1 = 2 * u + 1
            pre = self.bank(b0, w)
            pim = self.bank(b1, w)
            self.mm(pre, Bm[:, kk, 0, :], ubf[:, ch, c0:c1], True, True, R=[BBm, Bu[ri]], W=[self.BPS[b0]])
            self.mm(pim, Bm[:, kk, 1, :], ubf[:, ch, c0:c1], True, True, R=[BBm, Bu[ri]], W=[self.BPS[b1]])
            yield
            self.tt("dve", t1[u][:, 0:w], pre, EC[u][:, 0:w], ALU.mult, R=[self.BPS[b0], BEC[u]], W=[Bt1[u]])
            self.tt("dve", t2[u][:, 0:w], pim, ES[u][:, 0:w], ALU.mult, R=[self.BPS[b1], BES[u]], W=[Bt2[u]])
            yield
            self.tt("dve", t3[u][:, 0:w], pim, EC[u][:, 0:w], ALU.mult, R=[self.BPS[b1], BEC[u]], W=[Bt3[u]])
            self.tt("dve", t4[u][:, 0:w], pre, ES[u][:, 0:w], ALU.mult, R=[self.BPS[b0], BES[u]], W=[Bt4[u]])
            self.tt("pool", VR[u][:, 0:w], t1[u][:, 0:w], t2[u][:, 0:w], ALU.add, R=[Bt1[u], Bt2[u]], W=[BVR[u]])
            yield
            self.tt("pool", VI[u][:, 0:w], t3[u][:, 0:w], t4[u][:, 0:w], ALU.subtract, R=[Bt3[u], Bt4[u]], W=[BVI[u]])
            yield
            rho_bc = sc[:, RHO, kk:kk + 1].to_broadcast([128, w])
            first = (step == 0)
            for (Vb, BV, Gb, BG, cidx) in ((VR[u], BVR[u], GR[u], BGR[u], 0), (VI[u], BVI[u], GI[u], BGI[u], 1)):
                init = 0.0 if first else car[:, cidx, kk:kk + 1]
                if d == 0:
                    vin, gout = Vb[:, 0:w], Gb[:, 0:w]
                    last = Gb[:, w - 1:w]
                else:
                    v0 = Vb[:, 0:w]
                    g0_ = Gb[:, 0:w]
                    vin = AP(v0.tensor, v0.offset + w - 1, [list(v0.ap[0]), [-1, w]])
                    gout = AP(g0_.tensor, g0_.offset + w - 1, [list(g0_.ap[0]), [-1, w]])
                    last = Gb[:, 0:1]
                self.scan(gout, rho_bc, vin, init, R=[BV, Bs, Bck[kk]], W=[BG])
                self.cp("pool", car[:, cidx, kk:kk + 1], last, R=[BG], W=[Bck[kk]])
                yield
            self.tt("dve", Pb[u][0][:, 0:w], GR[u][:, 0:w], EC[u][:, 0:w], ALU.mult, R=[BGR[u], BEC[u]], W=[BPb[u][0]])
            self.tt("pool", Pb[u][1][:, 0:w], GI[u][:, 0:w], ES[u][:, 0:w], ALU.mult, R=[BGI[u], BES[u]], W=[BPb[u][1]])
            yield
            self.tt("pool", Pb[u][2][:, 0:w], GR[u][:, 0:w], ES[u][:, 0:w], ALU.mult, R=[BGR[u], BES[u]], W=[BPb[u][2]])
            self.tt("pool", Pb[u][3][:, 0:w], GI[u][:, 0:w], EC[u][:, 0:w], ALU.mult, R=[BGI[u], BEC[u]], W=[BPb[u][3]])
            yield
            py = self.bank(by, w)[64 * (g4 // 2):64 * (g4 // 2) + 64, :]
            for pi, li_ in ((0, 0), (1, 1), (2, 2), (3, 2)):
                self.mm(py, Lm[:, kk, li_, :], Pb[u][pi][:, 0:w], pi == 0 and g4 % 2 == 0, pi == 3 and g4 % 2 == 1,
                        R=[BLm, BPb[u][pi]], W=[self.BPS[by]])
            if g4 == 3:
                self.tt("dve", ysb[:, ch, c0:c1], ysb[:, ch, c0:c1], self.bank(by, w), ALU.add,
                        R=[self.BPS[by], By[ch][ri]], W=[By[ch][ri]])
            yield

        def stream(u):
            if u == 1:
                for _ in range(5):
                    yield
            for step in range(7):
                for d in range(2):
                    if u == 0 and d == 1 and step == 1:
                        junction()
                    for ch in range(2):
                        for g4 in (u, u + 2):
                            yield from iteration(step, d, ch, g4, u)

        live = [stream(0), stream(1)]
        while live:
            for g in list(live):
                try:
                    next(g)
                except StopIteration:
                    live.remove(g)
        self.C.flush()
        A.release(m)
        m = A.mark()
        ysb = A.f32(2 * NT).rearrange("p (c t) -> p c t", c=2)
        gl = A.bf16(2 * NT).rearrange("p (c t) -> p c t", c=2)
        Bgl = [Buf() for _ in RANGES]
        wglu = A.bf16(2 * 512).rearrange("p (k n) -> p k n", k=2)
        Bwg = Buf()
        src = T["w_glu"][l].rearrange("(k p) n -> p k n", p=128)
        self.ld(wglu[:, :, :], src, [Bwg], cast=True)
        wout, Bwo = self.load_wout(l, 6, 2)
        sT = [A.bf16(2 * 384).rearrange("p (c t) -> p c t", c=2) for _ in range(2)]
        BsT = [Buf(), Buf()]
        ta = [A.f32(384) for _ in range(2)]; Bta = [Buf(), Buf()]
        tb = [A.f32(384) for _ in range(2)]; Btb = [Buf(), Buf()]
        tc_ = [A.f32(384) for _ in range(2)]; Btc = [Buf(), Buf()]
        rl = list(range(len(RANGES))) if self.need_ctx else list(range(1, len(RANGES)))
        q = 0
        pbk = 0
        for ri in rl:
            c0, c1 = RANGES[ri]
            w = c1 - c0
            sidx = 1 if ri == 0 else 0
            for ch in range(2):
                u = q % 2
                q += 1
                yy = ysb[:, ch, c0:c1]
                self.tt("pool", ta[u][:, 0:w], yy, yy, ALU.mult, R=[By[ch][ri]], W=[Bta[u]])
                self.ts("pool", ta[u][:, 0:w], ta[u][:, 0:w], 0.044715, 1.0, ALU.mult, ALU.add, R=[Bta[u]], W=[Bta[u]])
                self.tt("pool", ta[u][:, 0:w], ta[u][:, 0:w], yy, ALU.mult, R=[Bta[u], By[ch][ri]], W=[Bta[u]])
                self.act(tb[u][:, 0:w], ta[u][:, 0:w], AF.Sigmoid, R=[Bta[u]], W=[Btb[u]], scale=1.5957691216057308)
                self.tt("dve", gl[:, ch, c0:c1], yy, tb[u][:, 0:w], ALU.mult, R=[By[ch][ri], Btb[u]], W=[Bgl[ri]])
            su = ri % 2
            for ch in range(2):
                ba = pbk % 6
                bg = (pbk + 1) % 6
                pbk += 2
                pa = self.bank(ba, w)
                pg = self.bank(bg, w)
                for k in range(2):
                    self.mm(pa, wglu[:, k, ch * 128:(ch + 1) * 128], gl[:, k, c0:c1], k == 0, k == 1,
                            R=[Bwg, Bgl[ri]], W=[self.BPS[ba]])
                for k in range(2):
                    self.mm(pg, wglu[:, k, 256 + ch * 128:256 + (ch + 1) * 128], gl[:, k, c0:c1], k == 0, k == 1,
                            R=[Bwg, Bgl[ri]], W=[self.BPS[bg]])
                u = q % 2
                q += 1
                self.act(tc_[u][:, 0:w], pa, AF.Identity, R=[self.BPS[ba], self.Bvp], W=[Btc[u]],
                         bias=self.vp[:, VP_BGLU + ch:VP_BGLU + ch + 1])
                self.act(tb[u][:, 0:w], pg, AF.Sigmoid, R=[self.BPS[bg], self.Bvp], W=[Btb[u]],
                         bias=self.vp[:, VP_BGLU + 2 + ch:VP_BGLU + 3 + ch])
                self.tt("dve", sT[su][:, ch, 0:w], tc_[u][:, 0:w], tb[u][:, 0:w], ALU.mult, R=[Btc[u], Btb[u]], W=[BsT[su]])
            for oc in range(KC):
                b = pbk % 6
                pbk += 1
                po = self.bank(b, w)
                for k in range(2):
                    self.mm(po, wout[:, k, oc * 128:(oc + 1) * 128], sT[su][:, k, 0:w], k == 0, k == 1,
                            R=[Bwo, BsT[su]], W=[self.BPS[b]])
                dst = self.hT[:, oc, c0:c1]
                self.stt("dve", dst, po, self.Gmod[:, 1, oc, sidx:sidx + 1], dst, ALU.mult, ALU.add,
                         R=[self.BPS[b], self.Bmod, self.Bh[ri]], W=[self.Bh[ri]])
        self.C.flush()
        A.release(m)

    def mix_mla(self, l, m_xm):
        T = self.T
        A = self.A
        xm, Bxm = self.xm2, self.Bxm2
        cqn = A.bf16(3 * NT).rearrange("p (c t) -> p c t", c=3)
        ckvn = A.bf16(2 * NT).rearrange("p (c t) -> p c t", c=2)
        krT = A.f32(NT)
        Bcq, Bckv, Bkr = Buf(), Buf(), Buf()
        m1 = A.mark()
        wbuf = A.bf16(KC * 384).rearrange("p (k n) -> p k n", k=KC)
        Bw = Buf()
        wbuf2 = A.bf16(KC * 288).rearrange("p (k n) -> p k n", k=KC)
        Bw2 = Buf()
        sq = [A.f32(384) for _ in range(2)]
        Bsq = [Buf(), Buf()]
        rs = A.f32(384)
        Brs = Buf()
        self.load_win(l, 1152, 384, wbuf, Bw)
        self.load_win(l, 1536, 288, wbuf2, Bw2)
        q = 0
        for (src_w, Bsw, nch, dstT, Bd, gcol) in ((wbuf, Bw, 3, cqn, Bcq, VP_CQG), (wbuf2, Bw2, 2, ckvn, Bckv, VP_CKVG)):
            for ri in range(len(RANGES)):
                c0, c1 = RANGES[ri]
                w = c1 - c0
                pss = self.bank(4 + (ri % 2), w)
                Bss = self.BPS[4 + (ri % 2)]
                for cch in range(nch):
                    pz = self.bank(cch, w)
                    for k in range(KC):
                        self.mm(pz, src_w[:, k, cch * 128:(cch + 1) * 128], xm[:, k, c0:c1], k == 0, k == KC - 1,
                                R=[Bsw, Bxm[ri]], W=[self.BPS[cch]])
                    u = q % 2
                    q += 1
                    self.act(sq[u][:, 0:w], pz, AF.Square, R=[self.BPS[cch]], W=[Bsq[u]])
                    self.mm(pss, self.ones, sq[u][:, 0:w], cch == 0, cch == nch - 1, R=[Bsq[u], self.Bcf], W=[Bss])
                self.act(rs[:, 0:w], pss, AF.Ln, R=[Bss, self.Bcf], W=[Brs], bias=self.misc[:, 0:1],
                         scale=1.0 / (nch * 128))
                self.act(rs[:, 0:w], rs[:, 0:w], AF.Exp, R=[Brs], W=[Brs], scale=-0.5)
                for cch in range(nch):
                    self.stt("dve", dstT[:, cch, c0:c1], self.bank(cch, w), self.vp[:, gcol + cch:gcol + cch + 1],
                             rs[:, 0:w], ALU.mult, ALU.mult, R=[self.BPS[cch], Brs, self.Bvp], W=[Bd])
        for ri in range(len(RANGES)):
            c0, c1 = RANGES[ri]
            w = c1 - c0
            b = 6 + (ri % 2)
            pz = self.bank(b, w)[64:96, :]
            for k in range(KC):
                self.mm(pz, wbuf2[:, k, 256:288], xm[:, k, c0:c1], k == 0, k == KC - 1,
                        R=[Bw2, Bxm[ri]], W=[self.BPS[b]])
            self.cp("act", krT[64:96, c0:c1], pz, R=[self.BPS[b]], W=[Bkr])
        self.C.flush()
        A.release(m1)
        A.high_release()
        wuq = A.bf16(3 * 576).rearrange("p (k n) -> p k n", k=3)
        wukv = A.bf16(2 * 768).rearrange("p (k n) -> p k n", k=2)
        Bwq, Bwkv = Buf(), Buf()
        self.ld(wuq[:, :, :], T["w_uq"][l].rearrange("(k p) n -> p k n", p=128), [Bwq], cast=True)
        self.ld(wukv[:, :, :], T["w_ukv"][l].rearrange("(k p) n -> p k n", p=128), [Bwkv], cast=True)
        rope = A.f32(2 * NT).rearrange("p (a t) -> p a t", a=2)
        Brope = Buf()
        self.ld(rope[0:96, :, :], T["rope"], [Brope])
        b_tok = A.bf16(18 * 384).rearrange("p (t d) -> p t d", t=18)
        Bbt = Buf()
        mT = A.mark()
        QT = [A.bf16(NT) for _ in range(2)]
        KT = [A.bf16(NT) for _ in range(2)]
        VT = [A.bf16(18 * 65).rearrange("p (t d) -> p t d", t=18) for _ in range(2)]
        BQ, BK, BV = [Buf(), Buf()], [Buf(), Buf()], [Buf(), Buf()]
        for hb in range(2):
            self.memset("pool", VT[hb][:, :, 64:65], 1.0, [BV[hb]])
        NSL = 2
        kcat = [A.f32(384) for _ in range(NSL)]; Bkc = [Buf() for _ in range(NSL)]
        sqh = [A.f32(384) for _ in range(NSL)]; Bsqh = [Buf() for _ in range(NSL)]
        rsh = [A.f32(384) for _ in range(NSL)]; Brsh = [Buf() for _ in range(NSL)]
        xn = [A.f32(384) for _ in range(NSL)]; Bxn = [Buf() for _ in range(NSL)]
        r1 = [A.f32(384) for _ in range(NSL)]; Br1 = [Buf() for _ in range(NSL)]
        r2 = [A.f32(384) for _ in range(NSL)]; Br2 = [Buf() for _ in range(NSL)]
        pT = [A.bf16(512) for _ in range(3)]
        BpT = [Buf() for _ in range(3)]
        rc = A.f32(8)
        Brc = Buf()
        ones96 = self.ones[0:96, 0:96]
        qranges = list(range(len(RANGES))) if self.need_ctx else list(range(1, len(RANGES)))
        gb = [0]

        def gbank():
            gb[0] += 1
            return 6 + (gb[0] % 2)

        def chain(h, which, ri, u):
            hb = h % 2
            dst, Bdst = (QT[hb], BQ[hb]) if which == 0 else (KT[hb], BK[hb])
            gain = self.small[0:96, 1:2] if which == 0 else self.vp[0:96, VP_MK:VP_MK + 1]
            c0, c1 = RANGES[ri]
            w = c1 - c0
            b = 6 + u
            if which == 0:
                pz = self.bank(b, w)[0:96, :]
                for k in range(3):
                    self.mm(pz, wuq[:, k, h * 96:(h + 1) * 96], cqn[:, k, c0:c1], k == 0, k == 2,
                            R=[Bwq, Bcq], W=[self.BPS[b]])
            else:
                pz = self.bank(b, w)[0:64, :]
                for k in range(2):
                    self.mm(pz, wukv[:, k, h * 128:h * 128 + 64], ckvn[:, k, c0:c1], k == 0, k == 1,
                            R=[Bwkv, Bckv], W=[self.BPS[b]])
            yield
            if which == 0:
                self.cp("dve", kcat[u][0:96, 0:w], pz, R=[self.BPS[b]], W=[Bkc[u]])
            else:
                self.cp("dve", kcat[u][0:64, 0:w], pz, R=[self.BPS[b]], W=[Bkc[u]])
                self.cp("pool", kcat[u][64:96, 0:w], krT[64:96, c0:c1], R=[Bkr], W=[Bkc[u]])
            zsrc, Bz = kcat[u][0:96, 0:w], Bkc[u]
            yield
            self.tt("pool", sqh[u][0:96, 0:w], zsrc, zsrc, ALU.mult, R=[Bz], W=[Bsqh[u]])
            bs_ = 6 + u
            pss = self.bank(bs_, w)[0:96, :]
            self.mm(pss, ones96, sqh[u][0:96, 0:w], True, True, R=[Bsqh[u], self.Bcf], W=[self.BPS[bs_]])
            yield
            self.act(rsh[u][0:96, 0:w], pss, AF.Ln, R=[self.BPS[bs_], self.Bcf], W=[Brsh[u]],
                     bias=self.misc[0:96, 0:1], scale=1.0 / 96)
            yield
            self.act(rsh[u][0:96, 0:w], rsh[u][0:96, 0:w], AF.Exp, R=[Brsh[u]], W=[Brsh[u]], scale=-0.5)
            yield
            self.stt("dve", xn[u][0:96, 0:w], zsrc, gain, rsh[u][0:96, 0:w], ALU.mult, ALU.mult,
                     R=[Bz, Brsh[u], self.Bsmall, self.Bvp], W=[Bxn[u]])
            bp = 6 + u
            pp = self.bank(bp, w)[0:96, :]
            self.mm(pp, self.perm, xn[u][0:96, 0:w], True, True, R=[Bxn[u], self.Bcf], W=[self.BPS[bp]])
            yield
            self.tt("pool", r1[u][0:96, 0:w], xn[u][0:96, 0:w], rope[0:96, 0, c0:c1], ALU.mult,
                    R=[Bxn[u], Brope], W=[Br1[u]])
            self.tt("dve", r2[u][0:96, 0:w], pp, rope[0:96, 1, c0:c1], ALU.mult,
                    R=[self.BPS[bp], Brope], W=[Br2[u]])
            yield
            self.tt("pool", dst[0:96, c0:c1], r1[u][0:96, 0:w], r2[u][0:96, 0:w], ALU.add,
                    R=[Br1[u], Br2[u]], W=[Bdst])
            yield

        def chains(h, lst, u):
            for (which, ri) in lst:
                yield from chain(h, which, ri, u)

        def vgen(h, u):
            hb = h % 2
            for t in range(9 * u, 9 * u + 9):
                b = 6 + u
                pv = self.bank(b, 64)
                for k in range(2):
                    self.mm(pv, ckvn[:, k, t * 128:(t + 1) * 128], wukv[:, k, h * 128 + 64:h * 128 + 128], k == 0, k == 1,
                            R=[Bckv, Bwkv], W=[self.BPS[b]])
                self.cp("dve", VT[hb][:, t, 0:64], pv, R=[self.BPS[b]], W=[BV[hb]])
                yield

        def merged(gens):
            live = list(gens)
            while live:
                for g in list(live):
                    try:
                        next(g)
                    except StopIteration:
                        live.remove(g)
                yield

        def gen(h):
            lst = [(0, ri) for ri in qranges] + [(1, ri) for ri in range(len(RANGES))]
            def stream(u):
                yield from chains(h, lst[u::2], u)
                yield from vgen(h, u)
            return merged([stream(0), stream(1)])

        def attn(h):
            hb = h % 2
            qblocks = [(256 + 512 * i, 256 + 512 * (i + 1), list(range(18))) for i in range(4)]
            if self.need_ctx:
                qblocks.append((0, 256, [0, 1]))
            seq = []
            for (qc0, qc1, kts) in qblocks:
                for ki_, kt in enumerate(kts):
                    seq.append((qc0, qc1, ki_, kt, len(kts)))

            def emit_S(i):
                qc0, qc1, ki_, kt, nk = seq[i]
                b = 4 + (i % 2)
                pS = self.bank(b, qc1 - qc0)
                self.mm(pS, KT[hb][0:96, kt * 128:(kt + 1) * 128], QT[hb][0:96, qc0:qc1], True, True,
                        R=[BK[hb], BQ[hb]], W=[self.BPS[b]])

            emit_S(0)
            for i, (qc0, qc1, ki_, kt, nk) in enumerate(seq):
                b = 4 + (i % 2)
                u = i % 3
                nq = (qc1 - qc0) // 128
                self.act(pT[u][:, 0:qc1 - qc0], self.bank(b, qc1 - qc0), AF.Exp, R=[self.BPS[b]], W=[BpT[u]])
                if i + 1 < len(seq):
                    emit_S(i + 1)
                for qs in range(nq):
                    self.mm(self.bank(qs, 65), pT[u][:, qs * 128:(qs + 1) * 128], VT[hb][:, kt, :],
                            ki_ == 0, ki_ == nk - 1, R=[BpT[u], BV[hb]], W=[self.BPS[qs]])
                if ki_ == nk - 1:
                    for qs in range(nq):
                        t = (qc0 // 128) + qs
                        po = self.bank(qs, 65)
                        self.recip(rc[:, qs:qs + 1], po[:, 64:65], R=[self.BPS[qs]], W=[Brc])
                        self.ts("dve", b_tok[:, t, h * 64:(h + 1) * 64], po[:, 0:64], rc[:, qs:qs + 1], None, ALU.mult, None,
                                R=[self.BPS[qs], Brc], W=[Bbt])
                yield

        for _ in gen(0):
            pass
        for h in range(6):
            streams = [attn(h)]
            if h + 1 < 6:
                streams.append(gen(h + 1))
            for _ in merged(streams):
                pass
        self.C.flush()
        A.release(mT)
        wout, Bwo = self.load_wout(l, 3, 3)
        tiles = list(range(2, 18)) + ([0, 1] if self.need_ctx else [])
        self.outproj_tiles(l, b_tok, Bbt, tiles, 3, 3, wout, Bwo)

    def final(self, out):
        A = self.A
        m = A.mark()
        stg = [A.f32(D) for _ in range(2)]
        Bst = [Buf(), Buf()]
        for t in range(2, 18):
            s = t % 2
            ri = self.ranges_of(t * 128, t * 128 + 128)
            for half in range(2):
                b = (2 * t + half) % 8
                pb = self.bank(b).rearrange("p (a q) -> p a q", a=4)
                for a in range(4):
                    c = half * 4 + a
                    self.tr(pb[:, a, :], self.hT[:, c, t * 128:(t + 1) * 128], self.ident,
                            R=[self.Bh[r] for r in ri] + [self.Bcf], W=[self.BPS[b]])
                self.cp("act" if half == 0 else "dve", stg[s][:, half * 512:(half + 1) * 512],
                        self.bank(b), R=[self.BPS[b]], W=[Bst[s]])
            self.st(out[(t - 2) * 128:(t - 1) * 128, :], stg[s], R=[Bst[s]])
        self.C.flush()
        A.release(m)


def _na_bias_index():
    W = 64
    rows = 32

    def row_start(r):
        return min(max(r - 4, 0), rows - 8)

    def col_start(c):
        return min(max(c - 8, 0), W - 16)

    tiles = []
    for dm in range(-2, 3):
        tiles.append((6, 6 + dm))
    for i in (0, 1, 14, 15):
        mlo = {0: 0, 1: 0, 14: 12, 15: 12}[i]
        for m in range(mlo, mlo + 4):
            tiles.append((i, m))
    idx = np.full((21, 128, 128), 465, dtype=np.int64)
    kk = np.arange(128)
    kr2, kc = kk // 64, kk % 64
    for ti, (i, m) in enumerate(tiles):
        for q in range(128):
            qr = 2 * i + q // 64
            qc = q % 64
            krow = 2 * m + kr2
            rs = row_start(qr)
            cs = col_start(qc)
            ok = (krow >= rs) & (krow < rs + 8) & (kc >= cs) & (kc < cs + 16)
            drow = krow - qr + 7
            dcol = np.clip(kc - qc, -15, 15) + 15
            val = drow * 31 + dcol
            idx[ti, :, q] = np.where(ok, val, 465)
    return idx


_CONST = {}


def _consts():
    if _CONST:
        return _CONST
    cf = np.zeros((128, 5, 128), np.float32)
    cf[:, 0, :] = np.eye(128, dtype=np.float32)
    cf[:, 1, :] = 1.0
    cf[0:64, 2, 0:64] = 1.0
    cf[64:128, 2, 64:128] = 1.0
    P = np.zeros((128, 128), np.float32)
    for base in (64, 80):
        for j in range(8):
            P[base + 8 + j, base + j] = -1.0
            P[base + j, base + 8 + j] = 1.0
    cf[:, 3, :] = P
    cf[:, 4, 0] = EPS
    cf[:, 4, 1] = math.pi / 2
    cf[:, 4, 16:24] = 1.0
    cf[:, 4, 24:32] = -1.0
    rope = np.zeros((96, 2, NT), np.float32)
    rope[:, 0, :] = 1.0
    t = np.arange(NX)
    row = (t // 64).astype(np.float32)
    col = (t % 64).astype(np.float32)
    inv = (10000.0 ** (-np.arange(8, dtype=np.float32) / 8)).astype(np.float32)
    for base, pos in ((64, row), (80, col)):
        ang = pos[None, :] * inv[:, None]
        c = np.cos(ang).astype(np.float32)
        s = np.sin(ang).astype(np.float32)
        rope[base:base + 8, 0, LCTX:] = c
        rope[base + 8:base + 16, 0, LCTX:] = c
        rope[base:base + 8, 1, LCTX:] = s
        rope[base + 8:base + 16, 1, LCTX:] = s
    iota = np.tile(np.arange(384, dtype=np.float32)[None, :], (128, 1))
    _CONST.update(cf=cf, rope=rope, iota=iota, nidx=_na_bias_index())
    return _CONST


def _prep_shared(inp):
    c = _consts()
    NL = 2
    sh = {}
    vp = np.zeros((NL, 128, 128), np.float32)
    for l in range(NL):
        vp[l, 0:72] = inp["b_mod"][l].reshape(72, 128)
        vp[l, 72:96] = inp["norm_g"][l].reshape(24, 128)
        vp[l, 96:98] = inp["s5_d"][l].reshape(2, 128)
        vp[l, 98:102] = inp["s5_b_glu"][l].reshape(4, 128)
        vp[l, 102:105] = inp["mla_cq_g"][l].reshape(3, 128)
        vp[l, 105:107] = inp["mla_ckv_g"][l].reshape(2, 128)
        vp[l, 107] = np.tile(inp["na_qk_g"][l, 0], 2)
        vp[l, 108] = np.tile(inp["na_qk_g"][l, 1], 2)
        vp[l, 109, 0:96] = inp["mla_qk_g"][l, 0]
        vp[l, 110, 0:96] = inp["mla_qk_g"][l, 1]
    sh["vecpack"] = vp
    for k in ("w_mod", "ffn_w_gu", "ffn_w_down", "w_in", "w_out", "mla_w_uq", "mla_w_ukv", "s5_w_glu"):
        sh[k] = np.ascontiguousarray(inp[k], dtype=np.float32)
    nidx = c["nidx"]
    nab = np.empty((NL, 128, 126, 128), np.float32)
    for l in range(NL):
        for h in range(6):
            ext = np.concatenate([inp["na_rpb"][l, h].reshape(-1), np.array([NEG], np.float32)])
            g = ext[nidx]
            nab[l, :, h * 21:(h + 1) * 21, :] = g.transpose(1, 0, 2)
    sh["nab"] = nab
    s5p = np.empty((NL, 128, 3, 16), np.float32)
    s5B = np.zeros((NL, 128, 16, 2, 128), np.float32)
    s5C = np.zeros((NL, 128, 16, 2, 64), np.float32)
    for l in range(NL):
        for name, a in (("s5_lam_re", 0), ("s5_lam_im", 1)):
            v = inp[name][l].reshape(2, 8, 2, 64)
            s5p[l, :, a, :] = v.transpose(2, 3, 0, 1).reshape(128, 16)
        ld = inp["s5_log_dt"][l].reshape(2, 8, 2)
        s5p[l, :, 2, :] = np.broadcast_to(ld.transpose(2, 0, 1)[:, None, :, :], (2, 64, 2, 8)).reshape(128, 16)
        for ci, (bn, cn) in enumerate((("s5_b_re", "s5_c_re"), ("s5_b_im", "s5_c_im"))):
            b = inp[bn][l]
            cm = inp[cn][l]
            for d in range(2):
                for gp in range(8):
                    kk = d * 8 + gp
                    for g2 in range(2):
                        g = 2 * gp + g2
                        r0 = 32 * (gp % 4) + 16 * g2
                        s5B[l, r0:r0 + 16, kk, ci, g2 * 64:(g2 + 1) * 64] = b[d, g].T
                        s5C[l, g2 * 64:(g2 + 1) * 64, kk, ci, 32 * (gp % 2) + g2 * 16:32 * (gp % 2) + (g2 + 1) * 16] = cm[d, g].T
    sh["s5p"], sh["s5B"], sh["s5C"] = s5p, s5B, s5C
    sh["cf32"], sh["rope"], sh["iota"] = c["cf"], c["rope"], c["iota"]
    return sh


def _in_maps(inp, cores):
    sh = _prep_shared(inp)
    maps = []
    for b in cores:
        m = dict(sh)
        m["x"] = np.ascontiguousarray(inp["x"][b], dtype=np.float32)
        m["ctx"] = np.ascontiguousarray(inp["ctx"][b], dtype=np.float32)
        cv = np.empty((16, 128), np.float32)
        cv[0:8] = inp["c"][b].reshape(8, 128)
        cv[8:16] = inp["c_ctx"].reshape(8, 128)
        m["cvec"] = cv
        maps.append(m)
    return maps


_NC_CACHE = {}


def kernel(**inputs):
    inp = {k: np.asarray(v) for k, v in inputs.items()}
    if "full" not in _NC_CACHE:
        _NC_CACHE["full"] = K(2, None).build()
    nc = _NC_CACHE["full"]
    maps = _in_maps(inp, list(range(8)))
    res = run_bass_kernel_spmd(nc, maps, core_ids=list(range(8)))
    return np.stack([np.asarray(r["out"], dtype=np.float32) for r in res.results], axis=0)
```

```python
import math
import numpy as np
from contextlib import ExitStack
import concourse.bass as bass
import concourse.mybir as mybir
from concourse.ap import AP
from concourse.bass_utils import run_bass_kernel_spmd

F32 = mybir.dt.float32
BF16 = mybir.dt.bfloat16
I32 = mybir.dt.int32
AF = mybir.ActivationFunctionType
ALU = mybir.AluOpType

D = 1024
NT = 2304
LCTX = 256
NX = 2048
DFF = 2816
KC = 8
EPS = 1e-6
RANGES = [(0, 256)] + [(256 + 384 * k, 256 + 384 * (k + 1)) for k in range(5)] + [(2176, 2304)]
NEG = -30000.0
TWO_PI = 2.0 * math.pi

VP_BMOD = 0
VP_NG = 72
VP_S5D = 96
VP_BGLU = 98
VP_CQG = 102
VP_CKVG = 105
VP_NAQ = 107
VP_NAK = 108
VP_MQ = 109
VP_MK = 110


class Buf:
    __slots__ = ("name", "w", "r", "ps")

    def __init__(self, name="", ps=False):
        self.name = name
        self.w = None
        self.r = {}
        self.ps = ps


class Eng:
    def __init__(self, name, sem):
        self.name = name
        self.sem = sem
        self.cnt = 0
        self.seen = {}
        self.ops = []


class Ctx:
    def __init__(self, nc, sems, dma_sems):
        self.nc = nc
        self.E = {n: Eng(n, sems[n]) for n in ("pe", "act", "dve", "pool", "sp")}
        self.semobj = {("c", n): sems[n] for n in self.E}
        self.dma_sems = dma_sems
        self.dma_cnt = {q: [0] * len(v) for q, v in dma_sems.items()}
        self.dma_rr = {q: 0 for q in dma_sems}
        for q, v in dma_sems.items():
            for i, s in enumerate(v):
                self.semobj[("d", q, i)] = s
        self.ninstr = 0

    def _need(self, E, ev, waits):
        if ev is None:
            return
        key, val = ev
        if E.name == "pe" and key == ("c", "pe"):
            return
        if E.seen.get(key, 0) >= val:
            return
        if waits.get(key, 0) < val:
            waits[key] = val

    def _deps(self, E, reads, writes, accumulate):
        waits = {}
        for b in reads:
            self._need(E, b.w, waits)
            if b.ps:
                for k, v in b.r.items():
                    if k != ("c", E.name):
                        self._need(E, (k, v), waits)
        for b in writes:
            if not (accumulate and b.w is not None and b.w[0] == ("c", E.name)):
                self._need(E, b.w, waits)
            for k, v in b.r.items():
                self._need(E, (k, v), waits)
        for k, v in waits.items():
            E.seen[k] = v
        return list(waits.items())

    def op(self, eng, fn, reads=(), writes=(), accumulate=False):
        E = self.E[eng]
        waits = self._deps(E, reads, writes, accumulate)
        E.cnt += 1
        ev = (("c", eng), E.cnt)
        sem = E.sem
        semobj = self.semobj

        def run(e):
            for k, v in waits:
                e.wait_ge(semobj[k], v)
            fn(e).then_inc(sem, 1)

        E.ops.append(run)
        for b in writes:
            b.w = ev
            b.r = {}
        for b in reads:
            b.r[ev[0]] = ev[1]
        self.ninstr += 1
        return ev

    def dma(self, queue, fn, reads=(), writes=()):
        E = self.E[queue]
        waits = self._deps(E, reads, writes, False)
        i = self.dma_rr[queue]
        self.dma_rr[queue] = (i + 1) % len(self.dma_sems[queue])
        key = ("d", queue, i)
        prev = self.dma_cnt[queue][i]
        if prev > 0 and E.seen.get(key, 0) < prev:
            waits.append((key, prev))
            E.seen[key] = prev
        val = prev + 16
        self.dma_cnt[queue][i] = val
        sem = self.semobj[key]
        semobj = self.semobj

        def run(e):
            for k, v in waits:
                e.wait_ge(semobj[k], v)
            fn(e).then_inc(sem, 16)

        E.ops.append(run)
        ev = (key, val)
        for b in writes:
            b.w = ev
            b.r = {}
        for b in reads:
            b.r[key] = val
        self.ninstr += 1
        return ev

    def barrier(self):
        evs = []
        for n, E in self.E.items():
            if E.cnt > 0:
                evs.append((("c", n), E.cnt))
        for q, cnts in self.dma_cnt.items():
            for i, c in enumerate(cnts):
                if c > 0:
                    evs.append((("d", q, i), c))
        semobj = self.semobj
        for n, E in self.E.items():
            waits = []
            for k, v in evs:
                if k == ("c", n):
                    continue
                if E.seen.get(k, 0) < v:
                    waits.append((k, v))
                    E.seen[k] = v

            def run(e, waits=waits):
                for k, v in waits:
                    e.wait_ge(semobj[k], v)

            E.ops.append(run)

    def flush(self):
        self.barrier()
        E = self.E
        with self.nc.Block() as block:
            @block.tensor
            def _(e):
                for f in E["pe"].ops:
                    f(e)

            @block.scalar
            def _(e):
                for f in E["act"].ops:
                    f(e)

            @block.vector
            def _(e):
                for f in E["dve"].ops:
                    f(e)

            @block.gpsimd
            def _(e):
                for f in E["pool"].ops:
                    f(e)

            @block.sync
            def _(e):
                for f in E["sp"].ops:
                    f(e)
        for n in E:
            E[n].ops = []


class Arena:
    def __init__(self, ap, words):
        self.ap = ap
        self.words = words
        self.total = words
        self.top = 0

    def high_f32(self, n):
        self.words -= n
        assert self.top <= self.words
        return self.ap[:, self.words:self.words + n]

    def high_release(self):
        self.words = self.total

    def mark(self):
        return self.top

    def release(self, m):
        self.top = m

    def f32(self, n):
        a = self.top
        assert a + n <= self.words, ("arena overflow", a, n, self.words)
        self.top = a + n
        return self.ap[:, a:a + n]

    def bf16(self, n):
        w = (n + 1) // 2
        return self.f32(w).bitcast(BF16)

    def i32(self, n):
        return self.f32(n).bitcast(I32)


class K:
    def __init__(self, n_layers=2, stop=None):
        self.n_layers = n_layers
        self.stop = stop

    def mm(self, out, lhsT, rhs, start, stop, R, W):
        self.C.op("pe", lambda e: e.matmul(out, lhsT=lhsT, rhs=rhs, start=start, stop=stop),
                  reads=R, writes=W, accumulate=not start)

    def tr(self, out, in_, ident, R, W):
        self.C.op("pe", lambda e: e.transpose(out=out, in_=in_, identity=ident), reads=R, writes=W)

    def act(self, out, in_, func, R, W, bias=None, scale=None):
        kw = {}
        if bias is not None:
            kw["bias"] = bias
        if scale is not None:
            kw["scale"] = scale
        self.C.op("act", lambda e: e.activation(out=out, in_=in_, func=func, **kw), reads=R, writes=W)

    def tt(self, eng, out, in0, in1, op, R, W):
        self.C.op(eng, lambda e: e.tensor_tensor(out=out, in0=in0, in1=in1, op=op), reads=R, writes=W)

    def ts(self, eng, out, in0, s1, s2, op0, op1, R, W):
        if op1 is None:
            self.C.op(eng, lambda e: e.tensor_scalar(out=out, in0=in0, scalar1=s1, scalar2=None, op0=op0),
                      reads=R, writes=W)
        else:
            self.C.op(eng, lambda e: e.tensor_scalar(out=out, in0=in0, scalar1=s1, scalar2=s2, op0=op0, op1=op1),
                      reads=R, writes=W)

    def stt(self, eng, out, in0, scalar, in1, op0, op1, R, W):
        self.C.op(eng, lambda e: e.scalar_tensor_tensor(out=out, in0=in0, scalar=scalar, in1=in1, op0=op0, op1=op1),
                  reads=R, writes=W)

    def cp(self, eng, out, in_, R, W):
        if eng == "act":
            self.C.op("act", lambda e: e.activation(out=out, in_=in_, func=AF.Copy), reads=R, writes=W)
        else:
            self.C.op(eng, lambda e: e.tensor_copy(out=out, in_=in_), reads=R, writes=W)

    def recip(self, out, in_, R, W):
        self.C.op("dve", lambda e: e.reciprocal(out=out, in_=in_), reads=R, writes=W)

    def memset(self, eng, ap, val, W):
        self.C.op(eng, lambda e: e.memset(ap, val), reads=[], writes=W)

    def scan(self, out, d0, d1, init, R, W):
        self.C.op("dve", lambda e: e.tensor_tensor_scan(out=out, data0=d0, data1=d1, initial=init,
                                                        op0=ALU.mult, op1=ALU.add), reads=R, writes=W)

    def ld(self, out, in_, W, R=(), cast=False):
        q = "pool" if cast else "sp"
        self.C.dma(q, lambda e: e.dma_start(out=out, in_=in_), reads=list(R), writes=W)

    def st(self, out, in_, R):
        self.C.dma("sp", lambda e: e.dma_start(out=out, in_=in_), reads=R, writes=[])

    def bank(self, i, w=512):
        return self.PS[i][:, 0:w]

    def build(self):
        nc = bass.Bass("TRN2", target_bir_lowering=False)
        self.nc = nc
        NL = 2

        def din(name, shape):
            return nc.dram_tensor(name, list(shape), F32, kind="ExternalInput").ap()

        T = {}
        T["x"] = din("x", [NX, D])
        T["ctx"] = din("ctx", [LCTX, D])
        T["cvec"] = din("cvec", [16, 128])
        T["vecpack"] = din("vecpack", [NL, 128, 128])
        T["w_mod"] = din("w_mod", [NL, D, 9 * D])
        T["w_gu"] = din("ffn_w_gu", [NL, 2, D, 2 * DFF])
        T["w_down"] = din("ffn_w_down", [NL, 2, DFF, D])
        T["w_in"] = din("w_in", [NL, D, 2080])
        T["w_out"] = din("w_out", [NL, D, D])
        T["w_uq"] = din("mla_w_uq", [NL, 384, 576])
        T["w_ukv"] = din("mla_w_ukv", [NL, 256, 768])
        T["w_glu"] = din("s5_w_glu", [NL, 256, 512])
        T["nab"] = din("nab", [NL, 128, 126, 128])
        T["s5p"] = din("s5p", [NL, 128, 3, 16])
        T["s5B"] = din("s5B", [NL, 128, 16, 2, 128])
        T["s5C"] = din("s5C", [NL, 128, 16, 2, 64])
        T["cf32"] = din("cf32", [128, 5, 128])
        T["rope"] = din("rope", [96, 2, NT])
        T["iota"] = din("iota", [128, 384])
        self.T = T
        out = nc.dram_tensor("out", [NX, D], F32, kind="ExternalOutput").ap()
        self.dbg = None
        if self.stop is not None:
            self.dbg = nc.dram_tensor("dbg", [128, KC * NT], F32, kind="ExternalOutput").ap()

        with ExitStack() as es:
            def sb(name, shape, dt):
                return es.enter_context(nc.sbuf_tensor(name, shape, dt))

            sems = {n: es.enter_context(nc.semaphore("s_" + n)) for n in ("pe", "act", "dve", "pool", "sp")}
            dsems = {q: [es.enter_context(nc.semaphore(f"d_{q}{i}")) for i in range(8)] for q in ("sp", "pool")}
            self.C = Ctx(nc, sems, dsems)
            self.PS = [es.enter_context(nc.psum_tensor(f"ps{i}", [128, 512], F32)) for i in range(8)]
            self.BPS = [Buf(f"ps{i}", ps=True) for i in range(8)]
            AW = 49152
            arena_t = sb("arena", [128, AW], F32)
            self.A = Arena(arena_t[:], AW)
            self.cf = sb("cf", [128, 5, 128], F32)
            self.Bcf = Buf("cf")
            self.identb = sb("identb", [128, 128], BF16)
            self.Bidb = Buf("identb")
            self.iota0 = sb("iota0", [128, 384], F32)
            self.Biota = Buf("iota")
            self.vp = sb("vp", [128, 128], F32)
            self.Bvp = Buf("vp")
            self.modT = sb("modT", [128, 72, 2], F32)
            self.Amod = sb("Amod", [128, 3, 8, 2], F32)
            self.Gmod = sb("Gmod", [128, 3, 8, 2], F32)
            self.Bmod = Buf("mod")
            self.scT = sb("scT", [128, 2, 8], BF16)
            self.Bsc = Buf("scT")
            self.small = sb("small", [128, 64], F32)
            self.Bsmall = Buf("small")

            self.hT = self.A.f32(KC * NT).rearrange("p (c t) -> p c t", c=KC)
            self.Bh = [Buf(f"h{r}") for r in range(len(RANGES))]

            self.setup()
            if self.stop == "load":
                self.dump_h()
            else:
                for l in range(self.n_layers):
                    if not self.layer(l):
                        break
                else:
                    self.final(out)
            self.C.flush()
        return nc

    def ranges_of(self, c0, c1):
        return [i for i, (a, b) in enumerate(RANGES) if a < c1 and b > c0]

    def dump_h(self):
        self.C.flush()
        self.st(self.dbg, self.hT.rearrange("p c t -> p (c t)"), R=self.Bh)

    def dump_ap(self, ap, nwords, R):
        self.C.flush()
        self.st(self.dbg[:, 0:nwords], ap, R=R)

    def setup(self):
        T = self.T
        self.ld(self.cf[:], T["cf32"], [self.Bcf])
        self.ld(self.identb[:], T["cf32"][:, 0, :], [self.Bidb], cast=True)
        self.ld(self.iota0[:], T["iota"], [self.Biota])
        self.ident = self.cf[:, 0, :]
        self.ones = self.cf[:, 1, :]
        self.bones = self.cf[:, 2, :]
        self.perm = self.cf[0:96, 3, 0:96]
        self.misc = self.cf[:, 4, :]
        m = self.A.mark()
        stg = [self.A.f32(D) for _ in range(2)]
        Bstg = [Buf("stg0"), Buf("stg1")]
        for t in range(18):
            src = T["ctx"][t * 128:(t + 1) * 128, :] if t < 2 else T["x"][(t - 2) * 128:(t - 1) * 128, :]
            s = t % 2
            self.ld(stg[s], src, [Bstg[s]])
            rs = self.ranges_of(t * 128, t * 128 + 128)
            for half in range(2):
                b = (2 * t + half) % 8
                pb = self.bank(b).rearrange("p (a q) -> p a q", a=4)
                for a in range(4):
                    c = half * 4 + a
                    self.tr(pb[:, a, :], stg[s][:, c * 128:(c + 1) * 128], self.ident,
                            R=[Bstg[s], self.Bcf], W=[self.BPS[b]])
                eng = "act" if half == 0 else "dve"
                self.cp(eng, self.hT[:, half * 4:half * 4 + 4, t * 128:(t + 1) * 128], pb,
                        R=[self.BPS[b]], W=[self.Bh[r] for r in rs])
        self.C.flush()
        self.A.release(m)

    def layer(self, l):
        self.l = l
        self.adaln(l)
        if self.stop == f"mod{l}":
            self.dump_ap(self.modT[:].rearrange("p a b -> p (a b)"), 144, [self.Bmod])
            return False
        self.ffn(l, 0, 0)
        if self.stop == f"ffn1_{l}":
            self.dump_h()
            return False
        if not self.mixer(l):
            return False
        if self.stop == f"mix_{l}":
            self.dump_h()
            return False
        self.ffn(l, 1, 2)
        if self.stop == f"ffn2_{l}":
            self.dump_h()
            return False
        return True

    def adaln(self, l):
        T = self.T
        A = self.A
        m = A.mark()
        stg = A.f32(128)
        Bst = Buf()
        b = 0
        if l == 0:
            self.ld(stg[0:16, :], T["cvec"], [Bst])
            self.tr(self.bank(b)[:, 0:16], stg[0:16, :], self.ident[0:16, 0:16], R=[Bst, self.Bcf], W=[self.BPS[b]])
            self.act(self.scT[:], self.bank(b)[:, 0:16].rearrange("p (s k) -> p s k", s=2), AF.Silu, R=[self.BPS[b]], W=[self.Bsc])
        stg2 = A.f32(128)
        Bst2 = Buf()
        self.ld(stg2, T["vecpack"][l], [Bst2])
        self.tr(self.bank(1)[:, 0:128], stg2, self.ident, R=[Bst2, self.Bcf], W=[self.BPS[1]])
        self.cp("dve", self.vp[:], self.bank(1)[:, 0:128], R=[self.BPS[1]], W=[self.Bvp])
        NP = 8
        PW = 1152
        wb = [A.bf16(KC * PW).rearrange("p (k n) -> p k n", k=KC) for _ in range(2)]
        Bwb = [Buf(), Buf()]
        pm = self.bank(2)
        wsrc = T["w_mod"][l].rearrange("(k p) n -> p k n", p=128)
        for pc in range(NP):
            s = pc % 2
            self.ld(wb[s][:, :, :], wsrc[:, :, pc * PW:(pc + 1) * PW], [Bwb[s]], cast=True)
            for j in range(9):
                cc = pc * 9 + j
                for k in range(KC):
                    self.mm(pm[:, 2 * cc:2 * cc + 2], wb[s][:, k, j * 128:(j + 1) * 128],
                            self.scT[:, :, k], k == 0, k == KC - 1, R=[Bwb[s], self.Bsc], W=[self.BPS[2]])
        self.tt("dve", self.modT[:], pm[:, 0:144].rearrange("p (a b) -> p a b", b=2),
                self.vp[:, 0:72].unsqueeze(2).to_broadcast([128, 72, 2]), ALU.add,
                R=[self.BPS[2], self.Bvp], W=[self.Bmod])
        for n in range(3):
            sc = self.modT[:, (3 * n + 1) * 8:(3 * n + 1) * 8 + 8, :]
            gt = self.modT[:, (3 * n + 2) * 8:(3 * n + 2) * 8 + 8, :]
            g = self.vp[:, VP_NG + 8 * n:VP_NG + 8 * n + 8].unsqueeze(2).to_broadcast([128, 8, 2])
            self.stt("dve", self.Amod[:, n, :, :], sc, 1.0, g, ALU.add, ALU.mult, R=[self.Bmod, self.Bvp], W=[self.Bmod])
            self.ts("dve", self.Gmod[:, n, :, :], gt, 0.5 if n != 1 else 1.0, None, ALU.mult, None,
                    R=[self.Bmod], W=[self.Bmod])
        self.ts("dve", self.small[:, 0:1], self.vp[:, VP_NAQ:VP_NAQ + 1], 0.125, None, ALU.mult, None,
                R=[self.Bvp], W=[self.Bsmall])
        self.ts("dve", self.small[:, 1:2], self.vp[:, VP_MQ:VP_MQ + 1], 96.0 ** -0.5, None, ALU.mult, None,
                R=[self.Bvp], W=[self.Bsmall])
        self.C.flush()
        A.release(m)

    def shift_ap(self, n, c, s):
        return self.modT[:, 3 * n * 8 + c, s:s + 1]

    def modulate(self, n, xm, Bxm, rlist):
        A = self.A
        m = A.mark()
        NSL = 3
        sq = [[A.f32(384) for _ in range(2)] for _ in range(NSL)]
        Bsq = [[Buf(), Buf()] for _ in range(NSL)]
        rstd = [A.f32(384) for _ in range(NSL)]
        Brs = [Buf() for _ in range(NSL)]
        tmp = [[A.f32(384) for _ in range(2)] for _ in range(NSL)]
        Btmp = [[Buf(), Buf()] for _ in range(NSL)]

        def chain(ri, u):
            c0, c1 = RANGES[ri]
            w = c1 - c0
            s = 1 if ri == 0 else 0
            b = u
            pb = self.bank(b, w)
            for c in range(KC):
                v = c % 2
                self.tt("pool", sq[u][v][:, 0:w], self.hT[:, c, c0:c1], self.hT[:, c, c0:c1], ALU.mult,
                        R=[self.Bh[ri]], W=[Bsq[u][v]])
                self.mm(pb, self.ones, sq[u][v][:, 0:w], c == 0, c == KC - 1, R=[Bsq[u][v], self.Bcf], W=[self.BPS[b]])
                if c % 2 == 1:
                    yield
            self.act(rstd[u][:, 0:w], pb, AF.Ln, R=[self.BPS[b], self.Bcf], W=[Brs[u]],
                     bias=self.misc[:, 0:1], scale=1.0 / D)
            yield
            self.act(rstd[u][:, 0:w], rstd[u][:, 0:w], AF.Exp, R=[Brs[u]], W=[Brs[u]], scale=-0.5)
            yield
            for c in range(KC):
                v = c % 2
                self.tt("dve", tmp[u][v][:, 0:w], self.hT[:, c, c0:c1], rstd[u][:, 0:w], ALU.mult,
                        R=[self.Bh[ri], Brs[u]], W=[Btmp[u][v]])
                self.act(xm[:, c, c0:c1], tmp[u][v][:, 0:w], AF.Identity, R=[Btmp[u][v], self.Bmod], W=[Bxm[ri]],
                         bias=self.shift_ap(n, c, s), scale=self.Amod[:, n, c, s:s + 1])
                if c % 2 == 1:
                    yield

        def stream(u):
            for _ in range(3 * u):
                yield
            for ri in rlist[u::NSL]:
                yield from chain(ri, u)

        live = [stream(u) for u in range(NSL)]
        while live:
            for g in list(live):
                try:
                    next(g)
                except StopIteration:
                    live.remove(g)
        self.C.flush()
        A.release(m)

    def ffn(self, l, f, n):
        T = self.T
        A = self.A
        rlist = list(range(len(RANGES)))
        if l == self.n_layers - 1 and f == 1:
            rlist = rlist[1:]
        m0 = A.mark()
        xm = A.bf16(KC * NT).rearrange("p (c t) -> p c t", c=KC)
        Bxm = [Buf() for _ in RANGES]
        GS = 4
        groups = [(j0, min(GS, 22 - j0)) for j0 in range(0, 22, GS)]
        wgu = [A.bf16(KC * 2 * 512).rearrange("p (k g n) -> p k g n", k=KC, g=2) for _ in range(2)]
        Bwgu = [Buf(), Buf()]
        wdn = [A.bf16(GS * D).rearrange("p (j n) -> p j n", j=GS) for _ in range(2)]
        Bwdn = [Buf(), Buf()]
        stmp = [A.f32(384) for _ in range(2)]
        Bstmp = [Buf(), Buf()]
        gsrc = T["w_gu"][l, f].rearrange("(k p) n -> p k n", p=128)
        dsrc = T["w_down"][l, f].rearrange("(j p) n -> p j n", p=128)

        def load_group(gi):
            j0, nj = groups[gi]
            s = gi % 2
            ncol = nj * 128
            self.ld(wgu[s][:, :, 0, 0:ncol], gsrc[:, :, j0 * 128:j0 * 128 + ncol], [Bwgu[s]], cast=True)
            self.ld(wgu[s][:, :, 1, 0:ncol], gsrc[:, :, DFF + j0 * 128:DFF + j0 * 128 + ncol], [Bwgu[s]], cast=True)
            self.ld(wdn[s][:, 0:nj, :], dsrc[:, j0:j0 + nj, :], [Bwdn[s]], cast=True)

        load_group(0)
        load_group(1)
        self.modulate(n, xm, Bxm, rlist)
        hid = A.bf16(GS * NT).rearrange("p (j t) -> p j t", j=GS)
        Bhid = [[Buf() for _ in RANGES] for _ in range(GS)]
        q = 0
        pbk = 0
        for gi, (j0, nj) in enumerate(groups):
            s = gi % 2
            ncol = nj * 128
            if gi >= 2:
                load_group(gi)
            for jj in range(nj):
                for ri in rlist:
                    c0, c1 = RANGES[ri]
                    w = c1 - c0
                    bg = pbk % 8
                    bu = (pbk + 1) % 8
                    pbk += 2
                    pg = self.bank(bg, w)
                    pu = self.bank(bu, w)
                    for k in range(KC):
                        self.mm(pg, wgu[s][:, k, 0, jj * 128:(jj + 1) * 128], xm[:, k, c0:c1], k == 0, k == KC - 1,
                                R=[Bwgu[s], Bxm[ri]], W=[self.BPS[bg]])
                    for k in range(KC):
                        self.mm(pu, wgu[s][:, k, 1, jj * 128:(jj + 1) * 128], xm[:, k, c0:c1], k == 0, k == KC - 1,
                                R=[Bwgu[s], Bxm[ri]], W=[self.BPS[bu]])
                    u = q % 2
                    q += 1
                    self.act(stmp[u][:, 0:w], pg, AF.Silu, R=[self.BPS[bg]], W=[Bstmp[u]])
                    self.tt("dve", hid[:, jj, c0:c1], stmp[u][:, 0:w], pu, ALU.mult,
                            R=[Bstmp[u], self.BPS[bu]], W=[Bhid[jj][ri]])
            for oc in range(KC):
                for ri in rlist:
                    c0, c1 = RANGES[ri]
                    w = c1 - c0
                    sidx = 1 if ri == 0 else 0
                    b = pbk % 8
                    pbk += 1
                    po = self.bank(b, w)
                    for jj in range(nj):
                        self.mm(po, wdn[s][:, jj, oc * 128:(oc + 1) * 128], hid[:, jj, c0:c1], jj == 0, jj == nj - 1,
                                R=[Bwdn[s], Bhid[jj][ri]], W=[self.BPS[b]])
                    self.stt("dve", self.hT[:, oc, c0:c1], po, self.Gmod[:, n, oc, sidx:sidx + 1], self.hT[:, oc, c0:c1],
                             ALU.mult, ALU.add, R=[self.BPS[b], self.Bmod, self.Bh[ri]], W=[self.Bh[ri]])
        self.C.flush()
        A.release(m0)

    def mixer(self, l):
        A = self.A
        m0 = A.mark()
        self.need_ctx = l < self.n_layers - 1
        xm = A.high_f32(KC * NT // 2).bitcast(BF16).rearrange("p (c t) -> p c t", c=KC)
        Bxm = [Buf() for _ in RANGES]
        self.xm2 = xm
        self.Bxm2 = Bxm
        self.modulate(1, xm, Bxm, list(range(len(RANGES))))
        self.mix_na(l)
        if self.stop == f"na_{l}":
            self.dump_h()
            return False
        self.mix_s5(l)
        if self.stop in ("s5a", "s5b", "s5c"):
            return False
        if self.stop == f"s5_{l}":
            self.dump_h()
            return False
        self.mix_mla(l, m0)
        A.release(m0)
        return True

    def load_win(self, l, col0, ncols, dst, Bdst):
        src = self.T["w_in"][l].rearrange("(k p) n -> p k n", p=128)
        self.ld(dst[:, :, 0:ncols], src[:, :, col0:col0 + ncols], [Bdst], cast=True)

    def outproj_tiles(self, l, a_tok, Ba, tiles, krow0, nkc, wout, Bwout):
        A = self.A
        m = A.mark()
        oT = [A.bf16(nkc * 128).rearrange("p (k t) -> p k t", k=nkc) for _ in range(2)]
        BoT = [Buf(), Buf()]
        pbk = 0
        for qi, t in enumerate(tiles):
            u = qi % 2
            ri = self.ranges_of(t * 128, t * 128 + 128)
            sidx = 1 if t < 2 else 0
            bt = 6 + (qi % 2)
            ptr = self.PS[bt][:].bitcast(BF16)
            for k in range(nkc):
                self.tr(ptr[:, k * 128:(k + 1) * 128], a_tok[:, t, k * 128:(k + 1) * 128], self.identb[:],
                        R=[Ba, self.Bidb], W=[self.BPS[bt]])
            self.cp("act", oT[u][:], ptr[:, 0:nkc * 128].rearrange("p (k t) -> p k t", k=nkc),
                    R=[self.BPS[bt]], W=[BoT[u]])
            for half in range(2):
                b = pbk % 6
                pbk += 1
                pb = self.bank(b).rearrange("p (a q) -> p a q", a=4)
                for a in range(4):
                    oc = half * 4 + a
                    for k in range(nkc):
                        self.mm(pb[:, a, :], wout[:, k, oc * 128:(oc + 1) * 128], oT[u][:, k, :], k == 0, k == nkc - 1,
                                R=[Bwout, BoT[u]], W=[self.BPS[b]])
                for a in range(4):
                    oc = half * 4 + a
                    dst = self.hT[:, oc, t * 128:(t + 1) * 128]
                    self.stt("dve", dst, pb[:, a, :], self.Gmod[:, 1, oc, sidx:sidx + 1], dst, ALU.mult, ALU.add,
                             R=[self.BPS[b], self.Bmod] + [self.Bh[r] for r in ri], W=[self.Bh[r] for r in ri])
        self.C.flush()
        A.release(m)

    def load_wout(self, l, krow0, nkc):
        wout = self.A.bf16(nkc * D).rearrange("p (k n) -> p k n", k=nkc)
        Bw = Buf()
        src = self.T["w_out"][l].rearrange("(k p) n -> p k n", p=128)
        self.ld(wout[:, :, :], src[:, krow0:krow0 + nkc, :], [Bw], cast=True)
        return wout, Bw

    def headnorm_fm(self, zps, Bz, w, nparts, onesmat, inv_d, gain_ap, out_ap, Bout, tmps):
        sq, Bsq, rs, Brs, bss = tmps
        self.act(sq[0:nparts, 0:w], zps, AF.Square, R=[Bz], W=[Bsq])
        pss = self.bank(bss, w)[0:nparts, :]
        self.mm(pss, onesmat, sq[0:nparts, 0:w], True, True, R=[Bsq, self.Bcf], W=[self.BPS[bss]])
        self.act(rs[0:nparts, 0:w], pss, AF.Ln, R=[self.BPS[bss], self.Bcf], W=[Brs],
                 bias=self.misc[0:nparts, 0:1], scale=inv_d)
        self.act(rs[0:nparts, 0:w], rs[0:nparts, 0:w], AF.Exp, R=[Brs], W=[Brs], scale=-0.5)
        self.stt("dve", out_ap, zps, gain_ap, rs[0:nparts, 0:w], ALU.mult, ALU.mult,
                 R=[Bz, Brs, self.Bsmall, self.Bvp], W=[Bout])

    def mix_na(self, l):
        T = self.T
        A = self.A
        m = A.mark()
        xm, Bxm = self.xm2, self.Bxm2
        qn = A.bf16(3 * NT).rearrange("p (c t) -> p c t", c=3)
        kn = A.bf16(3 * NT).rearrange("p (c t) -> p c t", c=3)
        Bqn, Bkn = Buf(), Buf()
        vna = A.bf16(18 * 6 * 65).rearrange("p (t h d) -> p t h d", t=18, h=6)
        Bv = Buf()
        a_tok = A.bf16(18 * 384).rearrange("p (t d) -> p t d", t=18)
        Ba = Buf()
        m1 = A.mark()
        wbuf = [A.bf16(KC * 384).rearrange("p (k n) -> p k n", k=KC) for _ in range(2)]
        Bw = [Buf(), Buf()]
        NSL = 3
        sq = [A.f32(384) for _ in range(NSL)]
        rs = [A.f32(384) for _ in range(NSL)]
        Bsq = [Buf() for _ in range(NSL)]
        Brs = [Buf() for _ in range(NSL)]
        self.memset("pool", vna[:, :, :, 64:65], 1.0, [Bv])
        self.load_win(l, 0, 384, wbuf[0], Bw[0])
        self.load_win(l, 384, 384, wbuf[1], Bw[1])

        def na_chain(part, cch, ri, u):
            dstT, Bd = (qn, Bqn) if part == 0 else (kn, Bkn)
            gain = self.small[:, 0:1] if part == 0 else self.vp[:, VP_NAK:VP_NAK + 1]
            c0, c1 = RANGES[ri]
            w = c1 - c0
            bz = 2 * u
            bs_ = 2 * u + 1
            pz = self.bank(bz, w)
            for k in range(KC):
                self.mm(pz, wbuf[part][:, k, cch * 128:(cch + 1) * 128], xm[:, k, c0:c1], k == 0, k == KC - 1,
                        R=[Bw[part], Bxm[ri]], W=[self.BPS[bz]])
            yield
            self.act(sq[u][:, 0:w], pz, AF.Square, R=[self.BPS[bz]], W=[Bsq[u]])
            pss = self.bank(bs_, w)
            self.mm(pss, self.bones, sq[u][:, 0:w], True, True, R=[Bsq[u], self.Bcf], W=[self.BPS[bs_]])
            yield
            self.act(rs[u][:, 0:w], pss, AF.Ln, R=[self.BPS[bs_], self.Bcf], W=[Brs[u]],
                     bias=self.misc[:, 0:1], scale=1.0 / 64)
            yield
            self.act(rs[u][:, 0:w], rs[u][:, 0:w], AF.Exp, R=[Brs[u]], W=[Brs[u]], scale=-0.5)
            yield
            self.stt("dve", dstT[:, cch, c0:c1], pz, gain, rs[u][:, 0:w], ALU.mult, ALU.mult,
                     R=[self.BPS[bz], Brs[u], self.Bsmall, self.Bvp], W=[Bd])
            yield

        lst = [(part, cch, ri) for part in range(2) for cch in range(3) for ri in range(len(RANGES))]

        def na_stream(u):
            for _ in range(2 * u):
                yield
            for (part, cch, ri) in lst[u::NSL]:
                yield from na_chain(part, cch, ri, u)

        live = [na_stream(u) for u in range(NSL)]
        while live:
            for g in list(live):
                try:
                    next(g)
                except StopIteration:
                    live.remove(g)
        self.load_win(l, 768, 384, wbuf[0], Bw[0])
        for t in range(18):
            b = t % 4
            pv = self.bank(b, 384)
            ri = self.ranges_of(t * 128, t * 128 + 128)
            for k in range(KC):
                self.mm(pv, xm[:, k, t * 128:(t + 1) * 128], wbuf[0][:, k, :], k == 0, k == KC - 1,
                        R=[Bw[0]] + [Bxm[r] for r in ri], W=[self.BPS[b]])
            self.cp("act" if t % 2 else "dve", vna[:, t, :, 0:64], pv.rearrange("p (h d) -> p h d", h=6),
                    R=[self.BPS[b]], W=[Bv])
        self.C.flush()
        A.release(m1)
        bias = [A.bf16(21 * 128).rearrange("p (t q) -> p t q", t=21) for _ in range(2)]
        Bb = [Buf(), Buf()]
        pT = [A.bf16(512).rearrange("p (t q) -> p t q", t=4) for _ in range(3)]
        BpT = [Buf() for _ in range(3)]
        rc = A.f32(8)
        Brc = Buf()
        cls = {}
        ci = 0
        for dm in range(-2, 3):
            cls[("int", dm)] = ci
            ci += 1
        for i in (0, 1, 14, 15):
            mlo = {0: 0, 1: 0, 14: 12, 15: 12}[i]
            for mm_ in range(mlo, mlo + 4):
                cls[(i, mm_)] = ci
                ci += 1
        units = []
        for h in range(6):
            qtiles = list(range(2, 18)) + ([0, 1] if self.need_ctx else [])
            for ti, t in enumerate(qtiles):
                if t >= 2:
                    i = t - 2
                    if 2 <= i <= 13:
                        kts = [(0, None), (1, None)] + [(2 + i + dm, cls[("int", dm)]) for dm in range(-2, 3)]
                    else:
                        mlo = {0: 0, 1: 0, 14: 12, 15: 12}[i]
                        kts = [(0, None), (1, None)] + [(2 + mm_, cls[(i, mm_)]) for mm_ in range(mlo, mlo + 4)]
                else:
                    kts = [(0, None), (1, None)]
                nk = len(kts)
                for g0 in range(0, nk, 4):
                    units.append(dict(h=h, t=t, grp=kts[g0:g0 + 4], g0=g0, nk=nk, pair=len(units) if g0 == 0 else None,
                                      newhead=(ti == 0 and g0 == 0)))
        npair = 0
        for un in units:
            if un["g0"] == 0:
                npair += 1
            un["bo"] = 4 + (npair % 2)

        def emit_S(i):
            un = units[i]
            h, t = un["h"], un["t"]
            hb = h % 2
            if un["newhead"]:
                self.ld(bias[hb][:], T["nab"][l][:, h * 21:(h + 1) * 21, :], [Bb[hb]], cast=True)
            cch = h // 2
            p0 = (h % 2) * 64
            qsl = qn[p0:p0 + 64, cch, t * 128:(t + 1) * 128]
            b = i % 4
            pS = self.bank(b).rearrange("p (a q) -> p a q", a=4)
            for gi, (kt, bc) in enumerate(un["grp"]):
                self.mm(pS[:, gi, :], kn[p0:p0 + 64, cch, kt * 128:(kt + 1) * 128], qsl, True, bc is None,
                        R=[Bkn, Bqn], W=[self.BPS[b]])
                if bc is not None:
                    self.mm(pS[:, gi, :], self.identb[:], bias[hb][:, bc, :], False, True,
                            R=[self.Bidb, Bb[hb]], W=[self.BPS[b]])

        emit_S(0)
        for i, un in enumerate(units):
            h, t = un["h"], un["t"]
            b = i % 4
            u = i % 3
            bo = un["bo"]
            po = self.bank(bo, 65)
            pS = self.bank(b).rearrange("p (a q) -> p a q", a=4)
            ng = len(un["grp"])
            self.act(pT[u][:, 0:ng, :], pS[:, 0:ng, :], AF.Exp, R=[self.BPS[b]], W=[BpT[u]])
            if i + 1 < len(units):
                emit_S(i + 1)
            for gi, (kt, bc) in enumerate(un["grp"]):
                done = un["g0"] + gi
                self.mm(po, pT[u][:, gi, :], vna[:, kt, h, :], done == 0, done == un["nk"] - 1,
                        R=[BpT[u], Bv], W=[self.BPS[bo]])
            if un["g0"] + ng == un["nk"]:
                self.recip(rc[:, 0:1], po[:, 64:65], R=[self.BPS[bo]], W=[Brc])
                self.ts("dve", a_tok[:, t, h * 64:(h + 1) * 64], po[:, 0:64], rc[:, 0:1], None, ALU.mult, None,
                        R=[self.BPS[bo], Brc], W=[Ba])
        self.C.flush()
        wout, Bwo = self.load_wout(l, 0, 3)
        tiles = list(range(2, 18)) + ([0, 1] if self.need_ctx else [])
        self.outproj_tiles(l, a_tok, Ba, tiles, 0, 3, wout, Bwo)
        A.release(m)

    def mix_s5(self, l):
        T = self.T
        A = self.A
        m = A.mark()
        xm, Bxm = self.xm2, self.Bxm2
        ysb = A.f32(2 * NT).rearrange("p (c t) -> p c t", c=2)
        By = [[Buf() for _ in RANGES] for _ in range(2)]
        ubf = A.bf16(2 * NT).rearrange("p (c t) -> p c t", c=2)
        Bu = [Buf() for _ in RANGES]
        m1 = A.mark()
        wbuf = A.bf16(KC * 256).rearrange("p (k n) -> p k n", k=KC)
        Bw = Buf()
        self.load_win(l, 1824, 256, wbuf, Bw)
        pbk = 0
        for ch in range(2):
            for ri in range(len(RANGES)):
                c0, c1 = RANGES[ri]
                w = c1 - c0
                b = pbk % 4
                pbk += 1
                pz = self.bank(b, w)
                for k in range(KC):
                    self.mm(pz, wbuf[:, k, ch * 128:(ch + 1) * 128], xm[:, k, c0:c1], k == 0, k == KC - 1,
                            R=[Bw, Bxm[ri]], W=[self.BPS[b]])
                self.ts("dve", ysb[:, ch, c0:c1], pz, self.vp[:, VP_S5D + ch:VP_S5D + ch + 1], None, ALU.mult, None,
                        R=[self.BPS[b], self.Bvp], W=[By[ch][ri]])
                self.cp("dve", ubf[:, ch, c0:c1], pz, R=[self.BPS[b]], W=[Bu[ri]])
        self.C.flush()
        A.release(m1)
        if self.stop == "s5a":
            self.dump_h()
            return
        prm = A.f32(48).rearrange("p (a k) -> p a k", a=3)
        Bp = Buf()
        self.ld(prm, T["s5p"][l], [Bp])
        NS = 24
        sc = A.f32(NS * 16).rearrange("p (a k) -> p a k", a=NS)
        Bs = Buf()
        (DT, RHO, TH, SS, UT, FR, SN, CS, NR, NI, DEN, FRE, FIM, NFI, T1, T2, AFR, CRE, CIM, OFF, KI) = range(21)
        lr, li, ldt = prm[:, 0, :], prm[:, 1, :], prm[:, 2, :]
        sgn = self.misc[:, 16:32]
        hp = self.misc[:, 1:2]
        R_ = [Bp, Bs, self.Bcf]
        W_ = [Bs]
        self.act(sc[:, DT, :], ldt, AF.Exp, R=R_, W=W_)
        self.tt("dve", sc[:, T1, :], lr, sc[:, DT, :], ALU.mult, R=R_, W=W_)
        self.act(sc[:, RHO, :], sc[:, T1, :], AF.Exp, R=R_, W=W_)
        self.tt("dve", sc[:, TH, :], li, sc[:, DT, :], ALU.mult, R=R_, W=W_)
        self.ts("dve", sc[:, UT, :], sc[:, TH, :], 1.0 / TWO_PI, None, ALU.mult, None, R=R_, W=W_)
        self.tt("dve", sc[:, SS, :], sc[:, UT, :], sgn, ALU.mult, R=R_, W=W_)
        ki = A.i32(16)
        self.cp("dve", ki, sc[:, UT, :], R=R_, W=W_)
        self.tt("dve", sc[:, FR, :], sc[:, UT, :], ki, ALU.subtract, R=R_, W=W_)
        self.act(sc[:, SN, :], sc[:, FR, :], AF.Sin, R=R_, W=W_, scale=TWO_PI)
        self.act(sc[:, AFR, :], sc[:, FR, :], AF.Abs, R=R_, W=W_)
        self.act(sc[:, CS, :], sc[:, AFR, :], AF.Sin, R=R_, W=W_, scale=-TWO_PI, bias=hp)
        self.tt("dve", sc[:, NR, :], sc[:, RHO, :], sc[:, CS, :], ALU.mult, R=R_, W=W_)
        self.ts("dve", sc[:, NR, :], sc[:, NR, :], -1.0, None, ALU.add, None, R=R_, W=W_)
        self.tt("dve", sc[:, NI, :], sc[:, RHO, :], sc[:, SN, :], ALU.mult, R=R_, W=W_)
        self.tt("dve", sc[:, DEN, :], lr, lr, ALU.mult, R=R_, W=W_)
        self.tt("dve", sc[:, T1, :], li, li, ALU.mult, R=R_, W=W_)
        self.tt("dve", sc[:, DEN, :], sc[:, DEN, :], sc[:, T1, :], ALU.add, R=R_, W=W_)
        self.recip(sc[:, DEN, :], sc[:, DEN, :], R=R_, W=W_)
        self.tt("dve", sc[:, T1, :], sc[:, NR, :], lr, ALU.mult, R=R_, W=W_)
        self.tt("dve", sc[:, T2, :], sc[:, NI, :], li, ALU.mult, R=R_, W=W_)
        self.tt("dve", sc[:, T1, :], sc[:, T1, :], sc[:, T2, :], ALU.add, R=R_, W=W_)
        self.tt("dve", sc[:, FRE, :], sc[:, T1, :], sc[:, DEN, :], ALU.mult, R=R_, W=W_)
        self.tt("dve", sc[:, T1, :], sc[:, NI, :], lr, ALU.mult, R=R_, W=W_)
        self.tt("dve", sc[:, T2, :], sc[:, NR, :], li, ALU.mult, R=R_, W=W_)
        self.tt("dve", sc[:, T1, :], sc[:, T1, :], sc[:, T2, :], ALU.subtract, R=R_, W=W_)
        self.tt("dve", sc[:, FIM, :], sc[:, T1, :], sc[:, DEN, :], ALU.mult, R=R_, W=W_)
        self.ts("dve", sc[:, NFI, :], sc[:, FIM, :], -1.0, None, ALU.mult, None, R=R_, W=W_)
        (ECN, ESN) = (21, 22)
        self.ts("dve", sc[:, T1, :], sc[:, SS, :], float(NT), 16.0, ALU.mult, ALU.add, R=R_, W=W_)
        ki2 = A.i32(16)
        self.cp("dve", ki2, sc[:, T1, :], R=R_, W=W_)
        self.tt("dve", sc[:, T1, :], sc[:, T1, :], ki2, ALU.subtract, R=R_, W=W_)
        self.act(sc[:, ESN, :], sc[:, T1, :], AF.Sin, R=R_, W=W_, scale=TWO_PI)
        self.act(sc[:, T2, :], sc[:, T1, :], AF.Abs, R=R_, W=W_)
        self.act(sc[:, ECN, :], sc[:, T2, :], AF.Sin, R=R_, W=W_, scale=-TWO_PI, bias=hp)
        if self.stop == "s5c":
            self.dump_ap(sc[:].rearrange("p a k -> p (a k)"), NS * 16, [Bs])
            return
        Bm = A.bf16(16 * 2 * 128).rearrange("p (k c n) -> p k c n", k=16, c=2)
        BBm = Buf()
        for k4 in range(4):
            self.ld(Bm[:, 4 * k4:4 * k4 + 4], T["s5B"][l][:, 4 * k4:4 * k4 + 4], [BBm], cast=True)
        Lm = A.bf16(16 * 3 * 64).rearrange("p (k c n) -> p k c n", k=16, c=3)
        BLm = Buf()
        mC = A.mark()
        Cf = A.f32(16 * 2 * 64).rearrange("p (k c n) -> p k c n", k=16, c=2)
        BCf = Buf()
        self.ld(Cf, T["s5C"][l], [BCf])
        ct = A.f32(128).rearrange("p (a n) -> p a n", a=2)
        Bct = Buf()
        for kk in range(16):
            fr = sc[:, FRE, kk:kk + 1]
            fi = sc[:, FIM, kk:kk + 1]
            nfi = sc[:, NFI, kk:kk + 1]
            RR = [BCf, Bs, Bct]
            self.ts("dve", ct[:, 0, :], Cf[:, kk, 0, :], fr, None, ALU.mult, None, R=RR, W=[Bct])
            self.stt("dve", ct[:, 0, :], Cf[:, kk, 1, :], nfi, ct[:, 0, :], ALU.mult, ALU.add, R=RR, W=[Bct])
            self.ts("dve", ct[:, 1, :], Cf[:, kk, 0, :], fi, None, ALU.mult, None, R=RR, W=[Bct])
            self.stt("dve", ct[:, 1, :], Cf[:, kk, 1, :], fr, ct[:, 1, :], ALU.mult, ALU.add, R=RR, W=[Bct])
            self.cp("dve", Lm[:, kk, 0, :], ct[:, 0, :], R=[Bct], W=[BLm])
            self.ts("dve", Lm[:, kk, 1, :], ct[:, 0, :], -1.0, None, ALU.mult, None, R=[Bct], W=[BLm])
            self.ts("dve", Lm[:, kk, 2, :], ct[:, 1, :], -1.0, None, ALU.mult, None, R=[Bct], W=[BLm])
        self.C.flush()
        A.release(mC)
        if self.stop == "s5b":
            self.dump_ap(sc[:].rearrange("p a k -> p (a k)"), NS * 16, [Bs])
            return
        car = A.f32(64).rearrange("p (a k) -> p a k", a=4)
        Bcar = Buf()
        NB = 2
        def blk():
            return [A.f32(384) for _ in range(NB)], [Buf() for _ in range(NB)]
        UA, BUA = blk()
        KI_, BKI = [A.i32(384) for _ in range(NB)], [Buf() for _ in range(NB)]
        ES, BES = blk()
        EC, BEC = blk()
        t1, Bt1 = blk(); t2, Bt2 = blk()
        VR, BVR = blk(); VI, BVI = blk()
        GR, BGR = blk(); GI, BGI = blk()
        AF_, BAF = VR, BVR
        t3, Bt3 = GR, BGR
        t4, Bt4 = GI, BGI
        Pb = [[A.bf16(384) for _ in range(4)] for _ in range(NB)]
        BPb = [[Buf() for _ in range(4)] for _ in range(NB)]
        offt = A.f32(4)
        Boff = Buf()
        border = [0, 6, 5, 4, 3, 2, 1]
        Bck = [Buf() for _ in range(16)]
        Bofu = [Buf(), Buf()]

        def junction():
            cr, cim = car[:, 0, 8:16], car[:, 1, 8:16]
            ecn, esn = sc[:, ECN, 8:16], sc[:, ESN, 8:16]
            tr_, ti_ = car[:, 2, 8:16], car[:, 3, 8:16]
            RR = Bck[8:16] + [Bs, Bcar]
            WW = Bck[8:16] + [Bcar]
            self.tt("dve", tr_, cr, ecn, ALU.mult, R=RR, W=WW)
            self.tt("dve", ti_, cim, esn, ALU.mult, R=RR, W=WW)
            self.tt("dve", tr_, tr_, ti_, ALU.add, R=RR, W=WW)
            self.tt("dve", ti_, cim, ecn, ALU.mult, R=RR, W=WW)
            self.tt("dve", cim, cr, esn, ALU.mult, R=RR, W=WW)
            self.tt("dve", cim, ti_, cim, ALU.subtract, R=RR, W=WW)
            self.cp("dve", cr, tr_, R=RR, W=WW)

        def iteration(step, d, ch, g4, u):
            ri = step if d == 0 else border[step]
            c0, c1 = RANGES[ri]
            w = c1 - c0
            by = 6 + ch
            gp = ch * 4 + g4
            kk = d * 8 + gp
            sap = sc[:, SS, kk:kk + 1]
            o = offt[:, u:u + 1]
            self.ts("dve", o, sap, float(c0), 16.0, ALU.mult, ALU.add, R=[Bs], W=[Bofu[u]])
            self.ts("dve", UA[u][:, 0:w], self.iota0[:, 0:w], sap, o, ALU.mult, ALU.add,
                    R=[self.Biota, Bs, Bofu[u]], W=[BUA[u]])
            yield
            kf = KI_[u].bitcast(F32)
            self.ts("dve", kf[:, 0:w], UA[u][:, 0:w], 12582912.0, 12582912.0, ALU.add, ALU.subtract, R=[BUA[u]], W=[BKI[u]])
            self.tt("dve", UA[u][:, 0:w], UA[u][:, 0:w], kf[:, 0:w], ALU.subtract,
                    R=[BUA[u], BKI[u]], W=[BUA[u]])
            yield
            self.act(ES[u][:, 0:w], UA[u][:, 0:w], AF.Sin, R=[BUA[u]], W=[BES[u]], scale=TWO_PI)
            self.act(AF_[u][:, 0:w], UA[u][:, 0:w], AF.Abs, R=[BUA[u]], W=[BAF[u]])
            self.act(EC[u][:, 0:w], AF_[u][:, 0:w], AF.Sin, R=[BAF[u], self.Bcf], W=[BEC[u]],
                     scale=-TWO_PI, bias=hp)
            b0 = 2 * u
            b1 = 2 * u + 1
            pre = self.bank(b0, w)
            pim = self.bank(b1, w)
            self.mm(pre, Bm[:, kk, 0, :], ubf[:, ch, c0:c1], True, True, R=[BBm, Bu[ri]], W=[self.BPS[b0]])
            self.mm(pim, Bm[:, kk, 1, :], ubf[:, ch, c0:c1], True, True, R=[BBm, Bu[ri]], W=[self.BPS[b1]])
            yield
            self.tt("dve", t1[u][:, 0:w], pre, EC[u][:, 0:w], ALU.mult, R=[self.BPS[b0], BEC[u]], W=[Bt1[u]])
            self.tt("dve", t2[u][:, 0:w], pim, ES[u][:, 0:w], ALU.mult, R=[self.BPS[b1], BES[u]], W=[Bt2[u]])
            yield
            self.tt("dve", t3[u][:, 0:w], pim, EC[u][:, 0:w], ALU.mult, R=[self.BPS[b1], BEC[u]], W=[Bt3[u]])
            self.tt("dve", t4[u][:, 0:w], pre, ES[u][:, 0:w], ALU.mult, R=[self.BPS[b0], BES[u]], W=[Bt4[u]])
            self.tt("pool", VR[u][:, 0:w], t1[u][:, 0:w], t2[u][:, 0:w], ALU.add, R=[Bt1[u], Bt2[u]], W=[BVR[u]])
            yield
            self.tt("pool", VI[u][:, 0:w], t3[u][:, 0:w], t4[u][:, 0:w], ALU.subtract, R=[Bt3[u], Bt4[u]], W=[BVI[u]])
            yield
            rho_bc = sc[:, RHO, kk:kk + 1].to_broadcast([128, w])
            first = (step == 0)
            for (Vb, BV, Gb, BG, cidx) in ((VR[u], BVR[u], GR[u], BGR[u], 0), (VI[u], BVI[u], GI[u], BGI[u], 1)):
                init = 0.0 if first else car[:, cidx, kk:kk + 1]
                if d == 0:
                    vin, gout = Vb[:, 0:w], Gb[:, 0:w]
                    last = Gb[:, w - 1:w]
                else:
                    v0 = Vb[:, 0:w]
                    g0_ = Gb[:, 0:w]
                    vin = AP(v0.tensor, v0.offset + w - 1, [list(v0.ap[0]), [-1, w]])
                    gout = AP(g0_.tensor, g0_.offset + w - 1, [list(g0_.ap[0]), [-1, w]])
                    last = Gb[:, 0:1]
                self.scan(gout, rho_bc, vin, init, R=[BV, Bs, Bck[kk]], W=[BG])
                self.cp("pool", car[:, cidx, kk:kk + 1], last, R=[BG], W=[Bck[kk]])
                yield
            self.tt("dve", Pb[u][0][:, 0:w], GR[u][:, 0:w], EC[u][:, 0:w], ALU.mult, R=[BGR[u], BEC[u]], W=[BPb[u][0]])
            self.tt("pool", Pb[u][1][:, 0:w], GI[u][:, 0:w], ES[u][:, 0:w], ALU.mult, R=[BGI[u], BES[u]], W=[BPb[u][1]])
            yield
            self.tt("pool", Pb[u][2][:, 0:w], GR[u][:, 0:w], ES[u][:, 0:w], ALU.mult, R=[BGR[u], BES[u]], W=[BPb[u][2]])
            self.tt("pool", Pb[u][3][:, 0:w], GI[u][:, 0:w], EC[u][:, 0:w], ALU.mult, R=[BGI[u], BEC[u]], W=[BPb[u][3]])
            yield
            py = self.bank(by, w)[64 * (g4 // 2):64 * (g4 // 2) + 64, :]
            for pi, li_ in ((0, 0), (1, 1), (2, 2), (3, 2)):
                self.mm(py, Lm[:, kk, li_, :], Pb[u][pi][:, 0:w], pi == 0 and g4 % 2 == 0, pi == 3 and g4 % 2 == 1,
                        R=[BLm, BPb[u][pi]], W=[self.BPS[by]])
            if g4 == 3:
                self.tt("dve", ysb[:, ch, c0:c1], ysb[:, ch, c0:c1], self.bank(by, w), ALU.add,
                        R=[self.BPS[by], By[ch][ri]], W=[By[ch][ri]])
            yield

        def stream(u):
            if u == 1:
                for _ in range(5):
                    yield
            for step in range(7):
                for d in range(2):
                    if u == 0 and d == 1 and step == 1:
                        junction()
                    for ch in range(2):
                        for g4 in (u, u + 2):
                            yield from iteration(step, d, ch, g4, u)

        live = [stream(0), stream(1)]
        while live:
            for g in list(live):
                try:
                    next(g)
                except StopIteration:
                    live.remove(g)
        self.C.flush()
        A.release(m)
        m = A.mark()
        ysb = A.f32(2 * NT).rearrange("p (c t) -> p c t", c=2)
        gl = A.bf16(2 * NT).rearrange("p (c t) -> p c t", c=2)
        Bgl = [Buf() for _ in RANGES]
        wglu = A.bf16(2 * 512).rearrange("p (k n) -> p k n", k=2)
        Bwg = Buf()
        src = T["w_glu"][l].rearrange("(k p) n -> p k n", p=128)
        self.ld(wglu[:, :, :], src, [Bwg], cast=True)
        wout, Bwo = self.load_wout(l, 6, 2)
        sT = [A.bf16(2 * 384).rearrange("p (c t) -> p c t", c=2) for _ in range(2)]
        BsT = [Buf(), Buf()]
        ta = [A.f32(384) for _ in range(2)]; Bta = [Buf(), Buf()]
        tb = [A.f32(384) for _ in range(2)]; Btb = [Buf(), Buf()]
        tc_ = [A.f32(384) for _ in range(2)]; Btc = [Buf(), Buf()]
        rl = list(range(len(RANGES))) if self.need_ctx else list(range(1, len(RANGES)))
        q = 0
        pbk = 0
        for ri in rl:
            c0, c1 = RANGES[ri]
            w = c1 - c0
            sidx = 1 if ri == 0 else 0
            for ch in range(2):
                u = q % 2
                q += 1
                yy = ysb[:, ch, c0:c1]
                self.tt("pool", ta[u][:, 0:w], yy, yy, ALU.mult, R=[By[ch][ri]], W=[Bta[u]])
                self.ts("pool", ta[u][:, 0:w], ta[u][:, 0:w], 0.044715, 1.0, ALU.mult, ALU.add, R=[Bta[u]], W=[Bta[u]])
                self.tt("pool", ta[u][:, 0:w], ta[u][:, 0:w], yy, ALU.mult, R=[Bta[u], By[ch][ri]], W=[Bta[u]])
                self.act(tb[u][:, 0:w], ta[u][:, 0:w], AF.Sigmoid, R=[Bta[u]], W=[Btb[u]], scale=1.5957691216057308)
                self.tt("dve", gl[:, ch, c0:c1], yy, tb[u][:, 0:w], ALU.mult, R=[By[ch][ri], Btb[u]], W=[Bgl[ri]])
            su = ri % 2
            for ch in range(2):
                ba = pbk % 6
                bg = (pbk + 1) % 6
                pbk += 2
                pa = self.bank(ba, w)
                pg = self.bank(bg, w)
                for k in range(2):
                    self.mm(pa, wglu[:, k, ch * 128:(ch + 1) * 128], gl[:, k, c0:c1], k == 0, k == 1,
                            R=[Bwg, Bgl[ri]], W=[self.BPS[ba]])
                for k in range(2):
                    self.mm(pg, wglu[:, k, 256 + ch * 128:256 + (ch + 1) * 128], gl[:, k, c0:c1], k == 0, k == 1,
                            R=[Bwg, Bgl[ri]], W=[self.BPS[bg]])
                u = q % 2
                q += 1
                self.act(tc_[u][:, 0:w], pa, AF.Identity, R=[self.BPS[ba], self.Bvp], W=[Btc[u]],
                         bias=self.vp[:, VP_BGLU + ch:VP_BGLU + ch + 1])
                self.act(tb[u][:, 0:w], pg, AF.Sigmoid, R=[self.BPS[bg], self.Bvp], W=[Btb[u]],
                         bias=self.vp[:, VP_BGLU + 2 + ch:VP_BGLU + 3 + ch])
                self.tt("dve", sT[su][:, ch, 0:w], tc_[u][:, 0:w], tb[u][:, 0:w], ALU.mult, R=[Btc[u], Btb[u]], W=[BsT[su]])
            for oc in range(KC):
                b = pbk % 6
                pbk += 1
                po = self.bank(b, w)
                for k in range(2):
                    self.mm(po, wout[:, k, oc * 128:(oc + 1) * 128], sT[su][:, k, 0:w], k == 0, k == 1,
                            R=[Bwo, BsT[su]], W=[self.BPS[b]])
                dst = self.hT[:, oc, c0:c1]
                self.stt("dve", dst, po, self.Gmod[:, 1, oc, sidx:sidx + 1], dst, ALU.mult, ALU.add,
                         R=[self.BPS[b], self.Bmod, self.Bh[ri]], W=[self.Bh[ri]])
        self.C.flush()
        A.release(m)

    def mix_mla(self, l, m_xm):
        T = self.T
        A = self.A
        xm, Bxm = self.xm2, self.Bxm2
        cqn = A.bf16(3 * NT).rearrange("p (c t) -> p c t", c=3)
        ckvn = A.bf16(2 * NT).rearrange("p (c t) -> p c t", c=2)
        krT = A.f32(NT)
        Bcq, Bckv, Bkr = Buf(), Buf(), Buf()
        m1 = A.mark()
        wbuf = A.bf16(KC * 384).rearrange("p (k n) -> p k n", k=KC)
        Bw = Buf()
        wbuf2 = A.bf16(KC * 288).rearrange("p (k n) -> p k n", k=KC)
        Bw2 = Buf()
        sq = [A.f32(384) for _ in range(2)]
        Bsq = [Buf(), Buf()]
        rs = A.f32(384)
        Brs = Buf()
        self.load_win(l, 1152, 384, wbuf, Bw)
        self.load_win(l, 1536, 288, wbuf2, Bw2)
        q = 0
        for (src_w, Bsw, nch, dstT, Bd, gcol) in ((wbuf, Bw, 3, cqn, Bcq, VP_CQG), (wbuf2, Bw2, 2, ckvn, Bckv, VP_CKVG)):
            for ri in range(len(RANGES)):
                c0, c1 = RANGES[ri]
                w = c1 - c0
                pss = self.bank(4 + (ri % 2), w)
                Bss = self.BPS[4 + (ri % 2)]
                for cch in range(nch):
                    pz = self.bank(cch, w)
                    for k in range(KC):
                        self.mm(pz, src_w[:, k, cch * 128:(cch + 1) * 128], xm[:, k, c0:c1], k == 0, k == KC - 1,
                                R=[Bsw, Bxm[ri]], W=[self.BPS[cch]])
                    u = q % 2
                    q += 1
                    self.act(sq[u][:, 0:w], pz, AF.Square, R=[self.BPS[cch]], W=[Bsq[u]])
                    self.mm(pss, self.ones, sq[u][:, 0:w], cch == 0, cch == nch - 1, R=[Bsq[u], self.Bcf], W=[Bss])
                self.act(rs[:, 0:w], pss, AF.Ln, R=[Bss, self.Bcf], W=[Brs], bias=self.misc[:, 0:1],
                         scale=1.0 / (nch * 128))
                self.act(rs[:, 0:w], rs[:, 0:w], AF.Exp, R=[Brs], W=[Brs], scale=-0.5)
                for cch in range(nch):
                    self.stt("dve", dstT[:, cch, c0:c1], self.bank(cch, w), self.vp[:, gcol + cch:gcol + cch + 1],
                             rs[:, 0:w], ALU.mult, ALU.mult, R=[self.BPS[cch], Brs, self.Bvp], W=[Bd])
        for ri in range(len(RANGES)):
            c0, c1 = RANGES[ri]
            w = c1 - c0
            b = 6 + (ri % 2)
            pz = self.bank(b, w)[64:96, :]
            for k in range(KC):
                self.mm(pz, wbuf2[:, k, 256:288], xm[:, k, c0:c1], k == 0, k == KC - 1,
                        R=[Bw2, Bxm[ri]], W=[self.BPS[b]])
            self.cp("act", krT[64:96, c0:c1], pz, R=[self.BPS[b]], W=[Bkr])
        self.C.flush()
        A.release(m1)
        A.high_release()
        wuq = A.bf16(3 * 576).rearrange("p (k n) -> p k n", k=3)
        wukv = A.bf16(2 * 768).rearrange("p (k n) -> p k n", k=2)
        Bwq, Bwkv = Buf(), Buf()
        self.ld(wuq[:, :, :], T["w_uq"][l].rearrange("(k p) n -> p k n", p=128), [Bwq], cast=True)
        self.ld(wukv[:, :, :], T["w_ukv"][l].rearrange("(k p) n -> p k n", p=128), [Bwkv], cast=True)
        rope = A.f32(2 * NT).rearrange("p (a t) -> p a t", a=2)
        Brope = Buf()
        self.ld(rope[0:96, :, :], T["rope"], [Brope])
        b_tok = A.bf16(18 * 384).rearrange("p (t d) -> p t d", t=18)
        Bbt = Buf()
        mT = A.mark()
        QT = [A.bf16(NT) for _ in range(2)]
        KT = [A.bf16(NT) for _ in range(2)]
        VT = [A.bf16(18 * 65).rearrange("p (t d) -> p t d", t=18) for _ in range(2)]
        BQ, BK, BV = [Buf(), Buf()], [Buf(), Buf()], [Buf(), Buf()]
        for hb in range(2):
            self.memset("pool", VT[hb][:, :, 64:65], 1.0, [BV[hb]])
        NSL = 2
        kcat = [A.f32(384) for _ in range(NSL)]; Bkc = [Buf() for _ in range(NSL)]
        sqh = [A.f32(384) for _ in range(NSL)]; Bsqh = [Buf() for _ in range(NSL)]
        rsh = [A.f32(384) for _ in range(NSL)]; Brsh = [Buf() for _ in range(NSL)]
        xn = [A.f32(384) for _ in range(NSL)]; Bxn = [Buf() for _ in range(NSL)]
        r1 = [A.f32(384) for _ in range(NSL)]; Br1 = [Buf() for _ in range(NSL)]
        r2 = [A.f32(384) for _ in range(NSL)]; Br2 = [Buf() for _ in range(NSL)]
        pT = [A.bf16(512) for _ in range(3)]
        BpT = [Buf() for _ in range(3)]
        rc = A.f32(8)
        Brc = Buf()
        ones96 = self.ones[0:96, 0:96]
        qranges = list(range(len(RANGES))) if self.need_ctx else list(range(1, len(RANGES)))
        gb = [0]

        def gbank():
            gb[0] += 1
            return 6 + (gb[0] % 2)

        def chain(h, which, ri, u):
            hb = h % 2
            dst, Bdst = (QT[hb], BQ[hb]) if which == 0 else (KT[hb], BK[hb])
            gain = self.small[0:96, 1:2] if which == 0 else self.vp[0:96, VP_MK:VP_MK + 1]
            c0, c1 = RANGES[ri]
            w = c1 - c0
            b = 6 + u
            if which == 0:
                pz = self.bank(b, w)[0:96, :]
                for k in range(3):
                    self.mm(pz, wuq[:, k, h * 96:(h + 1) * 96], cqn[:, k, c0:c1], k == 0, k == 2,
                            R=[Bwq, Bcq], W=[self.BPS[b]])
            else:
                pz = self.bank(b, w)[0:64, :]
                for k in range(2):
                    self.mm(pz, wukv[:, k, h * 128:h * 128 + 64], ckvn[:, k, c0:c1], k == 0, k == 1,
                            R=[Bwkv, Bckv], W=[self.BPS[b]])
            yield
            if which == 0:
                self.cp("dve", kcat[u][0:96, 0:w], pz, R=[self.BPS[b]], W=[Bkc[u]])
            else:
                self.cp("dve", kcat[u][0:64, 0:w], pz, R=[self.BPS[b]], W=[Bkc[u]])
                self.cp("pool", kcat[u][64:96, 0:w], krT[64:96, c0:c1], R=[Bkr], W=[Bkc[u]])
            zsrc, Bz = kcat[u][0:96, 0:w], Bkc[u]
            yield
            self.tt("pool", sqh[u][0:96, 0:w], zsrc, zsrc, ALU.mult, R=[Bz], W=[Bsqh[u]])
            bs_ = 6 + u
            pss = self.bank(bs_, w)[0:96, :]
            self.mm(pss, ones96, sqh[u][0:96, 0:w], True, True, R=[Bsqh[u], self.Bcf], W=[self.BPS[bs_]])
            yield
            self.act(rsh[u][0:96, 0:w], pss, AF.Ln, R=[self.BPS[bs_], self.Bcf], W=[Brsh[u]],
                     bias=self.misc[0:96, 0:1], scale=1.0 / 96)
            yield
            self.act(rsh[u][0:96, 0:w], rsh[u][0:96, 0:w], AF.Exp, R=[Brsh[u]], W=[Brsh[u]], scale=-0.5)
            yield
            self.stt("dve", xn[u][0:96, 0:w], zsrc, gain, rsh[u][0:96, 0:w], ALU.mult, ALU.mult,
                     R=[Bz, Brsh[u], self.Bsmall, self.Bvp], W=[Bxn[u]])
            bp = 6 + u
            pp = self.bank(bp, w)[0:96, :]
            self.mm(pp, self.perm, xn[u][0:96, 0:w], True, True, R=[Bxn[u], self.Bcf], W=[self.BPS[bp]])
            yield
            self.tt("pool", r1[u][0:96, 0:w], xn[u][0:96, 0:w], rope[0:96, 0, c0:c1], ALU.mult,
                    R=[Bxn[u], Brope], W=[Br1[u]])
            self.tt("dve", r2[u][0:96, 0:w], pp, rope[0:96, 1, c0:c1], ALU.mult,
                    R=[self.BPS[bp], Brope], W=[Br2[u]])
            yield
            self.tt("pool", dst[0:96, c0:c1], r1[u][0:96, 0:w], r2[u][0:96, 0:w], ALU.add,
                    R=[Br1[u], Br2[u]], W=[Bdst])
            yield

        def chains(h, lst, u):
            for (which, ri) in lst:
                yield from chain(h, which, ri, u)

        def vgen(h, u):
            hb = h % 2
            for t in range(9 * u, 9 * u + 9):
                b = 6 + u
                pv = self.bank(b, 64)
                for k in range(2):
                    self.mm(pv, ckvn[:, k, t * 128:(t + 1) * 128], wukv[:, k, h * 128 + 64:h * 128 + 128], k == 0, k == 1,
                            R=[Bckv, Bwkv], W=[self.BPS[b]])
                self.cp("dve", VT[hb][:, t, 0:64], pv, R=[self.BPS[b]], W=[BV[hb]])
                yield

        def merged(gens):
            live = list(gens)
            while live:
                for g in list(live):
                    try:
                        next(g)
                    except StopIteration:
                        live.remove(g)
                yield

        def gen(h):
            lst = [(0, ri) for ri in qranges] + [(1, ri) for ri in range(len(RANGES))]
            def stream(u):
                yield from chains(h, lst[u::2], u)
                yield from vgen(h, u)
            return merged([stream(0), stream(1)])

        def attn(h):
            hb = h % 2
            qblocks = [(256 + 512 * i, 256 + 512 * (i + 1), list(range(18))) for i in range(4)]
            if self.need_ctx:
                qblocks.append((0, 256, [0, 1]))
            seq = []
            for (qc0, qc1, kts) in qblocks:
                for ki_, kt in enumerate(kts):
                    seq.append((qc0, qc1, ki_, kt, len(kts)))

            def emit_S(i):
                qc0, qc1, ki_, kt, nk = seq[i]
                b = 4 + (i % 2)
                pS = self.bank(b, qc1 - qc0)
                self.mm(pS, KT[hb][0:96, kt * 128:(kt + 1) * 128], QT[hb][0:96, qc0:qc1], True, True,
                        R=[BK[hb], BQ[hb]], W=[self.BPS[b]])

            emit_S(0)
            for i, (qc0, qc1, ki_, kt, nk) in enumerate(seq):
                b = 4 + (i % 2)
                u = i % 3
                nq = (qc1 - qc0) // 128
                self.act(pT[u][:, 0:qc1 - qc0], self.bank(b, qc1 - qc0), AF.Exp, R=[self.BPS[b]], W=[BpT[u]])
                if i + 1 < len(seq):
                    emit_S(i + 1)
                for qs in range(nq):
                    self.mm(self.bank(qs, 65), pT[u][:, qs * 128:(qs + 1) * 128], VT[hb][:, kt, :],
                            ki_ == 0, ki_ == nk - 1, R=[BpT[u], BV[hb]], W=[self.BPS[qs]])
                if ki_ == nk - 1:
                    for qs in range(nq):
                        t = (qc0 // 128) + qs
                        po = self.bank(qs, 65)
                        self.recip(rc[:, qs:qs + 1], po[:, 64:65], R=[self.BPS[qs]], W=[Brc])
                        self.ts("dve", b_tok[:, t, h * 64:(h + 1) * 64], po[:, 0:64], rc[:, qs:qs + 1], None, ALU.mult, None,
                                R=[self.BPS[qs], Brc], W=[Bbt])
                yield

        for _ in gen(0):
            pass
        for h in range(6):
            streams = [attn(h)]
            if h + 1 < 6:
                streams.append(gen(h + 1))
            for _ in merged(streams):
                pass
        self.C.flush()
        A.release(mT)
        wout, Bwo = self.load_wout(l, 3, 3)
        tiles = list(range(2, 18)) + ([0, 1] if self.need_ctx else [])
        self.outproj_tiles(l, b_tok, Bbt, tiles, 3, 3, wout, Bwo)

    def final(self, out):
        A = self.A
        m = A.mark()
        stg = [A.f32(D) for _ in range(2)]
        Bst = [Buf(), Buf()]
        for t in range(2, 18):
            s = t % 2
            ri = self.ranges_of(t * 128, t * 128 + 128)
            for half in range(2):
                b = (2 * t + half) % 8
                pb = self.bank(b).rearrange("p (a q) -> p a q", a=4)
                for a in range(4):
                    c = half * 4 + a
                    self.tr(pb[:, a, :], self.hT[:, c, t * 128:(t + 1) * 128], self.ident,
                            R=[self.Bh[r] for r in ri] + [self.Bcf], W=[self.BPS[b]])
                self.cp("act" if half == 0 else "dve", stg[s][:, half * 512:(half + 1) * 512],
                        self.bank(b), R=[self.BPS[b]], W=[Bst[s]])
            self.st(out[(t - 2) * 128:(t - 1) * 128, :], stg[s], R=[Bst[s]])
        self.C.flush()
        A.release(m)


def _na_bias_index():
    W = 64
    rows = 32

    def row_start(r):
        return min(max(r - 4, 0), rows - 8)

    def col_start(c):
        return min(max(c - 8, 0), W - 16)

    tiles = []
    for dm in range(-2, 3):
        tiles.append((6, 6 + dm))
    for i in (0, 1, 14, 15):
        mlo = {0: 0, 1: 0, 14: 12, 15: 12}[i]
        for m in range(mlo, mlo + 4):
            tiles.append((i, m))
    idx = np.full((21, 128, 128), 465, dtype=np.int64)
    kk = np.arange(128)
    kr2, kc = kk // 64, kk % 64
    for ti, (i, m) in enumerate(tiles):
        for q in range(128):
            qr = 2 * i + q // 64
            qc = q % 64
            krow = 2 * m + kr2
            rs = row_start(qr)
            cs = col_start(qc)
            ok = (krow >= rs) & (krow < rs + 8) & (kc >= cs) & (kc < cs + 16)
            drow = krow - qr + 7
            dcol = np.clip(kc - qc, -15, 15) + 15
            val = drow * 31 + dcol
            idx[ti, :, q] = np.where(ok, val, 465)
    return idx


_CONST = {}


def _consts():
    if _CONST:
        return _CONST
    cf = np.zeros((128, 5, 128), np.float32)
    cf[:, 0, :] = np.eye(128, dtype=np.float32)
    cf[:, 1, :] = 1.0
    cf[0:64, 2, 0:64] = 1.0
    cf[64:128, 2, 64:128] = 1.0
    P = np.zeros((128, 128), np.float32)
    for base in (64, 80):
        for j in range(8):
            P[base + 8 + j, base + j] = -1.0
            P[base + j, base + 8 + j] = 1.0
    cf[:, 3, :] = P
    cf[:, 4, 0] = EPS
    cf[:, 4, 1] = math.pi / 2
    cf[:, 4, 16:24] = 1.0
    cf[:, 4, 24:32] = -1.0
    rope = np.zeros((96, 2, NT), np.float32)
    rope[:, 0, :] = 1.0
    t = np.arange(NX)
    row = (t // 64).astype(np.float32)
    col = (t % 64).astype(np.float32)
    inv = (10000.0 ** (-np.arange(8, dtype=np.float32) / 8)).astype(np.float32)
    for base, pos in ((64, row), (80, col)):
        ang = pos[None, :] * inv[:, None]
        c = np.cos(ang).astype(np.float32)
        s = np.sin(ang).astype(np.float32)
        rope[base:base + 8, 0, LCTX:] = c
        rope[base + 8:base + 16, 0, LCTX:] = c
        rope[base:base + 8, 1, LCTX:] = s
        rope[base + 8:base + 16, 1, LCTX:] = s
    iota = np.tile(np.arange(384, dtype=np.float32)[None, :], (128, 1))
    _CONST.update(cf=cf, rope=rope, iota=iota, nidx=_na_bias_index())
    return _CONST


def _prep_shared(inp):
    c = _consts()
    NL = 2
    sh = {}
    vp = np.zeros((NL, 128, 128), np.float32)
    for l in range(NL):
        vp[l, 0:72] = inp["b_mod"][l].reshape(72, 128)
        vp[l, 72:96] = inp["norm_g"][l].reshape(24, 128)
        vp[l, 96:98] = inp["s5_d"][l].reshape(2, 128)
        vp[l, 98:102] = inp["s5_b_glu"][l].reshape(4, 128)
        vp[l, 102:105] = inp["mla_cq_g"][l].reshape(3, 128)
        vp[l, 105:107] = inp["mla_ckv_g"][l].reshape(2, 128)
        vp[l, 107] = np.tile(inp["na_qk_g"][l, 0], 2)
        vp[l, 108] = np.tile(inp["na_qk_g"][l, 1], 2)
        vp[l, 109, 0:96] = inp["mla_qk_g"][l, 0]
        vp[l, 110, 0:96] = inp["mla_qk_g"][l, 1]
    sh["vecpack"] = vp
    for k in ("w_mod", "ffn_w_gu", "ffn_w_down", "w_in", "w_out", "mla_w_uq", "mla_w_ukv", "s5_w_glu"):
        sh[k] = np.ascontiguousarray(inp[k], dtype=np.float32)
    nidx = c["nidx"]
    nab = np.empty((NL, 128, 126, 128), np.float32)
    for l in range(NL):
        for h in range(6):
            ext = np.concatenate([inp["na_rpb"][l, h].reshape(-1), np.array([NEG], np.float32)])
            g = ext[nidx]
            nab[l, :, h * 21:(h + 1) * 21, :] = g.transpose(1, 0, 2)
    sh["nab"] = nab
    s5p = np.empty((NL, 128, 3, 16), np.float32)
    s5B = np.zeros((NL, 128, 16, 2, 128), np.float32)
    s5C = np.zeros((NL, 128, 16, 2, 64), np.float32)
    for l in range(NL):
        for name, a in (("s5_lam_re", 0), ("s5_lam_im", 1)):
            v = inp[name][l].reshape(2, 8, 2, 64)
            s5p[l, :, a, :] = v.transpose(2, 3, 0, 1).reshape(128, 16)
        ld = inp["s5_log_dt"][l].reshape(2, 8, 2)
        s5p[l, :, 2, :] = np.broadcast_to(ld.transpose(2, 0, 1)[:, None, :, :], (2, 64, 2, 8)).reshape(128, 16)
        for ci, (bn, cn) in enumerate((("s5_b_re", "s5_c_re"), ("s5_b_im", "s5_c_im"))):
            b = inp[bn][l]
            cm = inp[cn][l]
            for d in range(2):
                for gp in range(8):
                    kk = d * 8 + gp
                    for g2 in range(2):
                        g = 2 * gp + g2
                        r0 = 32 * (gp % 4) + 16 * g2
                        s5B[l, r0:r0 + 16, kk, ci, g2 * 64:(g2 + 1) * 64] = b[d, g].T
                        s5C[l, g2 * 64:(g2 + 1) * 64, kk, ci, 32 * (gp % 2) + g2 * 16:32 * (gp % 2) + (g2 + 1) * 16] = cm[d, g].T
    sh["s5p"], sh["s5B"], sh["s5C"] = s5p, s5B, s5C
    sh["cf32"], sh["rope"], sh["iota"] = c["cf"], c["rope"], c["iota"]
    return sh


def _in_maps(inp, cores):
    sh = _prep_shared(inp)
    maps = []
    for b in cores:
        m = dict(sh)
        m["x"] = np.ascontiguousarray(inp["x"][b], dtype=np.float32)
        m["ctx"] = np.ascontiguousarray(inp["ctx"][b], dtype=np.float32)
        cv = np.empty((16, 128), np.float32)
        cv[0:8] = inp["c"][b].reshape(8, 128)
        cv[8:16] = inp["c_ctx"].reshape(8, 128)
        m["cvec"] = cv
        maps.append(m)
    return maps


_NC_CACHE = {}


def kernel(**inputs):
    inp = {k: np.asarray(v) for k, v in inputs.items()}
    if "full" not in _NC_CACHE:
        _NC_CACHE["full"] = K(2, None).build()
    nc = _NC_CACHE["full"]
    maps = _in_maps(inp, list(range(8)))
    res = run_bass_kernel_spmd(nc, maps, core_ids=list(range(8)))
    return np.stack([np.asarray(r["out"], dtype=np.float32) for r in res.results], axis=0)
```

```python
import math
import numpy as np
from contextlib import ExitStack
import concourse.bass as bass
import concourse.mybir as mybir
from concourse.ap import AP
from concourse.bass_utils import run_bass_kernel_spmd

F32 = mybir.dt.float32
BF16 = mybir.dt.bfloat16
I32 = mybir.dt.int32
AF = mybir.ActivationFunctionType
ALU = mybir.AluOpType

D = 1024
NT = 2304
LCTX = 256
NX = 2048
DFF = 2816
KC = 8
EPS = 1e-6
RANGES = [(0, 256)] + [(256 + 384 * k, 256 + 384 * (k + 1)) for k in range(5)] + [(2176, 2304)]
NEG = -30000.0
TWO_PI = 2.0 * math.pi

VP_BMOD = 0
VP_NG = 72
VP_S5D = 96
VP_BGLU = 98
VP_CQG = 102
VP_CKVG = 105
VP_NAQ = 107
VP_NAK = 108
VP_MQ = 109
VP_MK = 110


class Buf:
    __slots__ = ("name", "w", "r", "ps")

    def __init__(self, name="", ps=False):
        self.name = name
        self.w = None
        self.r = {}
        self.ps = ps


class Eng:
    def __init__(self, name, sem):
        self.name = name
        self.sem = sem
        self.cnt = 0
        self.seen = {}
        self.ops = []


class Ctx:
    def __init__(self, nc, sems, dma_sems):
        self.nc = nc
        self.E = {n: Eng(n, sems[n]) for n in ("pe", "act", "dve", "pool", "sp")}
        self.semobj = {("c", n): sems[n] for n in self.E}
        self.dma_sems = dma_sems
        self.dma_cnt = {q: [0] * len(v) for q, v in dma_sems.items()}
        self.dma_rr = {q: 0 for q in dma_sems}
        for q, v in dma_sems.items():
            for i, s in enumerate(v):
                self.semobj[("d", q, i)] = s
        self.ninstr = 0

    def _need(self, E, ev, waits):
        if ev is None:
            return
        key, val = ev
        if E.name == "pe" and key == ("c", "pe"):
            return
        if E.seen.get(key, 0) >= val:
            return
        if waits.get(key, 0) < val:
            waits[key] = val

    def _deps(self, E, reads, writes, accumulate):
        waits = {}
        for b in reads:
            self._need(E, b.w, waits)
            if b.ps:
                for k, v in b.r.items():
                    if k != ("c", E.name):
                        self._need(E, (k, v), waits)
        for b in writes:
            if not (accumulate and b.w is not None and b.w[0] == ("c", E.name)):
                self._need(E, b.w, waits)
            for k, v in b.r.items():
                self._need(E, (k, v), waits)
        for k, v in waits.items():
            E.seen[k] = v
        return list(waits.items())

    def op(self, eng, fn, reads=(), writes=(), accumulate=False):
        E = self.E[eng]
        waits = self._deps(E, reads, writes, accumulate)
        E.cnt += 1
        ev = (("c", eng), E.cnt)
        sem = E.sem
        semobj = self.semobj

        def run(e):
            for k, v in waits:
                e.wait_ge(semobj[k], v)
            fn(e).then_inc(sem, 1)

        E.ops.append(run)
        for b in writes:
            b.w = ev
            b.r = {}
        for b in reads:
            b.r[ev[0]] = ev[1]
        self.ninstr += 1
        return ev

    def dma(self, queue, fn, reads=(), writes=()):
        E = self.E[queue]
        waits = self._deps(E, reads, writes, False)
        i = self.dma_rr[queue]
        self.dma_rr[queue] = (i + 1) % len(self.dma_sems[queue])
        key = ("d", queue, i)
        prev = self.dma_cnt[queue][i]
        if prev > 0 and E.seen.get(key, 0) < prev:
            waits.append((key, prev))
            E.seen[key] = prev
        val = prev + 16
        self.dma_cnt[queue][i] = val
        sem = self.semobj[key]
        semobj = self.semobj

        def run(e):
            for k, v in waits:
                e.wait_ge(semobj[k], v)
            fn(e).then_inc(sem, 16)

        E.ops.append(run)
        ev = (key, val)
        for b in writes:
            b.w = ev
            b.r = {}
        for b in reads:
            b.r[key] = val
        self.ninstr += 1
        return ev

    def barrier(self):
        evs = []
        for n, E in self.E.items():
            if E.cnt > 0:
                evs.append((("c", n), E.cnt))
        for q, cnts in self.dma_cnt.items():
            for i, c in enumerate(cnts):
                if c > 0:
                    evs.append((("d", q, i), c))
        semobj = self.semobj
        for n, E in self.E.items():
            waits = []
            for k, v in evs:
                if k == ("c", n):
                    continue
                if E.seen.get(k, 0) < v:
                    waits.append((k, v))
                    E.seen[k] = v

            def run(e, waits=waits):
                for k, v in waits:
                    e.wait_ge(semobj[k], v)

            E.ops.append(run)

    def flush(self):
        self.barrier()
        E = self.E
        with self.nc.Block() as block:
            @block.tensor
            def _(e):
                for f in E["pe"].ops:
                    f(e)

            @block.scalar
            def _(e):
                for f in E["act"].ops:
                    f(e)

            @block.vector
            def _(e):
                for f in E["dve"].ops:
                    f(e)

            @block.gpsimd
            def _(e):
                for f in E["pool"].ops:
                    f(e)

            @block.sync
            def _(e):
                for f in E["sp"].ops:
                    f(e)
        for n in E:
            E[n].ops = []


class Arena:
    def __init__(self, ap, words):
        self.ap = ap
        self.words = words
        self.total = words
        self.top = 0

    def high_f32(self, n):
        self.words -= n
        assert self.top <= self.words
        return self.ap[:, self.words:self.words + n]

    def high_release(self):
        self.words = self.total

    def mark(self):
        return self.top

    def release(self, m):
        self.top = m

    def f32(self, n):
        a = self.top
        assert a + n <= self.words, ("arena overflow", a, n, self.words)
        self.top = a + n
        return self.ap[:, a:a + n]

    def bf16(self, n):
        w = (n + 1) // 2
        return self.f32(w).bitcast(BF16)

    def i32(self, n):
        return self.f32(n).bitcast(I32)


class K:
    def __init__(self, n_layers=2, stop=None):
        self.n_layers = n_layers
        self.stop = stop

    def mm(self, out, lhsT, rhs, start, stop, R, W):
        self.C.op("pe", lambda e: e.matmul(out, lhsT=lhsT, rhs=rhs, start=start, stop=stop),
                  reads=R, writes=W, accumulate=not start)

    def tr(self, out, in_, ident, R, W):
        self.C.op("pe", lambda e: e.transpose(out=out, in_=in_, identity=ident), reads=R, writes=W)

    def act(self, out, in_, func, R, W, bias=None, scale=None):
        kw = {}
        if bias is not None:
            kw["bias"] = bias
        if scale is not None:
            kw["scale"] = scale
        self.C.op("act", lambda e: e.activation(out=out, in_=in_, func=func, **kw), reads=R, writes=W)

    def tt(self, eng, out, in0, in1, op, R, W):
        self.C.op(eng, lambda e: e.tensor_tensor(out=out, in0=in0, in1=in1, op=op), reads=R, writes=W)

    def ts(self, eng, out, in0, s1, s2, op0, op1, R, W):
        if op1 is None:
            self.C.op(eng, lambda e: e.tensor_scalar(out=out, in0=in0, scalar1=s1, scalar2=None, op0=op0),
                      reads=R, writes=W)
        else:
            self.C.op(eng, lambda e: e.tensor_scalar(out=out, in0=in0, scalar1=s1, scalar2=s2, op0=op0, op1=op1),
                      reads=R, writes=W)

    def stt(self, eng, out, in0, scalar, in1, op0, op1, R, W):
        self.C.op(eng, lambda e: e.scalar_tensor_tensor(out=out, in0=in0, scalar=scalar, in1=in1, op0=op0, op1=op1),
                  reads=R, writes=W)

    def cp(self, eng, out, in_, R, W):
        if eng == "act":
            self.C.op("act", lambda e: e.activation(out=out, in_=in_, func=AF.Copy), reads=R, writes=W)
        else:
            self.C.op(eng, lambda e: e.tensor_copy(out=out, in_=in_), reads=R, writes=W)

    def recip(self, out, in_, R, W):
        self.C.op("dve", lambda e: e.reciprocal(out=out, in_=in_), reads=R, writes=W)

    def memset(self, eng, ap, val, W):
        self.C.op(eng, lambda e: e.memset(ap, val), reads=[], writes=W)

    def scan(self, out, d0, d1, init, R, W):
        self.C.op("dve", lambda e: e.tensor_tensor_scan(out=out, data0=d0, data1=d1, initial=init,
                                                        op0=ALU.mult, op1=ALU.add), reads=R, writes=W)

    def ld(self, out, in_, W, R=(), cast=False):
        q = "pool" if cast else "sp"
        self.C.dma(q, lambda e: e.dma_start(out=out, in_=in_), reads=list(R), writes=W)

    def st(self, out, in_, R):
        self.C.dma("sp", lambda e: e.dma_start(out=out, in_=in_), reads=R, writes=[])

    def bank(self, i, w=512):
        return self.PS[i][:, 0:w]

    def build(self):
        nc = bass.Bass("TRN2", target_bir_lowering=False)
        self.nc = nc
        NL = 2

        def din(name, shape):
            return nc.dram_tensor(name, list(shape), F32, kind="ExternalInput").ap()

        T = {}
        T["x"] = din("x", [NX, D])
        T["ctx"] = din("ctx", [LCTX, D])
        T["cvec"] = din("cvec", [16, 128])
        T["vecpack"] = din("vecpack", [NL, 128, 128])
        T["w_mod"] = din("w_mod", [NL, D, 9 * D])
        T["w_gu"] = din("ffn_w_gu", [NL, 2, D, 2 * DFF])
        T["w_down"] = din("ffn_w_down", [NL, 2, DFF, D])
        T["w_in"] = din("w_in", [NL, D, 2080])
        T["w_out"] = din("w_out", [NL, D, D])
        T["w_uq"] = din("mla_w_uq", [NL, 384, 576])
        T["w_ukv"] = din("mla_w_ukv", [NL, 256, 768])
        T["w_glu"] = din("s5_w_glu", [NL, 256, 512])
        T["nab"] = din("nab", [NL, 128, 126, 128])
        T["s5p"] = din("s5p", [NL, 128, 3, 16])
        T["s5B"] = din("s5B", [NL, 128, 16, 2, 128])
        T["s5C"] = din("s5C", [NL, 128, 16, 2, 64])
        T["cf32"] = din("cf32", [128, 5, 128])
        T["rope"] = din("rope", [96, 2, NT])
        T["iota"] = din("iota", [128, 384])
        self.T = T
        out = nc.dram_tensor("out", [NX, D], F32, kind="ExternalOutput").ap()
        self.dbg = None
        if self.stop is not None:
            self.dbg = nc.dram_tensor("dbg", [128, KC * NT], F32, kind="ExternalOutput").ap()

        with ExitStack() as es:
            def sb(name, shape, dt):
                return es.enter_context(nc.sbuf_tensor(name, shape, dt))

            sems = {n: es.enter_context(nc.semaphore("s_" + n)) for n in ("pe", "act", "dve", "pool", "sp")}
            dsems = {q: [es.enter_context(nc.semaphore(f"d_{q}{i}")) for i in range(8)] for q in ("sp", "pool")}
            self.C = Ctx(nc, sems, dsems)
            self.PS = [es.enter_context(nc.psum_tensor(f"ps{i}", [128, 512], F32)) for i in range(8)]
            self.BPS = [Buf(f"ps{i}", ps=True) for i in range(8)]
            AW = 49152
            arena_t = sb("arena", [128, AW], F32)
            self.A = Arena(arena_t[:], AW)
            self.cf = sb("cf", [128, 5, 128], F32)
            self.Bcf = Buf("cf")
            self.identb = sb("identb", [128, 128], BF16)
            self.Bidb = Buf("identb")
            self.iota0 = sb("iota0", [128, 384], F32)
            self.Biota = Buf("iota")
            self.vp = sb("vp", [128, 128], F32)
            self.Bvp = Buf("vp")
            self.modT = sb("modT", [128, 72, 2], F32)
            self.Amod = sb("Amod", [128, 3, 8, 2], F32)
            self.Gmod = sb("Gmod", [128, 3, 8, 2], F32)
            self.Bmod = Buf("mod")
            self.scT = sb("scT", [128, 2, 8], BF16)
            self.Bsc = Buf("scT")
            self.small = sb("small", [128, 64], F32)
            self.Bsmall = Buf("small")

            self.hT = self.A.f32(KC * NT).rearrange("p (c t) -> p c t", c=KC)
            self.Bh = [Buf(f"h{r}") for r in range(len(RANGES))]

            self.setup()
            if self.stop == "load":
                self.dump_h()
            else:
                for l in range(self.n_layers):
                    if not self.layer(l):
                        break
                else:
                    self.final(out)
            self.C.flush()
        return nc

    def ranges_of(self, c0, c1):
        return [i for i, (a, b) in enumerate(RANGES) if a < c1 and b > c0]

    def dump_h(self):
        self.C.flush()
        self.st(self.dbg, self.hT.rearrange("p c t -> p (c t)"), R=self.Bh)

    def dump_ap(self, ap, nwords, R):
        self.C.flush()
        self.st(self.dbg[:, 0:nwords], ap, R=R)

    def setup(self):
        T = self.T
        self.ld(self.cf[:], T["cf32"], [self.Bcf])
        self.ld(self.identb[:], T["cf32"][:, 0, :], [self.Bidb], cast=True)
        self.ld(self.iota0[:], T["iota"], [self.Biota])
        self.ident = self.cf[:, 0, :]
        self.ones = self.cf[:, 1, :]
        self.bones = self.cf[:, 2, :]
        self.perm = self.cf[0:96, 3, 0:96]
        self.misc = self.cf[:, 4, :]
        m = self.A.mark()
        stg = [self.A.f32(D) for _ in range(2)]
        Bstg = [Buf("stg0"), Buf("stg1")]
        for t in range(18):
            src = T["ctx"][t * 128:(t + 1) * 128, :] if t < 2 else T["x"][(t - 2) * 128:(t - 1) * 128, :]
            s = t % 2
            self.ld(stg[s], src, [Bstg[s]])
            rs = self.ranges_of(t * 128, t * 128 + 128)
            for half in range(2):
                b = (2 * t + half) % 8
                pb = self.bank(b).rearrange("p (a q) -> p a q", a=4)
                for a in range(4):
                    c = half * 4 + a
                    self.tr(pb[:, a, :], stg[s][:, c * 128:(c + 1) * 128], self.ident,
                            R=[Bstg[s], self.Bcf], W=[self.BPS[b]])
                eng = "act" if half == 0 else "dve"
                self.cp(eng, self.hT[:, half * 4:half * 4 + 4, t * 128:(t + 1) * 128], pb,
                        R=[self.BPS[b]], W=[self.Bh[r] for r in rs])
        self.C.flush()
        self.A.release(m)

    def layer(self, l):
        self.l = l
        self.adaln(l)
        if self.stop == f"mod{l}":
            self.dump_ap(self.modT[:].rearrange("p a b -> p (a b)"), 144, [self.Bmod])
            return False
        self.ffn(l, 0, 0)
        if self.stop == f"ffn1_{l}":
            self.dump_h()
            return False
        if not self.mixer(l):
            return False
        if self.stop == f"mix_{l}":
            self.dump_h()
            return False
        self.ffn(l, 1, 2)
        if self.stop == f"ffn2_{l}":
            self.dump_h()
            return False
        return True

    def adaln(self, l):
        T = self.T
        A = self.A
        m = A.mark()
        stg = A.f32(128)
        Bst = Buf()
        b = 0
        if l == 0:
            self.ld(stg[0:16, :], T["cvec"], [Bst])
            self.tr(self.bank(b)[:, 0:16], stg[0:16, :], self.ident[0:16, 0:16], R=[Bst, self.Bcf], W=[self.BPS[b]])
            self.act(self.scT[:], self.bank(b)[:, 0:16].rearrange("p (s k) -> p s k", s=2), AF.Silu, R=[self.BPS[b]], W=[self.Bsc])
        stg2 = A.f32(128)
        Bst2 = Buf()
        self.ld(stg2, T["vecpack"][l], [Bst2])
        self.tr(self.bank(1)[:, 0:128], stg2, self.ident, R=[Bst2, self.Bcf], W=[self.BPS[1]])
        self.cp("dve", self.vp[:], self.bank(1)[:, 0:128], R=[self.BPS[1]], W=[self.Bvp])
        NP = 8
        PW = 1152
        wb = [A.bf16(KC * PW).rearrange("p (k n) -> p k n", k=KC) for _ in range(2)]
        Bwb = [Buf(), Buf()]
        pm = self.bank(2)
        wsrc = T["w_mod"][l].rearrange("(k p) n -> p k n", p=128)
        for pc in range(NP):
            s = pc % 2
            self.ld(wb[s][:, :, :], wsrc[:, :, pc * PW:(pc + 1) * PW], [Bwb[s]], cast=True)
            for j in range(9):
                cc = pc * 9 + j
                for k in range(KC):
                    self.mm(pm[:, 2 * cc:2 * cc + 2], wb[s][:, k, j * 128:(j + 1) * 128],
                            self.scT[:, :, k], k == 0, k == KC - 1, R=[Bwb[s], self.Bsc], W=[self.BPS[2]])
        self.tt("dve", self.modT[:], pm[:, 0:144].rearrange("p (a b) -> p a b", b=2),
                self.vp[:, 0:72].unsqueeze(2).to_broadcast([128, 72, 2]), ALU.add,
                R=[self.BPS[2], self.Bvp], W=[self.Bmod])
        for n in range(3):
            sc = self.modT[:, (3 * n + 1) * 8:(3 * n + 1) * 8 + 8, :]
            gt = self.modT[:, (3 * n + 2) * 8:(3 * n + 2) * 8 + 8, :]
            g = self.vp[:, VP_NG + 8 * n:VP_NG + 8 * n + 8].unsqueeze(2).to_broadcast([128, 8, 2])
            self.stt("dve", self.Amod[:, n, :, :], sc, 1.0, g, ALU.add, ALU.mult, R=[self.Bmod, self.Bvp], W=[self.Bmod])
            self.ts("dve", self.Gmod[:, n, :, :], gt, 0.5 if n != 1 else 1.0, None, ALU.mult, None,
                    R=[self.Bmod], W=[self.Bmod])
        self.ts("dve", self.small[:, 0:1], self.vp[:, VP_NAQ:VP_NAQ + 1], 0.125, None, ALU.mult, None,
                R=[self.Bvp], W=[self.Bsmall])
        self.ts("dve", self.small[:, 1:2], self.vp[:, VP_MQ:VP_MQ + 1], 96.0 ** -0.5, None, ALU.mult, None,
                R=[self.Bvp], W=[self.Bsmall])
        self.C.flush()
        A.release(m)

    def shift_ap(self, n, c, s):
        return self.modT[:, 3 * n * 8 + c, s:s + 1]

    def modulate(self, n, xm, Bxm, rlist):
        A = self.A
        m = A.mark()
        NSL = 3
        sq = [[A.f32(384) for _ in range(2)] for _ in range(NSL)]
        Bsq = [[Buf(), Buf()] for _ in range(NSL)]
        rstd = [A.f32(384) for _ in range(NSL)]
        Brs = [Buf() for _ in range(NSL)]
        tmp = [[A.f32(384) for _ in range(2)] for _ in range(NSL)]
        Btmp = [[Buf(), Buf()] for _ in range(NSL)]

        def chain(ri, u):
            c0, c1 = RANGES[ri]
            w = c1 - c0
            s = 1 if ri == 0 else 0
            b = u
            pb = self.bank(b, w)
            for c in range(KC):
                v = c % 2
                self.tt("pool", sq[u][v][:, 0:w], self.hT[:, c, c0:c1], self.hT[:, c, c0:c1], ALU.mult,
                        R=[self.Bh[ri]], W=[Bsq[u][v]])
                self.mm(pb, self.ones, sq[u][v][:, 0:w], c == 0, c == KC - 1, R=[Bsq[u][v], self.Bcf], W=[self.BPS[b]])
                if c % 2 == 1:
                    yield
            self.act(rstd[u][:, 0:w], pb, AF.Ln, R=[self.BPS[b], self.Bcf], W=[Brs[u]],
                     bias=self.misc[:, 0:1], scale=1.0 / D)
            yield
            self.act(rstd[u][:, 0:w], rstd[u][:, 0:w], AF.Exp, R=[Brs[u]], W=[Brs[u]], scale=-0.5)
            yield
            for c in range(KC):
                v = c % 2
                self.tt("dve", tmp[u][v][:, 0:w], self.hT[:, c, c0:c1], rstd[u][:, 0:w], ALU.mult,
                        R=[self.Bh[ri], Brs[u]], W=[Btmp[u][v]])
                self.act(xm[:, c, c0:c1], tmp[u][v][:, 0:w], AF.Identity, R=[Btmp[u][v], self.Bmod], W=[Bxm[ri]],
                         bias=self.shift_ap(n, c, s), scale=self.Amod[:, n, c, s:s + 1])
                if c % 2 == 1:
                    yield

        def stream(u):
            for _ in range(3 * u):
                yield
            for ri in rlist[u::NSL]:
                yield from chain(ri, u)

        live = [stream(u) for u in range(NSL)]
        while live:
            for g in list(live):
                try:
                    next(g)
                except StopIteration:
                    live.remove(g)
        self.C.flush()
        A.release(m)

    def ffn(self, l, f, n):
        T = self.T
        A = self.A
        rlist = list(range(len(RANGES)))
        if l == self.n_layers - 1 and f == 1:
            rlist = rlist[1:]
        m0 = A.mark()
        xm = A.bf16(KC * NT).rearrange("p (c t) -> p c t", c=KC)
        Bxm = [Buf() for _ in RANGES]
        GS = 4
        groups = [(j0, min(GS, 22 - j0)) for j0 in range(0, 22, GS)]
        wgu = [A.bf16(KC * 2 * 512).rearrange("p (k g n) -> p k g n", k=KC, g=2) for _ in range(2)]
        Bwgu = [Buf(), Buf()]
        wdn = [A.bf16(GS * D).rearrange("p (j n) -> p j n", j=GS) for _ in range(2)]
        Bwdn = [Buf(), Buf()]
        stmp = [A.f32(384) for _ in range(2)]
        Bstmp = [Buf(), Buf()]
        gsrc = T["w_gu"][l, f].rearrange("(k p) n -> p k n", p=128)
        dsrc = T["w_down"][l, f].rearrange("(j p) n -> p j n", p=128)

        def load_group(gi):
            j0, nj = groups[gi]
            s = gi % 2
            ncol = nj * 128
            self.ld(wgu[s][:, :, 0, 0:ncol], gsrc[:, :, j0 * 128:j0 * 128 + ncol], [Bwgu[s]], cast=True)
            self.ld(wgu[s][:, :, 1, 0:ncol], gsrc[:, :, DFF + j0 * 128:DFF + j0 * 128 + ncol], [Bwgu[s]], cast=True)
            self.ld(wdn[s][:, 0:nj, :], dsrc[:, j0:j0 + nj, :], [Bwdn[s]], cast=True)

        load_group(0)
        load_group(1)
        self.modulate(n, xm, Bxm, rlist)
        hid = A.bf16(GS * NT).rearrange("p (j t) -> p j t", j=GS)
        Bhid = [[Buf() for _ in RANGES] for _ in range(GS)]
        q = 0
        pbk = 0
        for gi, (j0, nj) in enumerate(groups):
            s = gi % 2
            ncol = nj * 128
            if gi >= 2:
                load_group(gi)
            for jj in range(nj):
                for ri in rlist:
                    c0, c1 = RANGES[ri]
                    w = c1 - c0
                    bg = pbk % 8
                    bu = (pbk + 1) % 8
                    pbk += 2
                    pg = self.bank(bg, w)
                    pu = self.bank(bu, w)
                    for k in range(KC):
                        self.mm(pg, wgu[s][:, k, 0, jj * 128:(jj + 1) * 128], xm[:, k, c0:c1], k == 0, k == KC - 1,
                                R=[Bwgu[s], Bxm[ri]], W=[self.BPS[bg]])
                    for k in range(KC):
                        self.mm(pu, wgu[s][:, k, 1, jj * 128:(jj + 1) * 128], xm[:, k, c0:c1], k == 0, k == KC - 1,
                                R=[Bwgu[s], Bxm[ri]], W=[self.BPS[bu]])
                    u = q % 2
                    q += 1
                    self.act(stmp[u][:, 0:w], pg, AF.Silu, R=[self.BPS[bg]], W=[Bstmp[u]])
                    self.tt("dve", hid[:, jj, c0:c1], stmp[u][:, 0:w], pu, ALU.mult,
                            R=[Bstmp[u], self.BPS[bu]], W=[Bhid[jj][ri]])
            for oc in range(KC):
                for ri in rlist:
                    c0, c1 = RANGES[ri]
                    w = c1 - c0
                    sidx = 1 if ri == 0 else 0
                    b = pbk % 8
                    pbk += 1
                    po = self.bank(b, w)
                    for jj in range(nj):
                        self.mm(po, wdn[s][:, jj, oc * 128:(oc + 1) * 128], hid[:, jj, c0:c1], jj == 0, jj == nj - 1,
                                R=[Bwdn[s], Bhid[jj][ri]], W=[self.BPS[b]])
                    self.stt("dve", self.hT[:, oc, c0:c1], po, self.Gmod[:, n, oc, sidx:sidx + 1], self.hT[:, oc, c0:c1],
                             ALU.mult, ALU.add, R=[self.BPS[b], self.Bmod, self.Bh[ri]], W=[self.Bh[ri]])
        self.C.flush()
        A.release(m0)

    def mixer(self, l):
        A = self.A
        m0 = A.mark()
        self.need_ctx = l < self.n_layers - 1
        xm = A.high_f32(KC * NT // 2).bitcast(BF16).rearrange("p (c t) -> p c t", c=KC)
        Bxm = [Buf() for _ in RANGES]
        self.xm2 = xm
        self.Bxm2 = Bxm
        self.modulate(1, xm, Bxm, list(range(len(RANGES))))
        self.mix_na(l)
        if self.stop == f"na_{l}":
            self.dump_h()
            return False
        self.mix_s5(l)
        if self.stop in ("s5a", "s5b", "s5c"):
            return False
        if self.stop == f"s5_{l}":
            self.dump_h()
            return False
        self.mix_mla(l, m0)
        A.release(m0)
        return True

    def load_win(self, l, col0, ncols, dst, Bdst):
        src = self.T["w_in"][l].rearrange("(k p) n -> p k n", p=128)
        self.ld(dst[:, :, 0:ncols], src[:, :, col0:col0 + ncols], [Bdst], cast=True)

    def outproj_tiles(self, l, a_tok, Ba, tiles, krow0, nkc, wout, Bwout):
        A = self.A
        m = A.mark()
        oT = [A.bf16(nkc * 128).rearrange("p (k t) -> p k t", k=nkc) for _ in range(2)]
        BoT = [Buf(), Buf()]
        pbk = 0
        for qi, t in enumerate(tiles):
            u = qi % 2
            ri = self.ranges_of(t * 128, t * 128 + 128)
            sidx = 1 if t < 2 else 0
            bt = 6 + (qi % 2)
            ptr = self.PS[bt][:].bitcast(BF16)
            for k in range(nkc):
                self.tr(ptr[:, k * 128:(k + 1) * 128], a_tok[:, t, k * 128:(k + 1) * 128], self.identb[:],
                        R=[Ba, self.Bidb], W=[self.BPS[bt]])
            self.cp("act", oT[u][:], ptr[:, 0:nkc * 128].rearrange("p (k t) -> p k t", k=nkc),
                    R=[self.BPS[bt]], W=[BoT[u]])
            for half in range(2):
                b = pbk % 6
                pbk += 1
                pb = self.bank(b).rearrange("p (a q) -> p a q", a=4)
                for a in range(4):
                    oc = half * 4 + a
                    for k in range(nkc):
                        self.mm(pb[:, a, :], wout[:, k, oc * 128:(oc + 1) * 128], oT[u][:, k, :], k == 0, k == nkc - 1,
                                R=[Bwout, BoT[u]], W=[self.BPS[b]])
                for a in range(4):
                    oc = half * 4 + a
                    dst = self.hT[:, oc, t * 128:(t + 1) * 128]
                    self.stt("dve", dst, pb[:, a, :], self.Gmod[:, 1, oc, sidx:sidx + 1], dst, ALU.mult, ALU.add,
                             R=[self.BPS[b], self.Bmod] + [self.Bh[r] for r in ri], W=[self.Bh[r] for r in ri])
        self.C.flush()
        A.release(m)

    def load_wout(self, l, krow0, nkc):
        wout = self.A.bf16(nkc * D).rearrange("p (k n) -> p k n", k=nkc)
        Bw = Buf()
        src = self.T["w_out"][l].rearrange("(k p) n -> p k n", p=128)
        self.ld(wout[:, :, :], src[:, krow0:krow0 + nkc, :], [Bw], cast=True)
        return wout, Bw

    def headnorm_fm(self, zps, Bz, w, nparts, onesmat, inv_d, gain_ap, out_ap, Bout, tmps):
        sq, Bsq, rs, Brs, bss = tmps
        self.act(sq[0:nparts, 0:w], zps, AF.Square, R=[Bz], W=[Bsq])
        pss = self.bank(bss, w)[0:nparts, :]
        self.mm(pss, onesmat, sq[0:nparts, 0:w], True, True, R=[Bsq, self.Bcf], W=[self.BPS[bss]])
        self.act(rs[0:nparts, 0:w], pss, AF.Ln, R=[self.BPS[bss], self.Bcf], W=[Brs],
                 bias=self.misc[0:nparts, 0:1], scale=inv_d)
        self.act(rs[0:nparts, 0:w], rs[0:nparts, 0:w], AF.Exp, R=[Brs], W=[Brs], scale=-0.5)
        self.stt("dve", out_ap, zps, gain_ap, rs[0:nparts, 0:w], ALU.mult, ALU.mult,
                 R=[Bz, Brs, self.Bsmall, self.Bvp], W=[Bout])

    def mix_na(self, l):
        T = self.T
        A = self.A
        m = A.mark()
        xm, Bxm = self.xm2, self.Bxm2
        qn = A.bf16(3 * NT).rearrange("p (c t) -> p c t", c=3)
        kn = A.bf16(3 * NT).rearrange("p (c t) -> p c t", c=3)
        Bqn, Bkn = Buf(), Buf()
        vna = A.bf16(18 * 6 * 65).rearrange("p (t h d) -> p t h d", t=18, h=6)
        Bv = Buf()
        a_tok = A.bf16(18 * 384).rearrange("p (t d) -> p t d", t=18)
        Ba = Buf()
        m1 = A.mark()
        wbuf = [A.bf16(KC * 384).rearrange("p (k n) -> p k n", k=KC) for _ in range(2)]
        Bw = [Buf(), Buf()]
        NSL = 3
        sq = [A.f32(384) for _ in range(NSL)]
        rs = [A.f32(384) for _ in range(NSL)]
        Bsq = [Buf() for _ in range(NSL)]
        Brs = [Buf() for _ in range(NSL)]
        self.memset("pool", vna[:, :, :, 64:65], 1.0, [Bv])
        self.load_win(l, 0, 384, wbuf[0], Bw[0])
        self.load_win(l, 384, 384, wbuf[1], Bw[1])

        def na_chain(part, cch, ri, u):
            dstT, Bd = (qn, Bqn) if part == 0 else (kn, Bkn)
            gain = self.small[:, 0:1] if part == 0 else self.vp[:, VP_NAK:VP_NAK + 1]
            c0, c1 = RANGES[ri]
            w = c1 - c0
            bz = 2 * u
            bs_ = 2 * u + 1
            pz = self.bank(bz, w)
            for k in range(KC):
                self.mm(pz, wbuf[part][:, k, cch * 128:(cch + 1) * 128], xm[:, k, c0:c1], k == 0, k == KC - 1,
                        R=[Bw[part], Bxm[ri]], W=[self.BPS[bz]])
            yield
            self.act(sq[u][:, 0:w], pz, AF.Square, R=[self.BPS[bz]], W=[Bsq[u]])
            pss = self.bank(bs_, w)
            self.mm(pss, self.bones, sq[u][:, 0:w], True, True, R=[Bsq[u], self.Bcf], W=[self.BPS[bs_]])
            yield
            self.act(rs[u][:, 0:w], pss, AF.Ln, R=[self.BPS[bs_], self.Bcf], W=[Brs[u]],
                     bias=self.misc[:, 0:1], scale=1.0 / 64)
            yield
            self.act(rs[u][:, 0:w], rs[u][:, 0:w], AF.Exp, R=[Brs[u]], W=[Brs[u]], scale=-0.5)
            yield
            self.stt("dve", dstT[:, cch, c0:c1], pz, gain, rs[u][:, 0:w], ALU.mult, ALU.mult,
                     R=[self.BPS[bz], Brs[u], self.Bsmall, self.Bvp], W=[Bd])
            yield

        lst = [(part, cch, ri) for part in range(2) for cch in range(3) for ri in range(len(RANGES))]

        def na_stream(u):
            for _ in range(2 * u):
                yield
            for (part, cch, ri) in lst[u::NSL]:
                yield from na_chain(part, cch, ri, u)

        live = [na_stream(u) for u in range(NSL)]
        while live:
            for g in list(live):
                try:
                    next(g)
                except StopIteration:
                    live.remove(g)
        self.load_win(l, 768, 384, wbuf[0], Bw[0])
        for t in range(18):
            b = t % 4
            pv = self.bank(b, 384)
            ri = self.ranges_of(t * 128, t * 128 + 128)
            for k in range(KC):
                self.mm(pv, xm[:, k, t * 128:(t + 1) * 128], wbuf[0][:, k, :], k == 0, k == KC - 1,
                        R=[Bw[0]] + [Bxm[r] for r in ri], W=[self.BPS[b]])
            self.cp("act" if t % 2 else "dve", vna[:, t, :, 0:64], pv.rearrange("p (h d) -> p h d", h=6),
                    R=[self.BPS[b]], W=[Bv])
        self.C.flush()
        A.release(m1)
        bias = [A.bf16(21 * 128).rearrange("p (t q) -> p t q", t=21) for _ in range(2)]
        Bb = [Buf(), Buf()]
        pT = [A.bf16(512).rearrange("p (t q) -> p t q", t=4) for _ in range(3)]
        BpT = [Buf() for _ in range(3)]
        rc = A.f32(8)
        Brc = Buf()
        cls = {}
        ci = 0
        for dm in range(-2, 3):
            cls[("int", dm)] = ci
            ci += 1
        for i in (0, 1, 14, 15):
            mlo = {0: 0, 1: 0, 14: 12, 15: 12}[i]
            for mm_ in range(mlo, mlo + 4):
                cls[(i, mm_)] = ci
                ci += 1
        units = []
        for h in range(6):
            qtiles = list(range(2, 18)) + ([0, 1] if self.need_ctx else [])
            for ti, t in enumerate(qtiles):
                if t >= 2:
                    i = t - 2
                    if 2 <= i <= 13:
                        kts = [(0, None), (1, None)] + [(2 + i + dm, cls[("int", dm)]) for dm in range(-2, 3)]
                    else:
                        mlo = {0: 0, 1: 0, 14: 12, 15: 12}[i]
                        kts = [(0, None), (1, None)] + [(2 + mm_, cls[(i, mm_)]) for mm_ in range(mlo, mlo + 4)]
                else:
                    kts = [(0, None), (1, None)]
                nk = len(kts)
                for g0 in range(0, nk, 4):
                    units.append(dict(h=h, t=t, grp=kts[g0:g0 + 4], g0=g0, nk=nk, pair=len(units) if g0 == 0 else None,
                                      newhead=(ti == 0 and g0 == 0)))
        npair = 0
        for un in units:
            if un["g0"] == 0:
                npair += 1
            un["bo"] = 4 + (npair % 2)

        def emit_S(i):
            un = units[i]
            h, t = un["h"], un["t"]
            hb = h % 2
            if un["newhead"]:
                self.ld(bias[hb][:], T["nab"][l][:, h * 21:(h + 1) * 21, :], [Bb[hb]], cast=True)
            cch = h // 2
            p0 = (h % 2) * 64
            qsl = qn[p0:p0 + 64, cch, t * 128:(t + 1) * 128]
            b = i % 4
            pS = self.bank(b).rearrange("p (a q) -> p a q", a=4)
            for gi, (kt, bc) in enumerate(un["grp"]):
                self.mm(pS[:, gi, :], kn[p0:p0 + 64, cch, kt * 128:(kt + 1) * 128], qsl, True, bc is None,
                        R=[Bkn, Bqn], W=[self.BPS[b]])
                if bc is not None:
                    self.mm(pS[:, gi, :], self.identb[:], bias[hb][:, bc, :], False, True,
                            R=[self.Bidb, Bb[hb]], W=[self.BPS[b]])

        emit_S(0)
        for i, un in enumerate(units):
            h, t = un["h"], un["t"]
            b = i % 4
            u = i % 3
            bo = un["bo"]
            po = self.bank(bo, 65)
            pS = self.bank(b).rearrange("p (a q) -> p a q", a=4)
            ng = len(un["grp"])
            self.act(pT[u][:, 0:ng, :], pS[:, 0:ng, :], AF.Exp, R=[self.BPS[b]], W=[BpT[u]])
            if i + 1 < len(units):
                emit_S(i + 1)
            for gi, (kt, bc) in enumerate(un["grp"]):
                done = un["g0"] + gi
                self.mm(po, pT[u][:, gi, :], vna[:, kt, h, :], done == 0, done == un["nk"] - 1,
                        R=[BpT[u], Bv], W=[self.BPS[bo]])
            if un["g0"] + ng == un["nk"]:
                self.recip(rc[:, 0:1], po[:, 64:65], R=[self.BPS[bo]], W=[Brc])
                self.ts("dve", a_tok[:, t, h * 64:(h + 1) * 64], po[:, 0:64], rc[:, 0:1], None, ALU.mult, None,
                        R=[self.BPS[bo], Brc], W=[Ba])
        self.C.flush()
        wout, Bwo = self.load_wout(l, 0, 3)
        tiles = list(range(2, 18)) + ([0, 1] if self.need_ctx else [])
        self.outproj_tiles(l, a_tok, Ba, tiles, 0, 3, wout, Bwo)
        A.release(m)

    def mix_s5(self, l):
        T = self.T
        A = self.A
        m = A.mark()
        xm, Bxm = self.xm2, self.Bxm2
        ysb = A.f32(2 * NT).rearrange("p (c t) -> p c t", c=2)
        By = [[Buf() for _ in RANGES] for _ in range(2)]
        ubf = A.bf16(2 * NT).rearrange("p (c t) -> p c t", c=2)
        Bu = [Buf() for _ in RANGES]
        m1 = A.mark()
        wbuf = A.bf16(KC * 256).rearrange("p (k n) -> p k n", k=KC)
        Bw = Buf()
        self.load_win(l, 1824, 256, wbuf, Bw)
        pbk = 0
        for ch in range(2):
            for ri in range(len(RANGES)):
                c0, c1 = RANGES[ri]
                w = c1 - c0
                b = pbk % 4
                pbk += 1
                pz = self.bank(b, w)
                for k in range(KC):
                    self.mm(pz, wbuf[:, k, ch * 128:(ch + 1) * 128], xm[:, k, c0:c1], k == 0, k == KC - 1,
                            R=[Bw, Bxm[ri]], W=[self.BPS[b]])
                self.ts("dve", ysb[:, ch, c0:c1], pz, self.vp[:, VP_S5D + ch:VP_S5D + ch + 1], None, ALU.mult, None,
                        R=[self.BPS[b], self.Bvp], W=[By[ch][ri]])
                self.cp("dve", ubf[:, ch, c0:c1], pz, R=[self.BPS[b]], W=[Bu[ri]])
        self.C.flush()
        A.release(m1)
        if self.stop == "s5a":
            self.dump_h()
            return
        prm = A.f32(48).rearrange("p (a k) -> p a k", a=3)
        Bp = Buf()
        self.ld(prm, T["s5p"][l], [Bp])
        NS = 24
        sc = A.f32(NS * 16).rearrange("p (a k) -> p a k", a=NS)
        Bs = Buf()
        (DT, RHO, TH, SS, UT, FR, SN, CS, NR, NI, DEN, FRE, FIM, NFI, T1, T2, AFR, CRE, CIM, OFF, KI) = range(21)
        lr, li, ldt = prm[:, 0, :], prm[:, 1, :], prm[:, 2, :]
        sgn = self.misc[:, 16:32]
        hp = self.misc[:, 1:2]
        R_ = [Bp, Bs, self.Bcf]
        W_ = [Bs]
        self.act(sc[:, DT, :], ldt, AF.Exp, R=R_, W=W_)
        self.tt("dve", sc[:, T1, :], lr, sc[:, DT, :], ALU.mult, R=R_, W=W_)
        self.act(sc[:, RHO, :], sc[:, T1, :], AF.Exp, R=R_, W=W_)
        self.tt("dve", sc[:, TH, :], li, sc[:, DT, :], ALU.mult, R=R_, W=W_)
        self.ts("dve", sc[:, UT, :], sc[:, TH, :], 1.0 / TWO_PI, None, ALU.mult, None, R=R_, W=W_)
        self.tt("dve", sc[:, SS, :], sc[:, UT, :], sgn, ALU.mult, R=R_, W=W_)
        ki = A.i32(16)
        self.cp("dve", ki, sc[:, UT, :], R=R_, W=W_)
        self.tt("dve", sc[:, FR, :], sc[:, UT, :], ki, ALU.subtract, R=R_, W=W_)
        self.act(sc[:, SN, :], sc[:, FR, :], AF.Sin, R=R_, W=W_, scale=TWO_PI)
        self.act(sc[:, AFR, :], sc[:, FR, :], AF.Abs, R=R_, W=W_)
        self.act(sc[:, CS, :], sc[:, AFR, :], AF.Sin, R=R_, W=W_, scale=-TWO_PI, bias=hp)
        self.tt("dve", sc[:, NR, :], sc[:, RHO, :], sc[:, CS, :], ALU.mult, R=R_, W=W_)
        self.ts("dve", sc[:, NR, :], sc[:, NR, :], -1.0, None, ALU.add, None, R=R_, W=W_)
        self.tt("dve", sc[:, NI, :], sc[:, RHO, :], sc[:, SN, :], ALU.mult, R=R_, W=W_)
        self.tt("dve", sc[:, DEN, :], lr, lr, ALU.mult, R=R_, W=W_)
        self.tt("dve", sc[:, T1, :], li, li, ALU.mult, R=R_, W=W_)
        self.tt("dve", sc[:, DEN, :], sc[:, DEN, :], sc[:, T1, :], ALU.add, R=R_, W=W_)
        self.recip(sc[:, DEN, :], sc[:, DEN, :], R=R_, W=W_)
        self.tt("dve", sc[:, T1, :], sc[:, NR, :], lr, ALU.mult, R=R_, W=W_)
        self.tt("dve", sc[:, T2, :], sc[:, NI, :], li, ALU.mult, R=R_, W=W_)
        self.tt("dve", sc[:, T1, :], sc[:, T1, :], sc[:, T2, :], ALU.add, R=R_, W=W_)
        self.tt("dve", sc[:, FRE, :], sc[:, T1, :], sc[:, DEN, :], ALU.mult, R=R_, W=W_)
        self.tt("dve", sc[:, T1, :], sc[:, NI, :], lr, ALU.mult, R=R_, W=W_)
        self.tt("dve", sc[:, T2, :], sc[:, NR, :], li, ALU.mult, R=R_, W=W_)
        self.tt("dve", sc[:, T1, :], sc[:, T1, :], sc[:, T2, :], ALU.subtract, R=R_, W=W_)
        self.tt("dve", sc[:, FIM, :], sc[:, T1, :], sc[:, DEN, :], ALU.mult, R=R_, W=W_)
        self.ts("dve", sc[:, NFI, :], sc[:, FIM, :], -1.0, None, ALU.mult, None, R=R_, W=W_)
        (ECN, ESN) = (21, 22)
        self.ts("dve", sc[:, T1, :], sc[:, SS, :], float(NT), 16.0, ALU.mult, ALU.add, R=R_, W=W_)
        ki2 = A.i32(16)
        self.cp("dve", ki2, sc[:, T1, :], R=R_, W=W_)
        self.tt("dve", sc[:, T1, :], sc[:, T1, :], ki2, ALU.subtract, R=R_, W=W_)
        self.act(sc[:, ESN, :], sc[:, T1, :], AF.Sin, R=R_, W=W_, scale=TWO_PI)
        self.act(sc[:, T2, :], sc[:, T1, :], AF.Abs, R=R_, W=W_)
        self.act(sc[:, ECN, :], sc[:, T2, :], AF.Sin, R=R_, W=W_, scale=-TWO_PI, bias=hp)
        if self.stop == "s5c":
            self.dump_ap(sc[:].rearrange("p a k -> p (a k)"), NS * 16, [Bs])
            return
        Bm = A.bf16(16 * 2 * 128).rearrange("p (k c n) -> p k c n", k=16, c=2)
        BBm = Buf()
        for k4 in range(4):
            self.ld(Bm[:, 4 * k4:4 * k4 + 4], T["s5B"][l][:, 4 * k4:4 * k4 + 4], [BBm], cast=True)
        Lm = A.bf16(16 * 3 * 64).rearrange("p (k c n) -> p k c n", k=16, c=3)
        BLm = Buf()
        mC = A.mark()
        Cf = A.f32(16 * 2 * 64).rearrange("p (k c n) -> p k c n", k=16, c=2)
        BCf = Buf()
        self.ld(Cf, T["s5C"][l], [BCf])
        ct = A.f32(128).rearrange("p (a n) -> p a n", a=2)
        Bct = Buf()
        for kk in range(16):
            fr = sc[:, FRE, kk:kk + 1]
            fi = sc[:, FIM, kk:kk + 1]
            nfi = sc[:, NFI, kk:kk + 1]
            RR = [BCf, Bs, Bct]
            self.ts("dve", ct[:, 0, :], Cf[:, kk, 0, :], fr, None, ALU.mult, None, R=RR, W=[Bct])
            self.stt("dve", ct[:, 0, :], Cf[:, kk, 1, :], nfi, ct[:, 0, :], ALU.mult, ALU.add, R=RR, W=[Bct])
            self.ts("dve", ct[:, 1, :], Cf[:, kk, 0, :], fi, None, ALU.mult, None, R=RR, W=[Bct])
            self.stt("dve", ct[:, 1, :], Cf[:, kk, 1, :], fr, ct[:, 1, :], ALU.mult, ALU.add, R=RR, W=[Bct])
            self.cp("dve", Lm[:, kk, 0, :], ct[:, 0, :], R=[Bct], W=[BLm])
            self.ts("dve", Lm[:, kk, 1, :], ct[:, 0, :], -1.0, None, ALU.mult, None, R=[Bct], W=[BLm])
            self.ts("dve", Lm[:, kk, 2, :], ct[:, 1, :], -1.0, None, ALU.mult, None, R=[Bct], W=[BLm])
        self.C.flush()
        A.release(mC)
        if self.stop == "s5b":
            self.dump_ap(sc[:].rearrange("p a k -> p (a k)"), NS * 16, [Bs])
            return
        car = A.f32(64).rearrange("p (a k) -> p a k", a=4)
        Bcar = Buf()
        NB = 2
        def blk():
            return [A.f32(384) for _ in range(NB)], [Buf() for _ in range(NB)]
        UA, BUA = blk()
        KI_, BKI = [A.i32(384) for _ in range(NB)], [Buf() for _ in range(NB)]
        ES, BES = blk()
        EC, BEC = blk()
        t1, Bt1 = blk(); t2, Bt2 = blk()
        VR, BVR = blk(); VI, BVI = blk()
        GR, BGR = blk(); GI, BGI = blk()
        AF_, BAF = VR, BVR
        t3, Bt3 = GR, BGR
        t4, Bt4 = GI, BGI
        Pb = [[A.bf16(384) for _ in range(4)] for _ in range(NB)]
        BPb = [[Buf() for _ in range(4)] for _ in range(NB)]
        offt = A.f32(4)
        Boff = Buf()
        border = [0, 6, 5, 4, 3, 2, 1]
        Bck = [Buf() for _ in range(16)]
        Bofu = [Buf(), Buf()]

        def junction():
            cr, cim = car[:, 0, 8:16], car[:, 1, 8:16]
            ecn, esn = sc[:, ECN, 8:16], sc[:, ESN, 8:16]
            tr_, ti_ = car[:, 2, 8:16], car[:, 3, 8:16]
            RR = Bck[8:16] + [Bs, Bcar]
            WW = Bck[8:16] + [Bcar]
            self.tt("dve", tr_, cr, ecn, ALU.mult, R=RR, W=WW)
            self.tt("dve", ti_, cim, esn, ALU.mult, R=RR, W=WW)
            self.tt("dve", tr_, tr_, ti_, ALU.add, R=RR, W=WW)
            self.tt("dve", ti_, cim, ecn, ALU.mult, R=RR, W=WW)
            self.tt("dve", cim, cr, esn, ALU.mult, R=RR, W=WW)
            self.tt("dve", cim, ti_, cim, ALU.subtract, R=RR, W=WW)
            self.cp("dve", cr, tr_, R=RR, W=WW)

        def iteration(step, d, ch, g4, u):
            ri = step if d == 0 else border[step]
            c0, c1 = RANGES[ri]
            w = c1 - c0
            by = 6 + ch
            gp = ch * 4 + g4
            kk = d * 8 + gp
            sap = sc[:, SS, kk:kk + 1]
            o = offt[:, u:u + 1]
            self.ts("dve", o, sap, float(c0), 16.0, ALU.mult, ALU.add, R=[Bs], W=[Bofu[u]])
            self.ts("dve", UA[u][:, 0:w], self.iota0[:, 0:w], sap, o, ALU.mult, ALU.add,
                    R=[self.Biota, Bs, Bofu[u]], W=[BUA[u]])
            yield
            kf = KI_[u].bitcast(F32)
            self.ts("dve", kf[:, 0:w], UA[u][:, 0:w], 12582912.0, 12582912.0, ALU.add, ALU.subtract, R=[BUA[u]], W=[BKI[u]])
            self.tt("dve", UA[u][:, 0:w], UA[u][:, 0:w], kf[:, 0:w], ALU.subtract,
                    R=[BUA[u], BKI[u]], W=[BUA[u]])
            yield
            self.act(ES[u][:, 0:w], UA[u][:, 0:w], AF.Sin, R=[BUA[u]], W=[BES[u]], scale=TWO_PI)
            self.act(AF_[u][:, 0:w], UA[u][:, 0:w], AF.Abs, R=[BUA[u]], W=[BAF[u]])
            self.act(EC[u][:, 0:w], AF_[u][:, 0:w], AF.Sin, R=[BAF[u], self.Bcf], W=[BEC[u]],
                     scale=-TWO_PI, bias=hp)
            b0 = 2 * u
            b1 = 2 * u + 1
            pre = self.bank(b0, w)
            pim = self.bank(b1, w)
            self.mm(pre, Bm[:, kk, 0, :], ubf[:, ch, c0:c1], True, True, R=[BBm, Bu[ri]], W=[self.BPS[b0]])
            self.mm(pim, Bm[:, kk, 1, :], ubf[:, ch, c0:c1], True, True, R=[BBm, Bu[ri]], W=[self.BPS[b1]])
            yield
            self.tt("dve", t1[u][:, 0:w], pre, EC[u][:, 0:w], ALU.mult, R=[self.BPS[b0], BEC[u]], W=[Bt1[u]])
            self.tt("dve", t2[u][:, 0:w], pim, ES[u][:, 0:w], ALU.mult, R=[self.BPS[b1], BES[u]], W=[Bt2[u]])
            yield
            self.tt("dve", t3[u][:, 0:w], pim, EC[u][:, 0:w], ALU.mult, R=[self.BPS[b1], BEC[u]], W=[Bt3[u]])
            self.tt("dve", t4[u][:, 0:w], pre, ES[u][:, 0:w], ALU.mult, R=[self.BPS[b0], BES[u]], W=[Bt4[u]])
            self.tt("pool", VR[u][:, 0:w], t1[u][:, 0:w], t2[u][:, 0:w], ALU.add, R=[Bt1[u], Bt2[u]], W=[BVR[u]])
            yield
            self.tt("pool", VI[u][:, 0:w], t3[u][:, 0:w], t4[u][:, 0:w], ALU.subtract, R=[Bt3[u], Bt4[u]], W=[BVI[u]])
            yield
            rho_bc = sc[:, RHO, kk:kk + 1].to_broadcast([128, w])
            first = (step == 0)
            for (Vb, BV, Gb, BG, cidx) in ((VR[u], BVR[u], GR[u], BGR[u], 0), (VI[u], BVI[u], GI[u], BGI[u], 1)):
                init = 0.0 if first else car[:, cidx, kk:kk + 1]
                if d == 0:
                    vin, gout = Vb[:, 0:w], Gb[:, 0:w]
                    last = Gb[:, w - 1:w]
                else:
                    v0 = Vb[:, 0:w]
                    g0_ = Gb[:, 0:w]
                    vin = AP(v0.tensor, v0.offset + w - 1, [list(v0.ap[0]), [-1, w]])
                    gout = AP(g0_.tensor, g0_.offset + w - 1, [list(g0_.ap[0]), [-1, w]])
                    last = Gb[:, 0:1]
                self.scan(gout, rho_bc, vin, init, R=[BV, Bs, Bck[kk]], W=[BG])
                self.cp("pool", car[:, cidx, kk:kk + 1], last, R=[BG], W=[Bck[kk]])
                yield
            self.tt("dve", Pb[u][0][:, 0:w], GR[u][:, 0:w], EC[u][:, 0:w], ALU.mult, R=[BGR[u], BEC[u]], W=[BPb[u][0]])
            self.tt("pool", Pb[u][1][:, 0:w], GI[u][:, 0:w], ES[u][:, 0:w], ALU.mult, R=[BGI[u], BES[u]], W=[BPb[u][1]])
            yield
            self.tt("pool", Pb[u][2][:, 0:w], GR[u][:, 0:w], ES[u][:, 0:w], ALU.mult, R=[BGR[u], BES[u]], W=[BPb[u][2]])
            self.tt("pool", Pb[u][3][:, 0:w], GI[u][:, 0:w], EC[u][:, 0:w], ALU.mult, R=[BGI[u], BEC[u]], W=[BPb[u][3]])
            yield
            py = self.bank(by, w)[64 * (g4 // 2):64 * (g4 // 2) + 64, :]
            for pi, li_ in ((0, 0), (1, 1), (2, 2), (3, 2)):
                self.mm(py, Lm[:, kk, li_, :], Pb[u][pi][:, 0:w], pi == 0 and g4 % 2 == 0, pi == 3 and g4 % 2 == 1,
                        R=[BLm, BPb[u][pi]], W=[self.BPS[by]])
            if g4 == 3:
                self.tt("dve", ysb[:, ch, c0:c1], ysb[:, ch, c0:c1], self.bank(by, w), ALU.add,
                        R=[self.BPS[by], By[ch][ri]], W=[By[ch][ri]])
            yield

        def stream(u):
            if u == 1:
                for _ in range(5):
                    yield
            for step in range(7):
                for d in range(2):
                    if u == 0 and d == 1 and step == 1:
                        junction()
                    for ch in range(2):
                        for g4 in (u, u + 2):
                            yield from iteration(step, d, ch, g4, u)

        live = [stream(0), stream(1)]
        while live:
            for g in list(live):
                try:
                    next(g)
                except StopIteration:
                    live.remove(g)
        self.C.flush()
        A.release(m)
        m = A.mark()
        ysb = A.f32(2 * NT).rearrange("p (c t) -> p c t", c=2)
        gl = A.bf16(2 * NT).rearrange("p (c t) -> p c t", c=2)
        Bgl = [Buf() for _ in RANGES]
        wglu = A.bf16(2 * 512).rearrange("p (k n) -> p k n", k=2)
        Bwg = Buf()
        src = T["w_glu"][l].rearrange("(k p) n -> p k n", p=128)
        self.ld(wglu[:, :, :], src, [Bwg], cast=True)
        wout, Bwo = self.load_wout(l, 6, 2)
        sT = [A.bf16(2 * 384).rearrange("p (c t) -> p c t", c=2) for _ in range(2)]
        BsT = [Buf(), Buf()]
        ta = [A.f32(384) for _ in range(2)]; Bta = [Buf(), Buf()]
        tb = [A.f32(384) for _ in range(2)]; Btb = [Buf(), Buf()]
        tc_ = [A.f32(384) for _ in range(2)]; Btc = [Buf(), Buf()]
        rl = list(range(len(RANGES))) if self.need_ctx else list(range(1, len(RANGES)))
        q = 0
        pbk = 0
        for ri in rl:
            c0, c1 = RANGES[ri]
            w = c1 - c0
            sidx = 1 if ri == 0 else 0
            for ch in range(2):
                u = q % 2
                q += 1
                yy = ysb[:, ch, c0:c1]
                self.tt("pool", ta[u][:, 0:w], yy, yy, ALU.mult, R=[By[ch][ri]], W=[Bta[u]])
                self.ts("pool", ta[u][:, 0:w], ta[u][:, 0:w], 0.044715, 1.0, ALU.mult, ALU.add, R=[Bta[u]], W=[Bta[u]])
                self.tt("pool", ta[u][:, 0:w], ta[u][:, 0:w], yy, ALU.mult, R=[Bta[u], By[ch][ri]], W=[Bta[u]])
                self.act(tb[u][:, 0:w], ta[u][:, 0:w], AF.Sigmoid, R=[Bta[u]], W=[Btb[u]], scale=1.5957691216057308)
                self.tt("dve", gl[:, ch, c0:c1], yy, tb[u][:, 0:w], ALU.mult, R=[By[ch][ri], Btb[u]], W=[Bgl[ri]])
            su = ri % 2
            for ch in range(2):
                ba = pbk % 6
                bg = (pbk + 1) % 6
                pbk += 2
                pa = self.bank(ba, w)
                pg = self.bank(bg, w)
                for k in range(2):
                    self.mm(pa, wglu[:, k, ch * 128:(ch + 1) * 128], gl[:, k, c0:c1], k == 0, k == 1,
                            R=[Bwg, Bgl[ri]], W=[self.BPS[ba]])
                for k in range(2):
                    self.mm(pg, wglu[:, k, 256 + ch * 128:256 + (ch + 1) * 128], gl[:, k, c0:c1], k == 0, k == 1,
                            R=[Bwg, Bgl[ri]], W=[self.BPS[bg]])
                u = q % 2
                q += 1
                self.act(tc_[u][:, 0:w], pa, AF.Identity, R=[self.BPS[ba], self.Bvp], W=[Btc[u]],
                         bias=self.vp[:, VP_BGLU + ch:VP_BGLU + ch + 1])
                self.act(tb[u][:, 0:w], pg, AF.Sigmoid, R=[self.BPS[bg], self.Bvp], W=[Btb[u]],
                         bias=self.vp[:, VP_BGLU + 2 + ch:VP_BGLU + 3 + ch])
                self.tt("dve", sT[su][:, ch, 0:w], tc_[u][:, 0:w], tb[u][:, 0:w], ALU.mult, R=[Btc[u], Btb[u]], W=[BsT[su]])
            for oc in range(KC):
                b = pbk % 6
                pbk += 1
                po = self.bank(b, w)
                for k in range(2):
                    self.mm(po, wout[:, k, oc * 128:(oc + 1) * 128], sT[su][:, k, 0:w], k == 0, k == 1,
                            R=[Bwo, BsT[su]], W=[self.BPS[b]])
                dst = self.hT[:, oc, c0:c1]
                self.stt("dve", dst, po, self.Gmod[:, 1, oc, sidx:sidx + 1], dst, ALU.mult, ALU.add,
                         R=[self.BPS[b], self.Bmod, self.Bh[ri]], W=[self.Bh[ri]])
        self.C.flush()
        A.release(m)

    def mix_mla(self, l, m_xm):
        T = self.T
        A = self.A
        xm, Bxm = self.xm2, self.Bxm2
        cqn = A.bf16(3 * NT).rearrange("p (c t) -> p c t", c=3)
        ckvn = A.bf16(2 * NT).rearrange("p (c t) -> p c t", c=2)
        krT = A.f32(NT)
        Bcq, Bckv, Bkr = Buf(), Buf(), Buf()
        m1 = A.mark()
        wbuf = A.bf16(KC * 384).rearrange("p (k n) -> p k n", k=KC)
        Bw = Buf()
        wbuf2 = A.bf16(KC * 288).rearrange("p (k n) -> p k n", k=KC)
        Bw2 = Buf()
        sq = [A.f32(384) for _ in range(2)]
        Bsq = [Buf(), Buf()]
        rs = A.f32(384)
        Brs = Buf()
        self.load_win(l, 1152, 384, wbuf, Bw)
        self.load_win(l, 1536, 288, wbuf2, Bw2)
        q = 0
        for (src_w, Bsw, nch, dstT, Bd, gcol) in ((wbuf, Bw, 3, cqn, Bcq, VP_CQG), (wbuf2, Bw2, 2, ckvn, Bckv, VP_CKVG)):
            for ri in range(len(RANGES)):
                c0, c1 = RANGES[ri]
                w = c1 - c0
                pss = self.bank(4 + (ri % 2), w)
                Bss = self.BPS[4 + (ri % 2)]
                for cch in range(nch):
                    pz = self.bank(cch, w)
                    for k in range(KC):
                        self.mm(pz, src_w[:, k, cch * 128:(cch + 1) * 128], xm[:, k, c0:c1], k == 0, k == KC - 1,
                                R=[Bsw, Bxm[ri]], W=[self.BPS[cch]])
                    u = q % 2
                    q += 1
                    self.act(sq[u][:, 0:w], pz, AF.Square, R=[self.BPS[cch]], W=[Bsq[u]])
                    self.mm(pss, self.ones, sq[u][:, 0:w], cch == 0, cch == nch - 1, R=[Bsq[u], self.Bcf], W=[Bss])
                self.act(rs[:, 0:w], pss, AF.Ln, R=[Bss, self.Bcf], W=[Brs], bias=self.misc[:, 0:1],
                         scale=1.0 / (nch * 128))
                self.act(rs[:, 0:w], rs[:, 0:w], AF.Exp, R=[Brs], W=[Brs], scale=-0.5)
                for cch in range(nch):
                    self.stt("dve", dstT[:, cch, c0:c1], self.bank(cch, w), self.vp[:, gcol + cch:gcol + cch + 1],
                             rs[:, 0:w], ALU.mult, ALU.mult, R=[self.BPS[cch], Brs, self.Bvp], W=[Bd])
        for ri in range(len(RANGES)):
            c0, c1 = RANGES[ri]
            w = c1 - c0
            b = 6 + (ri % 2)
            pz = self.bank(b, w)[64:96, :]
            for k in range(KC):
                self.mm(pz, wbuf2[:, k, 256:288], xm[:, k, c0:c1], k == 0, k == KC - 1,
                        R=[Bw2, Bxm[ri]], W=[self.BPS[b]])
            self.cp("act", krT[64:96, c0:c1], pz, R=[self.BPS[b]], W=[Bkr])
        self.C.flush()
        A.release(m1)
        A.high_release()
        wuq = A.bf16(3 * 576).rearrange("p (k n) -> p k n", k=3)
        wukv = A.bf16(2 * 768).rearrange("p (k n) -> p k n", k=2)
        Bwq, Bwkv = Buf(), Buf()
        self.ld(wuq[:, :, :], T["w_uq"][l].rearrange("(k p) n -> p k n", p=128), [Bwq], cast=True)
        self.ld(wukv[:, :, :], T["w_ukv"][l].rearrange("(k p) n -> p k n", p=128), [Bwkv], cast=True)
        rope = A.f32(2 * NT).rearrange("p (a t) -> p a t", a=2)
        Brope = Buf()
        self.ld(rope[0:96, :, :], T["rope"], [Brope])
        b_tok = A.bf16(18 * 384).rearrange("p (t d) -> p t d", t=18)
        Bbt = Buf()
        mT = A.mark()
        QT = [A.bf16(NT) for _ in range(2)]
        KT = [A.bf16(NT) for _ in range(2)]
        VT = [A.bf16(18 * 65).rearrange("p (t d) -> p t d", t=18) for _ in range(2)]
        BQ, BK, BV = [Buf(), Buf()], [Buf(), Buf()], [Buf(), Buf()]
        for hb in range(2):
            self.memset("pool", VT[hb][:, :, 64:65], 1.0, [BV[hb]])
        NSL = 2
        kcat = [A.f32(384) for _ in range(NSL)]; Bkc = [Buf() for _ in range(NSL)]
        sqh = [A.f32(384) for _ in range(NSL)]; Bsqh = [Buf() for _ in range(NSL)]
        rsh = [A.f32(384) for _ in range(NSL)]; Brsh = [Buf() for _ in range(NSL)]
        xn = [A.f32(384) for _ in range(NSL)]; Bxn = [Buf() for _ in range(NSL)]
        r1 = [A.f32(384) for _ in range(NSL)]; Br1 = [Buf() for _ in range(NSL)]
        r2 = [A.f32(384) for _ in range(NSL)]; Br2 = [Buf() for _ in range(NSL)]
        pT = [A.bf16(512) for _ in range(3)]
        BpT = [Buf() for _ in range(3)]
        rc = A.f32(8)
        Brc = Buf()
        ocp = A.f32(4 * 65).rearrange("p (a d) -> p a d", a=4)
        Bocp = [Buf() for _ in range(4)]
        ones96 = self.ones[0:96, 0:96]
        qranges = list(range(len(RANGES))) if self.need_ctx else list(range(1, len(RANGES)))
        gb = [0]

        def gbank():
            gb[0] += 1
            return 6 + (gb[0] % 2)

        def chain(h, which, ri, u):
            hb = h % 2
            dst, Bdst = (QT[hb], BQ[hb]) if which == 0 else (KT[hb], BK[hb])
            gain = self.small[0:96, 1:2] if which == 0 else self.vp[0:96, VP_MK:VP_MK + 1]
            c0, c1 = RANGES[ri]
            w = c1 - c0
            b = 6 + u
            if which == 0:
                pz = self.bank(b, w)[0:96, :]
                for k in range(3):
                    self.mm(pz, wuq[:, k, h * 96:(h + 1) * 96], cqn[:, k, c0:c1], k == 0, k == 2,
                            R=[Bwq, Bcq], W=[self.BPS[b]])
            else:
                pz = self.bank(b, w)[0:64, :]
                for k in range(2):
                    self.mm(pz, wukv[:, k, h * 128:h * 128 + 64], ckvn[:, k, c0:c1], k == 0, k == 1,
                            R=[Bwkv, Bckv], W=[self.BPS[b]])
            yield
            if which == 0:
                self.cp("dve", kcat[u][0:96, 0:w], pz, R=[self.BPS[b]], W=[Bkc[u]])
            else:
                self.cp("dve", kcat[u][0:64, 0:w], pz, R=[self.BPS[b]], W=[Bkc[u]])
                self.cp("pool", kcat[u][64:96, 0:w], krT[64:96, c0:c1], R=[Bkr], W=[Bkc[u]])
            zsrc, Bz = kcat[u][0:96, 0:w], Bkc[u]
            yield
            self.tt("pool", sqh[u][0:96, 0:w], zsrc, zsrc, ALU.mult, R=[Bz], W=[Bsqh[u]])
            bs_ = 6 + u
            pss = self.bank(bs_, w)[0:96, :]
            self.mm(pss, ones96, sqh[u][0:96, 0:w], True, True, R=[Bsqh[u], self.Bcf], W=[self.BPS[bs_]])
            yield
            self.act(rsh[u][0:96, 0:w], pss, AF.Ln, R=[self.BPS[bs_], self.Bcf], W=[Brsh[u]],
                     bias=self.misc[0:96, 0:1], scale=1.0 / 96)
            yield
            self.act(rsh[u][0:96, 0:w], rsh[u][0:96, 0:w], AF.Exp, R=[Brsh[u]], W=[Brsh[u]], scale=-0.5)
            yield
            self.stt("dve", xn[u][0:96, 0:w], zsrc, gain, rsh[u][0:96, 0:w], ALU.mult, ALU.mult,
                     R=[Bz, Brsh[u], self.Bsmall, self.Bvp], W=[Bxn[u]])
            bp = 6 + u
            pp = self.bank(bp, w)[0:96, :]
            self.mm(pp, self.perm, xn[u][0:96, 0:w], True, True, R=[Bxn[u], self.Bcf], W=[self.BPS[bp]])
            yield
            self.tt("pool", r1[u][0:96, 0:w], xn[u][0:96, 0:w], rope[0:96, 0, c0:c1], ALU.mult,
                    R=[Bxn[u], Brope], W=[Br1[u]])
            self.tt("dve", r2[u][0:96, 0:w], pp, rope[0:96, 1, c0:c1], ALU.mult,
                    R=[self.BPS[bp], Brope], W=[Br2[u]])
            yield
            self.tt("pool", dst[0:96, c0:c1], r1[u][0:96, 0:w], r2[u][0:96, 0:w], ALU.add,
                    R=[Br1[u], Br2[u]], W=[Bdst])
            yield

        def chains(h, lst, u):
            for (which, ri) in lst:
                yield from chain(h, which, ri, u)

        def vgen(h, u):
            hb = h % 2
            for t in range(9 * u, 9 * u + 9):
                b = 6 + u
                pv = self.bank(b, 64)
                for k in range(2):
                    self.mm(pv, ckvn[:, k, t * 128:(t + 1) * 128], wukv[:, k, h * 128 + 64:h * 128 + 128], k == 0, k == 1,
                            R=[Bckv, Bwkv], W=[self.BPS[b]])
                self.cp("dve", VT[hb][:, t, 0:64], pv, R=[self.BPS[b]], W=[BV[hb]])
                yield

        def merged(gens):
            live = list(gens)
            while live:
                for g in list(live):
                    try:
                        next(g)
                    except StopIteration:
                        live.remove(g)
                yield

        def gen(h):
            lst = [(0, ri) for ri in qranges] + [(1, ri) for ri in range(len(RANGES))]
            def stream(u):
                yield from chains(h, lst[u::2], u)
                yield from vgen(h, u)
            return merged([stream(0), stream(1)])

        def attn(h):
            hb = h % 2
            qblocks = [(256 + 512 * i, 256 + 512 * (i + 1), list(range(18))) for i in range(4)]
            if self.need_ctx:
                qblocks.append((0, 256, [0, 1]))
            seq = []
            for (qc0, qc1, kts) in qblocks:
                for ki_, kt in enumerate(kts):
                    seq.append((qc0, qc1, ki_, kt, len(kts)))

            def emit_S(i):
                qc0, qc1, ki_, kt, nk = seq[i]
                b = 4 + (i % 2)
                pS = self.bank(b, qc1 - qc0)
                self.mm(pS, KT[hb][0:96, kt * 128:(kt + 1) * 128], QT[hb][0:96, qc0:qc1], True, True,
                        R=[BK[hb], BQ[hb]], W=[self.BPS[b]])

            emit_S(0)
            for i, (qc0, qc1, ki_, kt, nk) in enumerate(seq):
                b = 4 + (i % 2)
                u = i % 3
                nq = (qc1 - qc0) // 128
                self.act(pT[u][:, 0:qc1 - qc0], self.bank(b, qc1 - qc0), AF.Exp, R=[self.BPS[b]], W=[BpT[u]])
                if i + 1 < len(seq):
                    emit_S(i + 1)
                for qs in range(nq):
                    self.mm(self.bank(qs, 65), pT[u][:, qs * 128:(qs + 1) * 128], VT[hb][:, kt, :],
                            ki_ == 0, ki_ == nk - 1, R=[BpT[u], BV[hb]], W=[self.BPS[qs]])
                if ki_ == nk - 1:
                    for qs in range(nq):
                        self.cp("dve", ocp[:, qs, :], self.bank(qs, 65), R=[self.BPS[qs]], W=[Bocp[qs]])
                    for qs in range(nq):
                        t = (qc0 // 128) + qs
                        self.recip(rc[:, qs:qs + 1], ocp[:, qs, 64:65], R=[Bocp[qs]], W=[Brc])
                        self.ts("dve", b_tok[:, t, h * 64:(h + 1) * 64], ocp[:, qs, 0:64], rc[:, qs:qs + 1], None, ALU.mult, None,
                                R=[Bocp[qs], Brc], W=[Bbt])
                yield

        for _ in gen(0):
            pass
        for h in range(6):
            streams = [attn(h)]
            if h + 1 < 6:
                streams.append(gen(h + 1))
            for _ in merged(streams):
                pass
        self.C.flush()
        A.release(mT)
        wout, Bwo = self.load_wout(l, 3, 3)
        tiles = list(range(2, 18)) + ([0, 1] if self.need_ctx else [])
        self.outproj_tiles(l, b_tok, Bbt, tiles, 3, 3, wout, Bwo)

    def final(self, out):
        A = self.A
        m = A.mark()
        stg = [A.f32(D) for _ in range(2)]
        Bst = [Buf(), Buf()]
        for t in range(2, 18):
            s = t % 2
            ri = self.ranges_of(t * 128, t * 128 + 128)
            for half in range(2):
                b = (2 * t + half) % 8
                pb = self.bank(b).rearrange("p (a q) -> p a q", a=4)
                for a in range(4):
                    c = half * 4 + a
                    self.tr(pb[:, a, :], self.hT[:, c, t * 128:(t + 1) * 128], self.ident,
                            R=[self.Bh[r] for r in ri] + [self.Bcf], W=[self.BPS[b]])
                self.cp("act" if half == 0 else "dve", stg[s][:, half * 512:(half + 1) * 512],
                        self.bank(b), R=[self.BPS[b]], W=[Bst[s]])
            self.st(out[(t - 2) * 128:(t - 1) * 128, :], stg[s], R=[Bst[s]])
        self.C.flush()
        A.release(m)


def _na_bias_index():
    W = 64
    rows = 32

    def row_start(r):
        return min(max(r - 4, 0), rows - 8)

    def col_start(c):
        return min(max(c - 8, 0), W - 16)

    tiles = []
    for dm in range(-2, 3):
        tiles.append((6, 6 + dm))
    for i in (0, 1, 14, 15):
        mlo = {0: 0, 1: 0, 14: 12, 15: 12}[i]
        for m in range(mlo, mlo + 4):
            tiles.append((i, m))
    idx = np.full((21, 128, 128), 465, dtype=np.int64)
    kk = np.arange(128)
    kr2, kc = kk // 64, kk % 64
    for ti, (i, m) in enumerate(tiles):
        for q in range(128):
            qr = 2 * i + q // 64
            qc = q % 64
            krow = 2 * m + kr2
            rs = row_start(qr)
            cs = col_start(qc)
            ok = (krow >= rs) & (krow < rs + 8) & (kc >= cs) & (kc < cs + 16)
            drow = krow - qr + 7
            dcol = np.clip(kc - qc, -15, 15) + 15
            val = drow * 31 + dcol
            idx[ti, :, q] = np.where(ok, val, 465)
    return idx


_CONST = {}


def _consts():
    if _CONST:
        return _CONST
    cf = np.zeros((128, 5, 128), np.float32)
    cf[:, 0, :] = np.eye(128, dtype=np.float32)
    cf[:, 1, :] = 1.0
    cf[0:64, 2, 0:64] = 1.0
    cf[64:128, 2, 64:128] = 1.0
    P = np.zeros((128, 128), np.float32)
    for base in (64, 80):
        for j in range(8):
            P[base + 8 + j, base + j] = -1.0
            P[base + j, base + 8 + j] = 1.0
    cf[:, 3, :] = P
    cf[:, 4, 0] = EPS
    cf[:, 4, 1] = math.pi / 2
    cf[:, 4, 16:24] = 1.0
    cf[:, 4, 24:32] = -1.0
    rope = np.zeros((96, 2, NT), np.float32)
    rope[:, 0, :] = 1.0
    t = np.arange(NX)
    row = (t // 64).astype(np.float32)
    col = (t % 64).astype(np.float32)
    inv = (10000.0 ** (-np.arange(8, dtype=np.float32) / 8)).astype(np.float32)
    for base, pos in ((64, row), (80, col)):
        ang = pos[None, :] * inv[:, None]
        c = np.cos(ang).astype(np.float32)
        s = np.sin(ang).astype(np.float32)
        rope[base:base + 8, 0, LCTX:] = c
        rope[base + 8:base + 16, 0, LCTX:] = c
        rope[base:base + 8, 1, LCTX:] = s
        rope[base + 8:base + 16, 1, LCTX:] = s
    iota = np.tile(np.arange(384, dtype=np.float32)[None, :], (128, 1))
    _CONST.update(cf=cf, rope=rope, iota=iota, nidx=_na_bias_index())
    return _CONST


def _prep_shared(inp):
    c = _consts()
    NL = 2
    sh = {}
    vp = np.zeros((NL, 128, 128), np.float32)
    for l in range(NL):
        vp[l, 0:72] = inp["b_mod"][l].reshape(72, 128)
        vp[l, 72:96] = inp["norm_g"][l].reshape(24, 128)
        vp[l, 96:98] = inp["s5_d"][l].reshape(2, 128)
        vp[l, 98:102] = inp["s5_b_glu"][l].reshape(4, 128)
        vp[l, 102:105] = inp["mla_cq_g"][l].reshape(3, 128)
        vp[l, 105:107] = inp["mla_ckv_g"][l].reshape(2, 128)
        vp[l, 107] = np.tile(inp["na_qk_g"][l, 0], 2)
        vp[l, 108] = np.tile(inp["na_qk_g"][l, 1], 2)
        vp[l, 109, 0:96] = inp["mla_qk_g"][l, 0]
        vp[l, 110, 0:96] = inp["mla_qk_g"][l, 1]
    sh["vecpack"] = vp
    for k in ("w_mod", "ffn_w_gu", "ffn_w_down", "w_in", "w_out", "mla_w_uq", "mla_w_ukv", "s5_w_glu"):
        sh[k] = np.ascontiguousarray(inp[k], dtype=np.float32)
    nidx = c["nidx"]
    nab = np.empty((NL, 128, 126, 128), np.float32)
    for l in range(NL):
        for h in range(6):
            ext = np.concatenate([inp["na_rpb"][l, h].reshape(-1), np.array([NEG], np.float32)])
            g = ext[nidx]
            nab[l, :, h * 21:(h + 1) * 21, :] = g.transpose(1, 0, 2)
    sh["nab"] = nab
    s5p = np.empty((NL, 128, 3, 16), np.float32)
    s5B = np.zeros((NL, 128, 16, 2, 128), np.float32)
    s5C = np.zeros((NL, 128, 16, 2, 64), np.float32)
    for l in range(NL):
        for name, a in (("s5_lam_re", 0), ("s5_lam_im", 1)):
            v = inp[name][l].reshape(2, 8, 2, 64)
            s5p[l, :, a, :] = v.transpose(2, 3, 0, 1).reshape(128, 16)
        ld = inp["s5_log_dt"][l].reshape(2, 8, 2)
        s5p[l, :, 2, :] = np.broadcast_to(ld.transpose(2, 0, 1)[:, None, :, :], (2, 64, 2, 8)).reshape(128, 16)
        for ci, (bn, cn) in enumerate((("s5_b_re", "s5_c_re"), ("s5_b_im", "s5_c_im"))):
            b = inp[bn][l]
            cm = inp[cn][l]
            for d in range(2):
                for gp in range(8):
                    kk = d * 8 + gp
                    for g2 in range(2):
                        g = 2 * gp + g2
                        r0 = 32 * (gp % 4) + 16 * g2
                        s5B[l, r0:r0 + 16, kk, ci, g2 * 64:(g2 + 1) * 64] = b[d, g].T
                        s5C[l, g2 * 64:(g2 + 1) * 64, kk, ci, 32 * (gp % 2) + g2 * 16:32 * (gp % 2) + (g2 + 1) * 16] = cm[d, g].T
    sh["s5p"], sh["s5B"], sh["s5C"] = s5p, s5B, s5C
    sh["cf32"], sh["rope"], sh["iota"] = c["cf"], c["rope"], c["iota"]
    return sh


def _in_maps(inp, cores):
    sh = _prep_shared(inp)
    maps = []
    for b in cores:
        m = dict(sh)
        m["x"] = np.ascontiguousarray(inp["x"][b], dtype=np.float32)
        m["ctx"] = np.ascontiguousarray(inp["ctx"][b], dtype=np.float32)
        cv = np.empty((16, 128), np.float32)
        cv[0:8] = inp["c"][b].reshape(8, 128)
        cv[8:16] = inp["c_ctx"].reshape(8, 128)
        m["cvec"] = cv
        maps.append(m)
    return maps


_NC_CACHE = {}


def kernel(**inputs):
    inp = {k: np.asarray(v) for k, v in inputs.items()}
    if "full" not in _NC_CACHE:
        _NC_CACHE["full"] = K(2, None).build()
    nc = _NC_CACHE["full"]
    maps = _in_maps(inp, list(range(8)))
    res = run_bass_kernel_spmd(nc, maps, core_ids=list(range(8)))
    return np.stack([np.asarray(r["out"], dtype=np.float32) for r in res.results], axis=0)
```
